# Optimizing a Trainium2 kernel written in Bass

```python
import math
import jax, jax.numpy as jnp
from jax import lax
import numpy as np

D_MODEL = 1024
BATCH = 16
SEQ = 2048
DEPTH = 1
DEC_BATCH = 16
DEC_SEQ = 16
PAST_LEN = 1024

CHUNK = 64
HEAD_DIM = 64
N_HEADS_A = 8
BAND_PAST_CHUNKS = 8
BAND_PAST = BAND_PAST_CHUNKS * CHUNK
BAND_LEN = (BAND_PAST_CHUNKS + 1) * CHUNK
REL_FUT = CHUNK - 1
REL_PAST = 256
N_REL = REL_FUT + REL_PAST + 1
N_HEADS_B = 4
WIDTH_A = N_HEADS_A * HEAD_DIM
WIDTH_B_QK = N_HEADS_B * 2 * HEAD_DIM
WIDTH_B_V = N_HEADS_B * 2 * HEAD_DIM
IN_WIDTH = 3 * WIDTH_A + 2 * WIDTH_B_QK + WIDTH_B_V
SPLITS = (WIDTH_A, 2 * WIDTH_A, 3 * WIDTH_A, 3 * WIDTH_A + WIDTH_B_QK, 3 * WIDTH_A + 2 * WIDTH_B_QK)
D_FF = 4 * D_MODEL
ROPE_THETA = 10000.0
Q_BLOCK = 128
EPS = 1e-6
NEG_INF = -1e30
ATTN_SCALE = HEAD_DIM ** -0.5

kernel_name = "hybrid_chunkband_diffattn_stream_step"


def rms_norm(x, g):
    xf = x.astype(jnp.float32)
    y = xf * lax.rsqrt(jnp.mean(xf * xf, axis=-1, keepdims=True) + EPS)
    return (y * g.astype(jnp.float32)).astype(x.dtype)


def rope(x, pos):
    half = HEAD_DIM // 2
    inv_freq = ROPE_THETA ** (-jnp.arange(half, dtype=jnp.float32) / half)
    ang = pos.astype(jnp.float32)[:, None] * inv_freq[None, :]
    cos = jnp.cos(ang)[:, None, None, :]
    sin = jnp.sin(ang)[:, None, None, :]
    xf = x.astype(jnp.float32)
    x1, x2 = xf[..., :half], xf[..., half:]
    return jnp.concatenate([x1 * cos - x2 * sin, x2 * cos + x1 * sin], axis=-1).astype(x.dtype)


def project_qkv(h, pos, w_in, qn_a, kn_a, qn_b, kn_b):
    b, s, _ = h.shape
    z = h @ w_in
    qa, ka, va, qb, kb, vb = jnp.split(z, SPLITS, axis=-1)
    qa = rms_norm(qa.reshape(b, s, N_HEADS_A, HEAD_DIM), qn_a)
    ka = rms_norm(ka.reshape(b, s, N_HEADS_A, HEAD_DIM), kn_a)
    va = va.reshape(b, s, N_HEADS_A, HEAD_DIM)
    qb = rope(rms_norm(qb.reshape(b, s, N_HEADS_B, 2, HEAD_DIM), qn_b), pos)
    kb = rope(rms_norm(kb.reshape(b, s, N_HEADS_B, 2, HEAD_DIM), kn_b), pos)
    vb = vb.reshape(b, s, N_HEADS_B, 2 * HEAD_DIM)
    return qa, ka, va, qb, kb, vb


def rel_bias_lookup(table, dist):
    idx = jnp.clip(dist, -REL_FUT, REL_PAST) + REL_FUT
    return table[:, idx].astype(jnp.float32)


def chunk_band_attn_prompt(q, k, v, rel_table):
    b, s, h, d = q.shape
    nc = s // CHUNK
    qc = q.reshape(b, nc, CHUNK, h, d)
    pad = ((0, 0), (BAND_PAST_CHUNKS, 0), (0, 0), (0, 0), (0, 0))
    kp = jnp.pad(k.reshape(b, nc, CHUNK, h, d), pad)
    vp = jnp.pad(v.reshape(b, nc, CHUNK, h, d), pad)
    band_idx = jnp.arange(nc)[:, None] + jnp.arange(BAND_PAST_CHUNKS + 1)[None, :]
    kband = kp[:, band_idx].reshape(b, nc, BAND_LEN, h, d)
    vband = vp[:, band_idx].reshape(b, nc, BAND_LEN, h, d)
    sc = jnp.einsum('bcqhd,bckhd->bhcqk', qc, kband).astype(jnp.float32) * ATTN_SCALE
    qi = jnp.arange(CHUNK)
    kj = jnp.arange(BAND_LEN)
    bias = rel_bias_lookup(rel_table, qi[:, None] + BAND_PAST - kj[None, :])
    k_valid = (jnp.arange(nc)[:, None] * CHUNK - BAND_PAST + kj[None, :]) >= 0
    sc = jnp.where(k_valid[None, None, :, None, :], sc + bias[None, :, None], NEG_INF)
    p = jax.nn.softmax(sc, axis=-1).astype(v.dtype)
    o = jnp.einsum('bhcqk,bckhd->bcqhd', p, vband)
    return o.reshape(b, s, h * d)


def chunk_band_attn_sample(q, k_new, v_new, k_cache, v_cache, rel_table):
    b, t, h, d = q.shape
    lc = k_cache.shape[1]
    keys = jnp.concatenate([k_cache, k_new], axis=1)
    vals = jnp.concatenate([v_cache, v_new], axis=1)
    q_pos = PAST_LEN + jnp.arange(t)
    k_pos = jnp.concatenate([PAST_LEN - lc + jnp.arange(lc), PAST_LEN + jnp.arange(t)])
    bias = rel_bias_lookup(rel_table, q_pos[:, None] - k_pos[None, :])
    sc = jnp.einsum('bqhd,bkhd->bhqk', q, keys).astype(jnp.float32) * ATTN_SCALE + bias[None]
    p = jax.nn.softmax(sc, axis=-1).astype(v_new.dtype)
    o = jnp.einsum('bhqk,bkhd->bqhd', p, vals)
    return o.reshape(b, t, h * d)


def diff_lambda(lq1, lk1, lq2, lk2, layer_idx):
    lam_init = 0.8 - 0.6 * math.exp(-0.3 * layer_idx)
    f = jnp.float32
    lam = (jnp.exp(jnp.sum(lq1.astype(f) * lk1.astype(f)))
           - jnp.exp(jnp.sum(lq2.astype(f) * lk2.astype(f))) + lam_init)
    return lam, lam_init


def diff_combine(sc, v, lam):
    p = jax.nn.softmax(sc, axis=-1)
    a = (p[:, :, 0] - lam * p[:, :, 1]).astype(v.dtype)
    return jnp.einsum('bhqk,bkhe->bqhe', a, v)


def diff_attn_prompt(q, k, v, lam):
    b, s, h, _, d = q.shape
    nb = s // Q_BLOCK
    q_blocks = jnp.moveaxis(q.reshape(b, nb, Q_BLOCK, h, 2, d), 1, 0)
    k_chunk = jnp.arange(s) // CHUNK

    def one_block(args):
        q_blk, start = args
        sc = jnp.einsum('bqhrd,bkhrd->bhrqk', q_blk, k).astype(jnp.float32) * ATTN_SCALE
        q_chunk = (start + jnp.arange(Q_BLOCK)) // CHUNK
        visible = k_chunk[None, :] <= q_chunk[:, None]
        sc = jnp.where(visible, sc, NEG_INF)
        return diff_combine(sc, v, lam)

    o = lax.map(one_block, (q_blocks, jnp.arange(nb) * Q_BLOCK))
    return jnp.moveaxis(o, 0, 1).reshape(b, s, h, 2 * d)


def diff_attn_sample(q, k_new, v_new, k_cache, v_cache, lam):
    keys = jnp.concatenate([k_cache, k_new], axis=1)
    vals = jnp.concatenate([v_cache, v_new], axis=1)
    sc = jnp.einsum('bqhrd,bkhrd->bhrqk', q, keys).astype(jnp.float32) * ATTN_SCALE
    return diff_combine(sc, vals, lam)


def diff_finish(o, subln_g, lam_init):
    b, s = o.shape[:2]
    return (rms_norm(o, subln_g) * (1.0 - lam_init)).reshape(b, s, WIDTH_B_V)


def gated_merge(h, ya, yb, w_gate, b_gate, w_proj_a, w_proj_b, w_out):
    g = jax.nn.sigmoid(h @ w_gate + b_gate)
    g_a, g_b = g[..., :D_MODEL], g[..., D_MODEL:]
    return (g_a * (ya @ w_proj_a) + g_b * (yb @ w_proj_b)) @ w_out


def sq_relu_mlp(x, g, w1, w2):
    u = jax.nn.relu(rms_norm(x, g) @ w1)
    return (u * u) @ w2


def setup_inputs(seed: int = 0) -> dict:
    key = jax.random.key(seed)
    ks = jax.random.split(key, 32)
    f32 = jnp.float32

    def nrm(k, shape, scale=1.0):
        return jax.random.normal(k, shape, f32) * scale

    def gain(k, shape):
        return 1.0 + 0.01 * jax.random.normal(k, shape, f32)

    la = min(BAND_PAST, PAST_LEN)
    return {
        "x_prompt": nrm(ks[0], (BATCH, SEQ, D_MODEL)),
        "x_sample": nrm(ks[1], (DEC_BATCH, DEC_SEQ, D_MODEL)),
        "cache_a_k": nrm(ks[2], (DEPTH, DEC_BATCH, la, N_HEADS_A, HEAD_DIM)),
        "cache_a_v": nrm(ks[3], (DEPTH, DEC_BATCH, la, N_HEADS_A, HEAD_DIM)),
        "cache_b_k": nrm(ks[4], (DEPTH, DEC_BATCH, PAST_LEN, N_HEADS_B, 2, HEAD_DIM)),
        "cache_b_v": nrm(ks[5], (DEPTH, DEC_BATCH, PAST_LEN, N_HEADS_B, 2 * HEAD_DIM)),
        "ln1_g": gain(ks[6], (DEPTH, D_MODEL)),
        "w_in": nrm(ks[7], (DEPTH, D_MODEL, IN_WIDTH), D_MODEL ** -0.5),
        "qn_a": gain(ks[8], (DEPTH, HEAD_DIM)),
        "kn_a": gain(ks[9], (DEPTH, HEAD_DIM)),
        "rel_bias": nrm(ks[10], (DEPTH, N_HEADS_A, N_REL), 0.1),
        "qn_b": gain(ks[11], (DEPTH, HEAD_DIM)),
        "kn_b": gain(ks[12], (DEPTH, HEAD_DIM)),
        "lam_q1": nrm(ks[13], (DEPTH, HEAD_DIM), 0.1),
        "lam_k1": nrm(ks[14], (DEPTH, HEAD_DIM), 0.1),
        "lam_q2": nrm(ks[15], (DEPTH, HEAD_DIM), 0.1),
        "lam_k2": nrm(ks[16], (DEPTH, HEAD_DIM), 0.1),
        "subln_g": gain(ks[17], (DEPTH, 2 * HEAD_DIM)),
        "w_gate": nrm(ks[18], (DEPTH, D_MODEL, 2 * D_MODEL), D_MODEL ** -0.5),
        "b_gate": nrm(ks[19], (DEPTH, 2 * D_MODEL), 0.01),
        "w_proj_a": nrm(ks[20], (DEPTH, WIDTH_A, D_MODEL), WIDTH_A ** -0.5),
        "w_proj_b": nrm(ks[21], (DEPTH, WIDTH_B_V, D_MODEL), WIDTH_B_V ** -0.5),
        "w_out": nrm(ks[22], (DEPTH, D_MODEL, D_MODEL), D_MODEL ** -0.5),
        "ln2_g": gain(ks[23], (DEPTH, D_MODEL)),
        "w_ff1": nrm(ks[24], (DEPTH, D_MODEL, D_FF), D_MODEL ** -0.5),
        "w_ff2": nrm(ks[25], (DEPTH, D_FF, D_MODEL), D_FF ** -0.5),
    }


def reference(x_prompt, x_sample, cache_a_k, cache_a_v, cache_b_k, cache_b_v,
              ln1_g, w_in, qn_a, kn_a, rel_bias, qn_b, kn_b,
              lam_q1, lam_k1, lam_q2, lam_k2, subln_g,
              w_gate, b_gate, w_proj_a, w_proj_b, w_out,
              ln2_g, w_ff1, w_ff2):
    s_p = x_prompt.shape[1]
    t_s = x_sample.shape[1]
    pos_p = jnp.arange(s_p)
    pos_s = PAST_LEN + jnp.arange(t_s)
    keep_p = min(BAND_PAST, s_p)
    xp, xs = x_prompt, x_sample
    ak_p, av_p, bk_p, bv_p = [], [], [], []
    ak_s, av_s, bk_s, bv_s = [], [], [], []
    for l in range(DEPTH):
        lam, lam_init = diff_lambda(lam_q1[l], lam_k1[l], lam_q2[l], lam_k2[l], l)

        hp = rms_norm(xp, ln1_g[l])
        qa, ka, va, qb, kb, vb = project_qkv(hp, pos_p, w_in[l], qn_a[l], kn_a[l], qn_b[l], kn_b[l])
        ya = chunk_band_attn_prompt(qa, ka, va, rel_bias[l])
        yb = diff_finish(diff_attn_prompt(qb, kb, vb, lam), subln_g[l], lam_init)
        xp = xp + gated_merge(hp, ya, yb, w_gate[l], b_gate[l], w_proj_a[l], w_proj_b[l], w_out[l])
        xp = xp + sq_relu_mlp(xp, ln2_g[l], w_ff1[l], w_ff2[l])
        ak_p.append(ka[:, s_p - keep_p:])
        av_p.append(va[:, s_p - keep_p:])
        bk_p.append(kb)
        bv_p.append(vb)

        hs = rms_norm(xs, ln1_g[l])
        qa, ka, va, qb, kb, vb = project_qkv(hs, pos_s, w_in[l], qn_a[l], kn_a[l], qn_b[l], kn_b[l])
        ya = chunk_band_attn_sample(qa, ka, va, cache_a_k[l], cache_a_v[l], rel_bias[l])
        yb = diff_finish(diff_attn_sample(qb, kb, vb, cache_b_k[l], cache_b_v[l], lam), subln_g[l], lam_init)
        xs = xs + gated_merge(hs, ya, yb, w_gate[l], b_gate[l], w_proj_a[l], w_proj_b[l], w_out[l])
        xs = xs + sq_relu_mlp(xs, ln2_g[l], w_ff1[l], w_ff2[l])
        ak_s.append(ka)
        av_s.append(va)
        bk_s.append(kb)
        bv_s.append(vb)

    return (xp, xs,
            jnp.stack(ak_p), jnp.stack(av_p), jnp.stack(bk_p), jnp.stack(bv_p),
            jnp.stack(ak_s), jnp.stack(av_s), jnp.stack(bk_s), jnp.stack(bv_s))
```

```python
import math
import numpy as np
import concourse.bass as bass
import concourse.mybir as mybir
from concourse.bass_utils import run_bass_kernel_spmd

F32 = mybir.dt.float32
BF16 = mybir.dt.bfloat16
AF = mybir.ActivationFunctionType
ALU = mybir.AluOpType
AX = mybir.AxisListType

D = 1024
NCORES = 8
EPS = 1e-6
PAST = 1024
LAM_INIT = 0.8 - 0.6 * math.exp(0.0)
NEG = -30000.0
NSLOT = 4
DEBUG = False
DEBUG_STOP = None
PIPE = True
PIPE_A = True
PIPE_B = True


class Sched:
    ENG = ("pe", "act", "dve", "pool", "sp")

    def __init__(self):
        self.lists = {e: [] for e in self.ENG}
        self.cnt = {}
        self.known = {e: {} for e in self.ENG}
        self.lastw = {}
        self.readers = {}

    def _need(self, eng, tok):
        k, v = tok
        if self.known[eng].get(k, 0) < v:
            self.known[eng][k] = v
            self.lists[eng].append(("wait", k, v))

    def _sync(self, eng, reads, writes):
        for r in reads:
            t = self.lastw.get(r)
            if t is not None:
                self._need(eng, t)
        for w in writes:
            t = self.lastw.get(w)
            if t is not None and (t[0] != eng or eng != "pe"):
                self._need(eng, t)
            for t in self.readers.get(w, ()):
                if t[0] != eng or eng != "pe":
                    self._need(eng, t)

    def _record(self, tok, reads, writes):
        for w in writes:
            self.lastw[w] = tok
            self.readers[w] = []
        for r in reads:
            self.readers.setdefault(r, []).append(tok)

    def op(self, eng, fn, reads=(), writes=()):
        self._sync(eng, reads, writes)
        self.cnt[eng] = self.cnt.get(eng, 0) + 1
        tok = (eng, self.cnt[eng])
        self.lists[eng].append(("op", fn, eng, 1))
        self._record(tok, reads, writes)
        return tok

    def dma(self, eng, fn, n, key, reads=(), writes=()):
        self._sync(eng, reads, writes)
        self.cnt[key] = self.cnt.get(key, 0) + 16 * n
        tok = (key, self.cnt[key])
        self.lists[eng].append(("dma", fn, key, 16))
        self._record(tok, reads, writes)
        return tok

    def fence(self, eng, resources):
        for r in resources:
            t = self.lastw.get(r)
            if t is not None:
                self._need(eng, t)
            for t in self.readers.get(r, ()):
                self._need(eng, t)

    def final_wait(self, eng, keys):
        for k in keys:
            if self.cnt.get(k, 0) > 0:
                self._need(eng, (k, self.cnt[k]))


def build_program(NSEQ, S, with_sample=True):
    assert S % 512 == 0
    nc = bass.Bass("TRN2", target_bir_lowering=False)
    NSUB = S // 128
    NROPE = NSUB + 1

    def din(name, shape):
        return nc.dram_tensor(name, list(shape), F32, kind="ExternalInput").ap()

    def dout(name, shape):
        return nc.dram_tensor(name, list(shape), F32, kind="ExternalOutput").ap()

    xp = din("xp", (NSEQ, S, D))
    xs = din("xs", (2, 16, D))
    cak = din("cak", (2, 512, 512)); cav = din("cav", (2, 512, 512))
    cbk = din("cbk", (2, 1024, 512)); cbv = din("cbv", (2, 1024, 512))
    w_in = din("w_in", (D, 3072)); w_gate = din("w_gate", (D, 2048))
    w_pa = din("w_pa", (512, D)); w_pb = din("w_pb", (512, D)); w_out = din("w_out", (D, D))
    w_ff1 = din("w_ff1", (D, 4096)); w_ff2 = din("w_ff2", (4096, D))
    d_g1T = din("g1T", (128, 8)); d_g2T = din("g2T", (128, 8)); d_bgT = din("bgT", (128, 16))
    d_subT = din("subT", (128, 1))
    d_normv = din("normv", (128, 4, 64))
    d_lamv = din("lamv", (128, 4, 64))
    d_cos = din("ropec", (128, NROPE, 32)); d_sin = din("ropes", (128, NROPE, 32))
    d_biasA = din("biasA", (128, 8, 5, 128)); d_biasS = din("biasS", (128, 8, 5, 16))
    d_ident = din("ident", (128, 128))

    yp = dout("yp", (NSEQ, S, D)); ys = dout("ys", (2, 16, D))
    akp = dout("akp", (NSEQ, 512, 512)); avp = dout("avp", (NSEQ, 512, 512))
    bkp = dout("bkp", (NSEQ, S, 512)); bvp = dout("bvp", (NSEQ, S, 512))
    aks = dout("aks", (2, 16, 512)); avs = dout("avs", (2, 16, 512))
    bks = dout("bks", (2, 16, 512)); bvs = dout("bvs", (2, 16, 512))

    S_ = Sched()
    SK = max(S, 1152)
    NKT = SK // 128

    import contextlib
    with contextlib.ExitStack() as es:
        def sb(name, shape, dt):
            return es.enter_context(nc.sbuf_tensor("sb_" + name, list(shape), dt))

        g1T = sb("g1T", (128, 8), F32); g2T = sb("g2T", (128, 8), F32); bgT = sb("bgT", (128, 16), F32)
        subT = sb("subT", (128, 1), F32); subc = sb("subc", (128, 1), F32)
        normv = sb("normv", (128, 4, 64), F32)
        lamv = sb("lamv", (128, 4, 64), F32); lamt = sb("lamt", (128, 8), F32); lamj = sb("lamj", (128, 2, 64), F32)
        cosT = sb("cosT", (128, NROPE, 32), F32); sinT = sb("sinT", (128, NROPE, 32), F32)
        biasA = sb("biasA", (128, 8, 5, 128), BF16); biasS = sb("biasS", (128, 8, 5, 16), BF16)
        identf = sb("identf", (128, 128), F32); ident = sb("ident", (128, 128), BF16)
        ones_b = sb("ones_b", (128, 128), BF16); ones_f = sb("ones_f", (128, 128), F32)
        epsT = sb("epsT", (128, 1), F32)
        kaT = sb("kaT", (128, 4, 1024), BF16)
        va = sb("va", (128, 8, 512), BF16)
        kbT = sb("kbT", (128, 4, SK), BF16)
        vb = sb("vb", (128, NKT, 512), BF16)
        wring = [sb(f"w{i}", (128, 8, 512), BF16) for i in range(NSLOT)]
        xt = [sb(f"xt{i}", (128, 4, D), F32) for i in range(1)]
        hbs = [sb(f"hb{i}", (128, D), BF16) for i in range(2)]
        hT = sb("hT", (128, 8, 512), BF16)
        qT = sb("qT", (128, 8, 512), BF16)
        mT = qT
        yaT = sb("yaT", (128, 4, 512), BF16); ybT = sb("ybT", (128, 4, 512), BF16)
        qaT = qT[:, 0:4, :]; qbT = qT[:, 4:8, :]
        stat = sb("stat", (128, 64), F32)
        NFB = 10
        fb = [sb(f"fb{i}", (128, 512), F32) for i in range(NFB)]
        junk = sb("junk", (128, 1024), BF16)
        xstage = [sb(f"xstage{i}", (128, D), F32) for i in range(2)]
        hstat = [sb(f"hstat{i}", (128, 24), F32) for i in range(4)]
        rsA = sb("rsA", (128, 8), F32)
        uT = sb("uT", (128, 32, 512), BF16)
        cst = uT[:, 0:8, :]
        ya = uT[:, 21, :]
        pA = [uT[:, 22 + 2 * i:24 + 2 * i, :].rearrange("p a b -> p (a b)")[:, 0:640].rearrange("p (j q) -> p j q", j=5) for i in range(2)]
        pB = [uT[:, 18 + i, :] for i in range(3)]
        qaz = [qT[:, 0:4, :], qT[:, 4:8, :]]
        qbz = [uT[:, 8:12, :], uT[:, 26:30, :]]
        NZB = 6
        zb = [uT[:, 12 + i, :] for i in range(NZB)]

        print("SBUF bytes remaining per partition:", nc.sbuf_bytes_remaining)
        P = es.enter_context(nc.psum_tensor("P", [128, 6, 512], F32))
        PT = es.enter_context(nc.psum_tensor("PT", [128, 2, 1024], BF16))

        PTf = [PT[:, 0, :].bitcast(F32), PT[:, 1, :].bitcast(F32)]
        SB = [P[:, 4, :], P[:, 5, :], PTf[0], PTf[1]]
        SBR = [("ps", 4), ("ps", 5), ("pt", 0), ("pt", 1)]
        BDEPTH = 3
        ctr = {"xs": 0, "fb": 0, "zb": 0, "hstat": 0, "pt": 0, "ps": 0, "w": 0, "pB": 0, "hb": 0}

        def nxt(k, n):
            v = ctr[k] % n
            ctr[k] += 1
            return v

        def load_const(dst, src, name):
            S_.dma("sp", lambda e: [e.dma_start(out=dst, in_=src)], 1, "c_" + name, writes=[name])

        load_const(g1T[:], d_g1T[:, :], "g1T"); load_const(g2T[:], d_g2T[:, :], "g2T")
        load_const(bgT[:], d_bgT[:, :], "bgT"); load_const(subT[:], d_subT[:, :], "subT")
        load_const(normv[:], d_normv[:, :, :], "normv"); load_const(lamv[:], d_lamv[:, :, :], "lamv")
        load_const(cosT[:], d_cos[:, :, :], "cosT"); load_const(sinT[:], d_sin[:, :, :], "sinT")
        S_.dma("pool", lambda e: [e.dma_start(out=biasA[:], in_=d_biasA[:, :, :, :])], 1, "c_biasA", writes=["biasA"])
        S_.dma("pool", lambda e: [e.dma_start(out=biasS[:], in_=d_biasS[:, :, :, :])], 1, "c_biasS", writes=["biasS"])
        load_const(identf[:], d_ident[:, :], "identf")
        S_.op("dve", lambda e: e.tensor_copy(out=ident[:], in_=identf[:]), reads=["identf"], writes=["ident"])
        S_.op("dve", lambda e: e.tensor_scalar(out=biasA[:], in0=biasA[:], scalar1=8.0, scalar2=None, op0=ALU.mult), reads=["biasA"], writes=["biasA"])
        S_.op("dve", lambda e: e.tensor_scalar(out=biasS[:], in0=biasS[:], scalar1=8.0, scalar2=None, op0=ALU.mult), reads=["biasS"], writes=["biasS"])
        S_.op("pool", lambda e: e.memset(ones_b[:], 1.0), writes=["ones_b"])
        S_.op("pool", lambda e: e.memset(ones_f[:], 1.0), writes=["ones_f"])
        S_.op("pool", lambda e: e.memset(epsT[:], EPS), writes=["epsT"])
        S_.op("dve", lambda e: e.tensor_tensor(out=lamj[:, 0, :], in0=lamv[:, 0, :], in1=lamv[:, 1, :], op=ALU.mult),
              reads=["lamv"], writes=["lamj0"])
        S_.op("dve", lambda e: e.tensor_tensor(out=lamj[:, 1, :], in0=lamv[:, 2, :], in1=lamv[:, 3, :], op=ALU.mult),
              reads=["lamv"], writes=["lamj1"])
        S_.op("dve", lambda e: e.reduce_sum(out=lamt[:, 0:2], in_=lamj[:, :, :], axis=AX.X),
              reads=["lamj0", "lamj1"], writes=["lamt01"])
        S_.op("act", lambda e: e.activation(out=lamt[:, 2:4], in_=lamt[:, 0:2], func=AF.Exp),
              reads=["lamt01"], writes=["lamt23"])
        S_.op("dve", lambda e: e.tensor_tensor(out=lamt[:, 5:6], in0=lamt[:, 3:4], in1=lamt[:, 2:3], op=ALU.subtract),
              reads=["lamt23"], writes=["lamt5"])
        S_.op("dve", lambda e: e.tensor_scalar(out=lamt[:, 4:5], in0=lamt[:, 5:6], scalar1=-LAM_INIT, scalar2=None, op0=ALU.add),
              reads=["lamt5"], writes=["neglam"])
        S_.op("dve", lambda e: e.tensor_scalar(out=subc[:], in0=subT[:], scalar1=1.0 - LAM_INIT, scalar2=None, op0=ALU.mult),
              reads=["subT"], writes=["subc"])
        neglam = lamt[:, 4:5]

        NPIECE = 30
        wscr = nc.dram_tensor("wscr", [NPIECE, 128, 4096], BF16, kind="Internal").ap()
        pieces = []
        for g in range(6):
            pieces.append([(w_in[:, g * 512:(g + 1) * 512], 8, 0)])
        for half in range(2):
            pieces.append([(w_gate[:, half * 512:(half + 1) * 512], 8, 0)])
            pieces.append([(w_gate[:, 1024 + half * 512:1024 + (half + 1) * 512], 8, 0)])
            pieces.append([(w_pa[:, half * 512:(half + 1) * 512], 4, 0), (w_pb[:, half * 512:(half + 1) * 512], 4, 4)])
        for g in range(2):
            pieces.append([(w_out[:, g * 512:(g + 1) * 512], 8, 0)])
        for i in range(8):
            pieces.append([(w_ff1[:, i * 512:(i + 1) * 512], 8, 0)])
        for g in range(2):
            for i in range(4):
                pieces.append([(w_ff2[i * 1024:(i + 1) * 1024, g * 512:(g + 1) * 512], 8, 0)])
        assert len(pieces) == NPIECE
        for k, parts in enumerate(pieces):
            def cv(e, k=k, parts=parts):
                out = []
                for (src, kc, k0) in parts:
                    dst = wscr[k].rearrange("p (k n) -> p k n", k=8)[:, k0:k0 + kc, :]
                    out.append(e.dma_start(out=dst, in_=src.rearrange("(k p) n -> p k n", p=128)))
                return out
            S_.dma("pool", cv, len(parts), f"cv{k}", writes=[("wscr", k)])

        wstate = {"emitted": 0, "total": 0, "released": -1, "cur": -1}
        PF = 2

        def wpump():
            while (wstate["emitted"] <= min(wstate["cur"] + PF, wstate["total"] - 1)
                   and wstate["emitted"] - NSLOT <= wstate["released"]):
                m = wstate["emitted"]
                k = m % NPIECE
                i = m % NSLOT
                S_.dma("sp", lambda e, i=i, k=k: [e.dma_start(out=wring[i][:].rearrange("p a b -> p (a b)"), in_=wscr[k])],
                       1, f"w{i}", reads=[("wscr", k)], writes=[("w", i)])
                wstate["emitted"] += 1

        def wget(n):
            wstate["cur"] = max(wstate["cur"], n)
            wpump()
            assert wstate["emitted"] > n, (n, wstate)
            i = n % NSLOT
            return i, wring[i]

        def wdone(n):
            wstate["released"] = max(wstate["released"], n)
            wpump()

        def rms_rows(xin_ap, np_, width, stat_ap, rd, wr_name):
            S_.op("act", lambda e: e.activation(out=junk[0:np_, 0:width], in_=xin_ap, func=AF.Square,
                                                scale=1.0 / math.sqrt(width), accum_out=stat_ap),
                  reads=rd, writes=[wr_name + "_ms", "junk"])
            S_.op("act", lambda e: e.activation(out=stat_ap, in_=stat_ap, func=AF.Sqrt, bias=epsT[0:np_, 0:1], scale=1.0),
                  reads=[wr_name + "_ms", "epsT"], writes=[wr_name + "_sd"])
            S_.op("dve", lambda e: e.reciprocal(out=stat_ap, in_=stat_ap), reads=[wr_name + "_sd"], writes=[wr_name])

        def to_fm(src_tm, np_, nch, gT, dst_fn, rd, wr, evac="act", split=None):
            b = nxt("pt", 2)

            def tr(e):
                ins = None
                for c in range(nch):
                    ins = e.transpose(out=PT[:, b, c * 128:c * 128 + np_], in_=src_tm[:, c * 128:(c + 1) * 128],
                                      identity=ident[0:np_, 0:np_])
                return ins
            S_.op("pe", tr, reads=rd + ["ident"], writes=[("pt", b)])
            src = PT[:, b, 0:nch * 128].rearrange("p (c t) -> p c t", c=nch)[:, :, 0:np_]
            if split is not None:
                dA, dB = split
                S_.op("act", lambda e: e.activation(out=dA()[0:64], in_=src[0:64], func=AF.Copy), reads=[("pt", b)], writes=[wr[0]])
                S_.op("dve", lambda e: e.tensor_copy(out=dB()[64:128], in_=src[64:128]), reads=[("pt", b)], writes=[(wr[0][0] + "_hi", wr[0][1])])
            elif gT is None and evac == "dve":
                S_.op("dve", lambda e: e.tensor_copy(out=dst_fn(), in_=src), reads=[("pt", b)], writes=wr)
            elif gT is None:
                S_.op("act", lambda e: e.activation(out=dst_fn(), in_=src, func=AF.Copy), reads=[("pt", b)], writes=wr)
            else:
                S_.op("dve", lambda e: e.tensor_tensor(out=dst_fn(), in0=src,
                                                       in1=gT.unsqueeze(2).broadcast_to([128, nch, np_]), op=ALU.mult),
                      reads=[("pt", b), "g1T", "g2T"], writes=wr)

        def pipeline(items, PIPE=True, depth=1):
            n = len(items)
            if not PIPE:
                depth = 0
            for i in range(min(depth, n)):
                items[i][0]()
            for i in range(n):
                if i + depth < n:
                    items[i + depth][0]()
                items[i][1]()
                items[i][2]()
                for fn in items[i][3]:
                    fn()

        class Tile:
            pass

        def make_tile(kind, seq, t0, subs, tidx):
            T = Tile()
            wb0 = tidx * NPIECE
            nsub = len(subs)
            np_ = subs[0]
            NT = nsub * np_
            X = xt[0]
            col = [i * np_ for i in range(nsub)]

            def load_xs(s):
                i = nxt("xs", 2)
                if kind == "p":
                    src = xp[seq, t0 + s * 128:t0 + (s + 1) * 128, :]
                else:
                    src = xs[s, :, :]
                S_.dma("sp", lambda e: [e.dma_start(out=xstage[i][0:np_, :], in_=src)], 1, f"xs{i}", writes=[("xs", i)])
                return i

            hb_of = {}

            def p1a_A(s):
                xi = load_xs(s)
                st = stat[0:np_, s:s + 1]
                rms_rows(xstage[xi][0:np_, :], np_, D, st, [("xs", xi)], f"rs1_{s}")
                hi = nxt("hb", 2)
                hb_of[s] = hi
                S_.op("dve", lambda e, st=st, hi=hi, xi=xi: e.tensor_scalar(out=hbs[hi][0:np_, :], in0=xstage[xi][0:np_, :], scalar1=st,
                                                                          scalar2=None, op0=ALU.mult),
                      reads=[("xs", xi), f"rs1_{s}"], writes=[("hb", hi)])

            def p1a_B(s):
                hi = hb_of[s]
                to_fm(hbs[hi][0:np_, :], np_, 8, g1T[:, :], lambda s=s: hT[:, :, col[s]:col[s] + np_],
                      [("hb", hi)], [("hT", s)])

            def p1a():
                for s in range(nsub):
                    p1a_A(s)
                    p1a_B(s)
            T.p1a = p1a
            T.p1a_A = p1a_A
            T.p1a_B = p1a_B
            T.nsub = nsub

            def make_item(g, s, grp):
                I = Tile()
                stt = {}
                if kind == "p":
                    ktg = t0 // 128 + s
                    slotA = ktg % 8
                    kcolB = ktg * 128
                    ktB = ktg
                    pos_idx = ktg
                else:
                    slotA = 4; kcolB = 1024; ktB = 8; pos_idx = NSUB
                isnorm = g in (0, 1, 3, 4)
                nvec = {0: 0, 1: 1, 3: 2, 4: 3}.get(g)
                rope_idx = pos_idx if g >= 3 else None

                def A():
                    if s == 0:
                        grp["w"] = wget(wb0 + g)
                    wi, W = grp["w"]
                    b = nxt("ps", 6)

                    def mm(e):
                        ins = None
                        for kc in range(8):
                            ins = e.matmul(P[0:np_, b, :], lhsT=hT[:, kc, col[s]:col[s] + np_], rhs=W[:, kc, :],
                                           start=(kc == 0), stop=(kc == 7))
                        return ins
                    S_.op("pe", mm, reads=[("hT", s), ("w", wi)], writes=[("ps", b)])
                    if s == nsub - 1:
                        wdone(wb0 + g)
                    z = nxt("fb", NFB)
                    stt["z"] = z
                    S_.op("act", lambda e: e.activation(out=fb[z][0:np_, :], in_=P[0:np_, b, :], func=AF.Copy),
                          reads=[("ps", b)], writes=[("fb", z)])
                    if isnorm:
                        j = nxt("fb", NFB)
                        hs = nxt("hstat", 4)
                        stt["hs"] = hs
                        S_.op("act", lambda e: e.activation(out=fb[j][0:np_, :], in_=P[0:np_, b, :], func=AF.Square),
                              reads=[("ps", b)], writes=[("fb", j)])
                        S_.op("dve", lambda e: e.reduce_sum(out=hstat[hs][0:np_, 0:8],
                                                            in_=fb[j][0:np_, :].rearrange("p (h d) -> p h d", h=8), axis=AX.X),
                              reads=[("fb", j)], writes=[("hstat", hs)])
                I.A = A

                def B():
                    if not isnorm:
                        return
                    z, hs = stt["z"], stt["hs"]
                    Z = fb[z]
                    S_.op("act", lambda e: e.activation(out=hstat[hs][0:np_, 8:16], in_=hstat[hs][0:np_, 0:8], func=AF.Sqrt,
                                                        bias=epsT[0:np_, 0:1], scale=1.0 / 64),
                          reads=[("hstat", hs), "epsT"], writes=[("hstat_b", hs)])
                    S_.op("dve", lambda e: e.reciprocal(out=hstat[hs][0:np_, 16:24], in_=hstat[hs][0:np_, 8:16]),
                          reads=[("hstat_b", hs)], writes=[("hstat_c", hs)])
                    S_.op("dve", lambda e: e.tensor_tensor(out=Z[0:np_, :].rearrange("p (h d) -> p h d", h=8),
                                                           in0=Z[0:np_, :].rearrange("p (h d) -> p h d", h=8),
                                                           in1=hstat[hs][0:np_, 16:24].unsqueeze(2).broadcast_to([np_, 8, 64]),
                                                           op=ALU.mult),
                          reads=[("hstat_c", hs), ("fb", z)], writes=[("fb", z)])
                    S_.op("pool", lambda e: e.tensor_tensor(out=Z[0:np_, :].rearrange("p (h d) -> p h d", h=8),
                                                            in0=Z[0:np_, :].rearrange("p (h d) -> p h d", h=8),
                                                            in1=normv[0:np_, nvec, :].unsqueeze(1).broadcast_to([np_, 8, 64]), op=ALU.mult),
                          reads=[("fb", z), "normv"], writes=[("fb", z)])
                    if rope_idx is None:
                        stt["Rf"], stt["rres"] = Z, ("fb", z)
                        return
                    r = nxt("fb", NFB)
                    R = fb[r]
                    Zv = Z[0:np_, :].rearrange("p (h t d) -> p h t d", h=8, t=2)
                    Rv = R[0:np_, :].rearrange("p (h t d) -> p h t d", h=8, t=2)
                    sn = sinT[0:np_, rope_idx, :].unsqueeze(1).broadcast_to([np_, 8, 32])
                    j2 = nxt("fb", NFB)
                    Jv = fb[j2][0:np_, :].rearrange("p (h t d) -> p h t d", h=8, t=2)
                    S_.op("dve", lambda e: e.tensor_tensor(out=R[0:np_, :].rearrange("p (g d) -> p g d", g=16),
                                                           in0=Z[0:np_, :].rearrange("p (g d) -> p g d", g=16),
                                                           in1=cosT[0:np_, rope_idx, :].unsqueeze(1).broadcast_to([np_, 16, 32]), op=ALU.mult),
                          reads=[("fb", z), "cosT"], writes=[("fb", r)])
                    S_.op("pool", lambda e: e.tensor_tensor(out=Jv[:, :, 0, :], in0=Zv[:, :, 1, :], in1=sn, op=ALU.mult),
                          reads=[("fb", z), "sinT"], writes=[("fb", j2)])
                    S_.op("pool", lambda e: e.tensor_tensor(out=Jv[:, :, 1, :], in0=Zv[:, :, 0, :], in1=sn, op=ALU.mult),
                          reads=[("fb", z), "sinT"], writes=[("fb", j2, 1)])
                    S_.op("dve", lambda e: e.tensor_tensor(out=Rv[:, :, 0, :], in0=Rv[:, :, 0, :], in1=Jv[:, :, 0, :], op=ALU.subtract),
                          reads=[("fb", r), ("fb", j2)], writes=[("fb", r)])
                    S_.op("dve", lambda e: e.tensor_tensor(out=Rv[:, :, 1, :], in0=Rv[:, :, 1, :], in1=Jv[:, :, 1, :], op=ALU.add),
                          reads=[("fb", r), ("fb", j2, 1)], writes=[("fb", r)])
                    S_.lastw[("fb", j2)] = S_.lastw[("fb", r)]
                    S_.readers[("fb", j2)] = []
                    stt["Rf"], stt["rres"] = R, ("fb", r)
                I.B = B

                def C(pend):
                    z = stt["z"]
                    if isnorm:
                        Rf, rres = stt["Rf"], stt["rres"]
                        okey = f"o_fb{rres[1]}"
                        if g == 1:
                            if kind == "p" and t0 >= S - 512:
                                r0 = t0 - (S - 512) + s * 128
                                S_.dma("sp", lambda e: [e.dma_start(out=akp[seq, r0:r0 + 128, :], in_=Rf[0:np_, :])], 1, okey, reads=[rres])
                            elif kind == "s":
                                S_.dma("sp", lambda e: [e.dma_start(out=aks[s, :, :], in_=Rf[0:np_, :])], 1, okey, reads=[rres])
                        if g == 4:
                            if kind == "p":
                                S_.dma("sp", lambda e: [e.dma_start(out=bkp[seq, t0 + s * 128:t0 + (s + 1) * 128, :], in_=Rf[0:np_, :])],
                                       1, okey, reads=[rres])
                            else:
                                S_.dma("sp", lambda e: [e.dma_start(out=bks[s, :, :], in_=Rf[0:np_, :])], 1, okey, reads=[rres])
                        zi = nxt("zb", NZB)
                        S_.op("act", lambda e: e.activation(out=zb[zi][0:np_, :], in_=Rf[0:np_, :], func=AF.Copy),
                              reads=[rres], writes=[("zb", zi)])
                        if g == 0:
                            dst = lambda: qaT[:, :, col[s]:col[s] + np_]
                            wr = [("qaT", s)]
                        elif g == 3:
                            dst = lambda: qbT[:, :, col[s]:col[s] + np_]
                            wr = [("qbT", s)]
                        elif g == 1:
                            if kind == "p":
                                dst = lambda: kaT[:, :, slotA * 128:slotA * 128 + np_]
                                wr = [("kaT", slotA)]
                            else:
                                dst = lambda: qaT[:, :, 32 + col[s]:32 + col[s] + np_]
                                wr = [("kaS", s)]
                        else:
                            if kind == "p":
                                dst = lambda: kbT[:, :, kcolB:kcolB + np_]
                                wr = [("kbT", ktB)]
                            else:
                                dst = lambda: qbT[:, :, 32 + col[s]:32 + col[s] + np_]
                                wr = [("kbS", s)]
                        if g == 0:
                            pend.append(lambda: to_fm(zb[zi][0:np_, :], np_, 4, None, None, [("zb", zi)], wr,
                                                      split=(lambda: qaz[0][:, :, col[s]:col[s] + np_], lambda: qaz[1][:, :, col[s]:col[s] + np_])))
                        elif g == 3:
                            pend.append(lambda: to_fm(zb[zi][0:np_, :], np_, 4, None, None, [("zb", zi)], wr,
                                                      split=(lambda: qbz[0][:, :, col[s]:col[s] + np_], lambda: qbz[1][:, :, col[s]:col[s] + np_])))
                        else:
                            pend.append(lambda: to_fm(zb[zi][0:np_, :], np_, 4, None, dst, [("zb", zi)], wr, evac=("dve" if (g + s) % 2 else "act")))
                    elif g == 2:
                        if kind == "p" and t0 >= S - 512:
                            r0 = t0 - (S - 512) + s * 128
                            S_.dma("sp", lambda e: [e.dma_start(out=avp[seq, r0:r0 + 128, :], in_=fb[z][0:np_, :])], 1, f"o_fb{z}", reads=[("fb", z)])
                        elif kind == "s":
                            S_.dma("sp", lambda e: [e.dma_start(out=avs[s, :, :], in_=fb[z][0:np_, :])], 1, f"o_fb{z}", reads=[("fb", z)])
                        if kind == "p":
                            S_.op("pool", lambda e: e.tensor_copy(out=va[0:np_, slotA, :], in_=fb[z][0:np_, :]), reads=[("fb", z)], writes=[("va", slotA)])
                        else:
                            S_.op("pool", lambda e: e.tensor_copy(out=cst[0:np_, 6 + s, :], in_=fb[z][0:np_, :]), reads=[("fb", z)], writes=[("vaS", s)])
                    else:
                        if kind == "p":
                            S_.dma("sp", lambda e: [e.dma_start(out=bvp[seq, t0 + s * 128:t0 + (s + 1) * 128, :], in_=fb[z][0:np_, :])],
                                   1, f"o_fb{z}", reads=[("fb", z)])
                            S_.op("pool", lambda e: e.tensor_copy(out=vb[0:np_, ktB, :], in_=fb[z][0:np_, :]), reads=[("fb", z)], writes=[("vb", ktB)])
                        else:
                            S_.dma("sp", lambda e: [e.dma_start(out=bvs[s, :, :], in_=fb[z][0:np_, :])], 1, f"o_fb{z}", reads=[("fb", z)])
                            S_.op("pool", lambda e: e.tensor_copy(out=cst[0:np_, 4 + s, :], in_=fb[z][0:np_, :]), reads=[("fb", z)], writes=[("vbS", s)])
                I.C = C
                return I

            def p1b():
                S_.op("pool", lambda e: e.memset(qaz[0][64:128], 0.0), writes=[("mT", m2) for m2 in range(4)] + ["qzero0"])
                S_.op("pool", lambda e: e.memset(qaz[1][0:64], 0.0), writes=[("mT", m2) for m2 in range(4, 8)] + ["qzero1"])
                S_.op("pool", lambda e: e.memset(qbz[0][64:128], 0.0), writes=[("uT", j2) for j2 in range(8, 12)] + ["qzero2"])
                S_.op("pool", lambda e: e.memset(qbz[1][0:64], 0.0), writes=[("uT", j2) for j2 in range(26, 30)] + ["qzero3"])
                items = []
                for g in range(6):
                    grp = {}
                    for s in range(nsub):
                        items.append(make_item(g, s, grp))
                n = len(items)
                pend = []
                for i in range(n + 2):
                    if i < n:
                        items[i].A()
                    if 0 <= i - 1 < n:
                        items[i - 1].B()
                    if 0 <= i - 2 < n:
                        items[i - 2].C(pend)
                    while len(pend) > 4:
                        pend.pop(0)()
                while pend:
                    pend.pop(0)()
            T.p1b = p1b

            def attn_a():
                nq = np_
                pso, pss = 4, 5
                pend = []

                def sub_items(s):
                    if kind == "p":
                        qt = t0 // 128 + s
                        kts = [k for k in range(qt - 4, qt + 1) if k >= 0]
                        jlist = [k - (qt - 4) for k in kts]
                        ktinfo = [(k % 8, 128) for k in kts]
                        btab = biasA
                        hist_rd = [("kaT", k % 8) for k in kts]
                        hist_rv = [("va", k % 8) for k in kts]
                    else:
                        jlist = [0, 1, 2, 3, 4]
                        ktinfo = [(0, 128), (1, 128), (2, 128), (3, 128), (4, 16)]
                        btab = biasS
                        hist_rd = [("kaT", k) for k in range(5)]
                        hist_rv = [("va", k) for k in range(5)]
                    c0 = col[s]
                    its = []
                    for h in range(8):
                        c, po = h // 2, (h % 2) * 64
                        ab = h % 2
                        bA, bB = (0, 1) if ab == 0 else (2, 3)

                        def s1(c=c, po=po, bA=bA, bB=bB, h=h):
                            def qk(e):
                                ins = None
                                for (j, (slot, nk)) in zip(jlist, ktinfo):
                                    bank, cc = (bA, j * 128) if j < 4 else (bB, 0)
                                    e.matmul(P[0:nk, bank, cc:cc + nq], lhsT=kaT[:, c, slot * 128:slot * 128 + nk],
                                             rhs=qaz[h % 2][:, c, c0:c0 + nq], start=True, stop=False)
                                    ins = e.matmul(P[0:nk, bank, cc:cc + nq], lhsT=ident[0:nk, 0:nk], rhs=btab[0:nk, h, j, 0:nq],
                                                   start=False, stop=True)
                                return ins
                            S_.op("pe", qk, reads=hist_rd + [("qaT", s), ("qaT_hi", s), "qzero0", "qzero1", "ident", "biasA", "biasS"], writes=[("ps", bA), ("ps", bB)])

                        def s2(bA=bA, bB=bB, ab=ab):
                            js = [j for j in jlist if j < 4]
                            if js:
                                jm = min(js)
                                S_.op("act", lambda e: e.activation(out=pA[ab][:, jm:4, 0:nq],
                                                                    in_=P[:, bA, :].rearrange("p (j q) -> p j q", j=4)[:, jm:4, 0:nq],
                                                                    func=AF.Exp, scale=0.125),
                                      reads=[("ps", bA)], writes=[("pA", ab, 0)])
                            nk4 = ktinfo[-1][1]
                            S_.op("act", lambda e: e.activation(out=pA[ab][0:nk4, 4, 0:nq], in_=P[0:nk4, bB, 0:nq], func=AF.Exp, scale=0.125),
                                  reads=[("ps", bB)], writes=[("pA", ab, 1)])

                        def s3(h=h, ab=ab):
                            def pv(e):
                                ins = None
                                n = len(jlist)
                                for i, (j, (slot, nk)) in enumerate(zip(jlist, ktinfo)):
                                    e.matmul(P[0:nq, pso, h * 64:(h + 1) * 64], lhsT=pA[ab][0:nk, j, 0:nq], rhs=va[0:nk, slot, h * 64:(h + 1) * 64],
                                             start=(i == 0), stop=(i == n - 1))
                                    ins = e.matmul(P[0:nq, pss, 2 * h:2 * h + 2], lhsT=pA[ab][0:nk, j, 0:nq], rhs=ones_b[0:nk, 0:2],
                                                   start=(i == 0), stop=(i == n - 1))
                                return ins
                            S_.op("pe", pv, reads=[("pA", ab, 0), ("pA", ab, 1)] + hist_rv + ["ones_b"], writes=[("ps", pso), ("ps", pss)])
                        post = []
                        if h == 1:
                            post.append(lambda: pend.pop()() if pend else None)
                        if h == 7:
                            def fin(s=s, c0=c0):
                                S_.op("dve", lambda e: e.reciprocal(out=rsA[0:nq, :], in_=P[0:nq, pss, 0:16].rearrange("p (h t) -> p h t", t=2)[:, :, 0]),
                                      reads=[("ps", pss)], writes=["rsA"])
                                S_.op("dve", lambda e: e.tensor_tensor(out=ya[0:nq, :].rearrange("p (h d) -> p h d", h=8),
                                                                       in0=P[0:nq, pso, :].rearrange("p (h d) -> p h d", h=8),
                                                                       in1=rsA[0:nq, :].unsqueeze(2).broadcast_to([nq, 8, 64]), op=ALU.mult),
                                      reads=[("ps", pso), "rsA"], writes=["ya"])
                                pend.append(lambda: to_fm(ya[0:nq, :], nq, 4, None, lambda: yaT[:, :, c0:c0 + nq], ["ya"], [("yaT", s)]))
                            post.append(fin)
                        its.append((s1, s2, s3, post))
                    return its

                if kind == "p":
                    allit = []
                    for s in range(nsub):
                        allit += sub_items(s)
                    pipeline(allit, PIPE_A)
                    while pend:
                        pend.pop()()
                else:
                    for s in range(nsub):
                        S_.dma("pool", lambda e, s=s: [e.dma_start(out=cst[:, 0:4, :], in_=cak[s].rearrange("(k p) n -> p k n", p=128))],
                               1, "cstA", writes=[("cst", k) for k in range(4)])
                        for k in range(4):
                            to_fm(cst[:, k, :], 128, 4, None, lambda k=k: kaT[:, :, k * 128:(k + 1) * 128], [("cst", k)], [("kaT", k)])
                        S_.dma("pool", lambda e, s=s: [e.dma_start(out=va[:, 0:4, :], in_=cav[s].rearrange("(k p) n -> p k n", p=128))],
                               1, "vaL", writes=[("va", k) for k in range(4)])
                        S_.op("act", lambda e, s=s: e.activation(out=kaT[:, :, 512:528], in_=qaT[:, :, 32 + col[s]:32 + col[s] + 16], func=AF.Copy),
                              reads=[("kaS", s)], writes=[("kaT", 4)])
                        S_.op("pool", lambda e, s=s: e.tensor_copy(out=va[0:16, 4, :], in_=cst[0:16, 6 + s, :]),
                              reads=[("vaS", s)], writes=[("va", 4)])
                        pipeline(sub_items(s), PIPE_A)
                        while pend:
                            pend.pop()()
            T.attn_a = attn_a

            def attn_b():
                if kind == "p":
                    groups = [(0, NT, list(range(0, (t0 + 512) // 128)), None)]
                else:
                    groups = [(col[s], 16, list(range(9)), s) for s in range(nsub)]
                pend2 = []

                def make_comb1(h, qc0, Nq):
                    def comb1(h=h, qc0=qc0, Nq=Nq):
                        f0, f1, f2, f3, f4 = [nxt("fb", NFB) for _ in range(5)]
                        S_.op("dve", lambda e: e.reciprocal(out=fb[f0][:, 0:Nq], in_=P[:, 1, 0:Nq]), reads=[("ps", 1)], writes=[("fb", f0)])
                        S_.op("dve", lambda e: e.tensor_tensor(out=fb[f1][:, 0:Nq], in0=P[:, 0, 0:Nq], in1=fb[f0][:, 0:Nq], op=ALU.mult),
                              reads=[("ps", 0), ("fb", f0)], writes=[("fb", f1)])
                        S_.op("dve", lambda e: e.reciprocal(out=fb[f2][:, 0:Nq], in_=P[:, 3, 0:Nq]), reads=[("ps", 3)], writes=[("fb", f2)])
                        S_.op("dve", lambda e: e.tensor_tensor(out=fb[f3][:, 0:Nq], in0=P[:, 2, 0:Nq], in1=fb[f2][:, 0:Nq], op=ALU.mult),
                              reads=[("ps", 2), ("fb", f2)], writes=[("fb", f3)])
                        S_.op("dve", lambda e: e.scalar_tensor_tensor(out=fb[f4][:, 0:Nq], in0=fb[f3][:, 0:Nq], scalar=neglam,
                                                                      in1=fb[f1][:, 0:Nq], op0=ALU.mult, op1=ALU.add),
                              reads=[("fb", f3), ("fb", f1), "neglam"], writes=[("fb", f4)])

                        def comb2(nb):
                            S_.op("act", lambda e: e.activation(out=fb[f0][:, 0:Nq], in_=fb[f4][:, 0:Nq], func=AF.Square),
                                  reads=[("fb", f4)], writes=[("fb", f0)])
                            S_.op("pe", lambda e: e.matmul(SB[nb][:, 0:Nq], lhsT=ones_f[:, :], rhs=fb[f0][:, 0:Nq], start=True, stop=True),
                                  reads=[("fb", f0), "ones_f"], writes=[SBR[nb]])
                            S_.op("act", lambda e: e.activation(out=fb[f2][:, 0:Nq], in_=SB[nb][:, 0:Nq], func=AF.Sqrt, bias=epsT[:, 0:1],
                                                                scale=1.0 / 128),
                                  reads=[SBR[nb], "epsT"], writes=[("fb", f2)])
                            S_.op("dve", lambda e: e.reciprocal(out=fb[f3][:, 0:Nq], in_=fb[f2][:, 0:Nq]), reads=[("fb", f2)], writes=[("fb", f3)])
                            S_.op("dve", lambda e: e.scalar_tensor_tensor(out=ybT[:, h, qc0:qc0 + Nq], in0=fb[f4][:, 0:Nq], scalar=subc[:, 0:1],
                                                                          in1=fb[f3][:, 0:Nq], op0=ALU.mult, op1=ALU.mult),
                                  reads=[("fb", f4), ("fb", f3), "subc"], writes=[("ybT", h)])
                        pend2.append(comb2)
                    return comb1

                for (qc0, nqb, ktl, ss) in groups:
                    if kind == "s":
                        S_.dma("pool", lambda e, ss=ss: [e.dma_start(out=cst[:, 0:4, :], in_=cbk[ss, 0:512].rearrange("(k p) n -> p k n", p=128))],
                               1, "cstA", writes=[("cst", k) for k in range(4)])
                        for k in range(4):
                            to_fm(cst[:, k, :], 128, 4, None, lambda k=k: kbT[:, :, k * 128:(k + 1) * 128], [("cst", k)], [("kbT", k)])
                        S_.dma("pool", lambda e, ss=ss: [e.dma_start(out=cst[:, 0:4, :], in_=cbk[ss, 512:1024].rearrange("(k p) n -> p k n", p=128))],
                               1, "cstA", writes=[("cst", k) for k in range(4)])
                        for k in range(4):
                            to_fm(cst[:, k, :], 128, 4, None, lambda k=k: kbT[:, :, (4 + k) * 128:(5 + k) * 128], [("cst", k)], [("kbT", 4 + k)])
                        S_.dma("pool", lambda e, ss=ss: [e.dma_start(out=vb[:, 0:8, :], in_=cbv[ss].rearrange("(k p) n -> p k n", p=128))],
                               1, "vbL", writes=[("vb", k) for k in range(8)])
                        S_.op("act", lambda e, ss=ss: e.activation(out=kbT[:, :, 1024:1040], in_=qbT[:, :, 32 + col[ss]:32 + col[ss] + 16], func=AF.Copy),
                              reads=[("kbS", ss)], writes=[("kbT", 8)])
                        S_.op("pool", lambda e, ss=ss: e.tensor_copy(out=vb[0:16, 8, :], in_=cst[0:16, 4 + ss, :]),
                              reads=[("vbS", ss)], writes=[("vb", 8)])
                    Nq = nqb
                    its = []
                    for h in range(4):
                        for r in range(2):
                            accO, accS = (0, 1) if r == 0 else (2, 3)
                            if kind == "s":
                                idx = len(its)
                                st = {}

                                def s1(h=h, r=r, st=st, qc0=qc0, idx=idx):
                                    sb_ = idx % (BDEPTH + 1)
                                    st["sb"] = sb_

                                    def sc(e):
                                        ins = None
                                        for kt in ktl:
                                            nk = 16 if kt == 8 else 128
                                            ins = e.matmul(SB[sb_][0:nk, kt * 16:(kt + 1) * 16], lhsT=kbT[:, h, kt * 128:kt * 128 + nk],
                                                           rhs=qbz[r][:, h, qc0:qc0 + 16], start=True, stop=True)
                                        return ins
                                    S_.op("pe", sc, reads=[("kbT", kt) for kt in ktl] + ["qzero2", "qzero3"] + [("qbT", s2) for s2 in range(nsub)]
                                          + [("qbT_hi", s2) for s2 in range(nsub)], writes=[SBR[sb_]])

                                def s2(st=st):
                                    st["pi"] = nxt("pB", 3)
                                    pi, sb_ = st["pi"], st["sb"]
                                    S_.op("act", lambda e: e.activation(out=pB[pi][:, 0:128], in_=SB[sb_][:, 0:128], func=AF.Exp, scale=0.125),
                                          reads=[SBR[sb_]], writes=[("pB", pi)])
                                    S_.op("act", lambda e: e.activation(out=pB[pi][0:16, 128:144], in_=SB[sb_][0:16, 128:144], func=AF.Exp, scale=0.125),
                                          reads=[SBR[sb_]], writes=[("pB", pi, 1)])

                                def s3(h=h, st=st, accO=accO, accS=accS):
                                    pi = st["pi"]

                                    def pvb(e):
                                        ins = None
                                        for ki, kt in enumerate(ktl):
                                            nk = 16 if kt == 8 else 128
                                            e.matmul(P[:, accO, 0:16], lhsT=vb[0:nk, kt, h * 128:(h + 1) * 128], rhs=pB[pi][0:nk, kt * 16:(kt + 1) * 16],
                                                     start=(ki == 0), stop=(ki == len(ktl) - 1), skip_group_check=True)
                                            ins = e.matmul(P[:, accS, 0:16], lhsT=ones_b[0:nk, :], rhs=pB[pi][0:nk, kt * 16:(kt + 1) * 16],
                                                           start=(ki == 0), stop=(ki == len(ktl) - 1), skip_group_check=True)
                                        return ins
                                    S_.op("pe", pvb, reads=[("pB", pi), ("pB", pi, 1), "ones_b"] + [("vb", kt) for kt in ktl],
                                          writes=[("ps", accO), ("ps", accS)])
                                post = []
                                if r == 0:
                                    post.append(lambda idx=idx: pend2.pop()(idx % (BDEPTH + 1)) if pend2 else None)
                                if r == 1:
                                    post.append(make_comb1(h, qc0, Nq))
                                its.append((s1, s2, s3, post))
                                continue
                            for ki, kt in enumerate(ktl):
                                nk = 16 if (kind == "s" and kt == 8) else 128
                                if kind == "p":
                                    qlo = max(0, kt * 128 - t0)
                                    diag = kt * 128 >= t0
                                else:
                                    qlo = 0
                                    diag = False
                                N = nqb - qlo
                                first, last = (ki == 0), (ki == len(ktl) - 1)
                                st = {}

                                idx = len(its)

                                def s1(h=h, r=r, kt=kt, nk=nk, qlo=qlo, N=N, st=st, qc0=qc0, idx=idx):
                                    st["sb"] = idx % (BDEPTH + 1)
                                    sb_ = st["sb"]
                                    S_.op("pe", lambda e: e.matmul(
                                        SB[sb_][0:nk, 0:N], lhsT=kbT[:, h, kt * 128:kt * 128 + nk],
                                        rhs=qbz[r][:, h, qc0 + qlo:qc0 + qlo + N], start=True, stop=True),
                                        reads=[("kbT", kt), "qzero2", "qzero3"] + [("qbT", s2) for s2 in range(nsub)] + [("qbT_hi", s2) for s2 in range(nsub)],
                                        writes=[SBR[sb_]])

                                def s2(nk=nk, N=N, st=st, diag=diag):
                                    st["pi"] = nxt("pB", 3)
                                    pi, sb_ = st["pi"], st["sb"]
                                    S_.op("act", lambda e: e.activation(out=pB[pi][0:nk, 0:N], in_=SB[sb_][0:nk, 0:N], func=AF.Exp, scale=0.125),
                                          reads=[SBR[sb_]], writes=[("pB", pi)])
                                    if diag:
                                        S_.op("act", lambda e: e.memzero(pB[pi][64:128, 0:64]), reads=[], writes=[("pB", pi)])

                                def s3(h=h, kt=kt, nk=nk, qlo=qlo, N=N, st=st, first=first, last=last, accO=accO, accS=accS):
                                    pi = st["pi"]

                                    def pvb(e):
                                        e.matmul(P[:, accO, qlo:qlo + N], lhsT=vb[0:nk, kt, h * 128:(h + 1) * 128], rhs=pB[pi][0:nk, 0:N],
                                                 start=first, stop=last, skip_group_check=True)
                                        return e.matmul(P[:, accS, qlo:qlo + N], lhsT=ones_b[0:nk, :], rhs=pB[pi][0:nk, 0:N],
                                                        start=first, stop=last, skip_group_check=True)
                                    S_.op("pe", pvb, reads=[("pB", pi), ("vb", kt), "ones_b"], writes=[("ps", accO), ("ps", accS)])
                                post = []
                                if r == 0 and ki == min(2, len(ktl) - 1):
                                    post.append(lambda idx=idx: pend2.pop()(idx % (BDEPTH + 1)) if pend2 else None)
                                if r == 1 and last:
                                    post.append(make_comb1(h, qc0, Nq))
                                if False:
                                    def comb1(h=h, qc0=qc0, Nq=Nq):
                                        f0, f1, f2, f3, f4 = [nxt("fb", NFB) for _ in range(5)]
                                        S_.op("dve", lambda e: e.reciprocal(out=fb[f0][:, 0:Nq], in_=P[:, 1, 0:Nq]), reads=[("ps", 1)], writes=[("fb", f0)])
                                        S_.op("dve", lambda e: e.tensor_tensor(out=fb[f1][:, 0:Nq], in0=P[:, 0, 0:Nq], in1=fb[f0][:, 0:Nq], op=ALU.mult),
                                              reads=[("ps", 0), ("fb", f0)], writes=[("fb", f1)])
                                        S_.op("dve", lambda e: e.reciprocal(out=fb[f2][:, 0:Nq], in_=P[:, 3, 0:Nq]), reads=[("ps", 3)], writes=[("fb", f2)])
                                        S_.op("dve", lambda e: e.tensor_tensor(out=fb[f3][:, 0:Nq], in0=P[:, 2, 0:Nq], in1=fb[f2][:, 0:Nq], op=ALU.mult),
                                              reads=[("ps", 2), ("fb", f2)], writes=[("fb", f3)])
                                        S_.op("dve", lambda e: e.scalar_tensor_tensor(out=fb[f4][:, 0:Nq], in0=fb[f3][:, 0:Nq], scalar=neglam,
                                                                                      in1=fb[f1][:, 0:Nq], op0=ALU.mult, op1=ALU.add),
                                              reads=[("fb", f3), ("fb", f1), "neglam"], writes=[("fb", f4)])

                                        def comb2(nb):
                                            S_.op("act", lambda e: e.activation(out=fb[f0][:, 0:Nq], in_=fb[f4][:, 0:Nq], func=AF.Square),
                                                  reads=[("fb", f4)], writes=[("fb", f0)])
                                            S_.op("pe", lambda e: e.matmul(SB[nb][:, 0:Nq], lhsT=ones_f[:, :], rhs=fb[f0][:, 0:Nq], start=True, stop=True),
                                                  reads=[("fb", f0), "ones_f"], writes=[SBR[nb]])
                                            S_.op("act", lambda e: e.activation(out=fb[f2][:, 0:Nq], in_=SB[nb][:, 0:Nq], func=AF.Sqrt, bias=epsT[:, 0:1],
                                                                                scale=1.0 / 128),
                                                  reads=[SBR[nb], "epsT"], writes=[("fb", f2)])
                                            S_.op("dve", lambda e: e.reciprocal(out=fb[f3][:, 0:Nq], in_=fb[f2][:, 0:Nq]), reads=[("fb", f2)], writes=[("fb", f3)])
                                            S_.op("dve", lambda e: e.scalar_tensor_tensor(out=ybT[:, h, qc0:qc0 + Nq], in0=fb[f4][:, 0:Nq], scalar=subc[:, 0:1],
                                                                                          in1=fb[f3][:, 0:Nq], op0=ALU.mult, op1=ALU.mult),
                                                  reads=[("fb", f4), ("fb", f3), "subc"], writes=[("ybT", h)])
                                        pend2.append(comb2)
                                its.append((s1, s2, s3, post))
                    pipeline(its, PIPE_B, BDEPTH)
                    while pend2:
                        pend2.pop()(0)
            T.attn_b = attn_b

            def p2():
                ctr["ps"] = 4
                gw = {}
                gw[0] = wget(wb0 + 6)
                gw[2] = wget(wb0 + 7)
                wab = wget(wb0 + 8)
                for m in range(8):
                    half, mc = m // 4, (m % 4) * 128
                    if m == 4:
                        gw[1] = wget(wb0 + 9)
                        gw[3] = wget(wb0 + 10)
                        wab = wget(wb0 + 11)
                    wa = (wab[0], wab[1][:, 0:4, :])
                    wb = (wab[0], wab[1][:, 4:8, :])
                    banks = [nxt("ps", 6) for _ in range(4)]
                    specs = [(gw[half], 8, hT, [("hT", s2) for s2 in range(nsub)], mc),
                             (gw[2 + half], 8, hT, [("hT", s2) for s2 in range(nsub)], mc),
                             (wa, 4, yaT, [("yaT", s2) for s2 in range(nsub)], mc),
                             (wb, 4, ybT, [("ybT", h2) for h2 in range(4)], mc)]
                    for bnk, ((wi, W), nk_, act, rd, cc) in zip(banks, specs):
                        def mm(e, bnk=bnk, W=W, nk_=nk_, act=act, cc=cc):
                            ins = None
                            for kc in range(nk_):
                                ins = e.matmul(P[:, bnk, 0:NT], lhsT=W[:, kc, cc:cc + 128], rhs=act[:, kc, 0:NT],
                                               start=(kc == 0), stop=(kc == nk_ - 1))
                            return ins
                        S_.op("pe", mm, reads=rd + [("w", wi)], writes=[("ps", bnk)])
                    s0, s1, s2_, s3 = [nxt("fb", NFB) for _ in range(4)]
                    S_.op("act", lambda e, b=banks[0], s0=s0, m=m: e.activation(out=fb[s0][:, 0:NT], in_=P[:, b, 0:NT], func=AF.Sigmoid,
                                                                               bias=bgT[:, m:m + 1], scale=1.0),
                          reads=[("ps", banks[0]), "bgT"], writes=[("fb", s0)])
                    S_.op("act", lambda e, b=banks[1], s1=s1, m=m: e.activation(out=fb[s1][:, 0:NT], in_=P[:, b, 0:NT], func=AF.Sigmoid,
                                                                               bias=bgT[:, 8 + m:9 + m], scale=1.0),
                          reads=[("ps", banks[1]), "bgT"], writes=[("fb", s1)])
                    S_.op("dve", lambda e, b=banks[2], s0=s0, s2_=s2_: e.tensor_tensor(out=fb[s2_][:, 0:NT], in0=P[:, b, 0:NT], in1=fb[s0][:, 0:NT], op=ALU.mult),
                          reads=[("ps", banks[2]), ("fb", s0)], writes=[("fb", s2_)])
                    S_.op("dve", lambda e, b=banks[3], s1=s1, s3=s3: e.tensor_tensor(out=fb[s3][:, 0:NT], in0=P[:, b, 0:NT], in1=fb[s1][:, 0:NT], op=ALU.mult),
                          reads=[("ps", banks[3]), ("fb", s1)], writes=[("fb", s3)])
                    S_.op("pool", lambda e, s2_=s2_, s3=s3, m=m: e.tensor_tensor(out=mT[:, m, 0:NT], in0=fb[s2_][:, 0:NT], in1=fb[s3][:, 0:NT], op=ALU.add),
                          reads=[("fb", s2_), ("fb", s3)], writes=[("mT", m)])
                    if m == 3:
                        wdone(wb0 + 6); wdone(wb0 + 7); wdone(wb0 + 8)
                    if m == 7:
                        wdone(wb0 + 9); wdone(wb0 + 10); wdone(wb0 + 11)
                wo = [wget(wb0 + 12 + g) for g in range(2)]
                hb2 = {}

                def n2_A(s):
                    st = stat[0:np_, 8 + s:9 + s]
                    rd = [("x1", s, 0), ("x1", s, 1)]
                    rms_rows(X[0:np_, s, :], np_, D, st, rd, f"rs2_{s}")
                    hi = nxt("hb", 2)
                    hb2[s] = hi
                    S_.op("dve", lambda e, s=s, st=st, hi=hi: e.tensor_scalar(out=hbs[hi][0:np_, :], in0=X[0:np_, s, :], scalar1=st, scalar2=None, op0=ALU.mult),
                          reads=rd + [f"rs2_{s}"], writes=[("hb", hi)])

                def n2_B(s):
                    hi = hb2[s]
                    to_fm(hbs[hi][0:np_, :], np_, 8, g2T[:, :], lambda s=s: hT[:, :, col[s]:col[s] + np_], [("hb", hi)], [("hT", s)])

                for s in range(nsub):
                    xi = load_xs(s)
                    for g in range(2):
                        wi, W = wo[g]
                        b = nxt("ps", 6)

                        def mm(e, s=s, b=b, W=W):
                            ins = None
                            for kc in range(8):
                                ins = e.matmul(P[0:np_, b, :], lhsT=mT[:, kc, col[s]:col[s] + np_], rhs=W[:, kc, :], start=(kc == 0), stop=(kc == 7))
                            return ins
                        S_.op("pe", mm, reads=[("mT", m2) for m2 in range(8)] + [("w", wi)], writes=[("ps", b)])
                        S_.op("dve", lambda e, s=s, b=b, g=g, xi=xi: e.tensor_tensor(out=X[0:np_, s, g * 512:(g + 1) * 512], in0=P[0:np_, b, :],
                                                                                   in1=xstage[xi][0:np_, g * 512:(g + 1) * 512], op=ALU.add),
                              reads=[("ps", b), ("xs", xi)], writes=[("x1", s, g)])
                    n2_A(s)
                    if s >= 1:
                        n2_B(s - 1)
                wdone(wb0 + 12); wdone(wb0 + 13)
                n2_B(nsub - 1)
            T.p2 = p2

            def ff1(pre=()):
                for fn in pre:
                    fn()
                for i in range(8):
                    wi, W = wget(wb0 + 14 + i)
                    for mc in range(4):
                        j = i * 4 + mc
                        b = nxt("ps", 6)

                        def mm(e, b=b, W=W, mc=mc):
                            ins = None
                            for kc in range(8):
                                ins = e.matmul(P[:, b, 0:NT], lhsT=W[:, kc, mc * 128:(mc + 1) * 128], rhs=hT[:, kc, 0:NT], start=(kc == 0), stop=(kc == 7))
                            return ins
                        S_.op("pe", mm, reads=[("hT", s2) for s2 in range(nsub)] + [("w", wi)], writes=[("ps", b)])
                        ri = nxt("fb", NFB)
                        S_.op("act", lambda e, b=b, ri=ri: e.activation(out=fb[ri][:, 0:NT], in_=P[:, b, 0:NT], func=AF.Relu),
                              reads=[("ps", b)], writes=[("fb", ri)])
                        eng = "pool" if (j % 2 == 0) else "dve"
                        S_.op(eng, lambda e, ri=ri, j=j: e.tensor_tensor(out=uT[:, j, 0:NT], in0=fb[ri][:, 0:NT], in1=fb[ri][:, 0:NT], op=ALU.mult),
                              reads=[("fb", ri)], writes=[("uT", j)])
                    wdone(wb0 + 14 + i)
            T.ff1 = ff1

            def ff2(hooks=None):
                hooks = hooks or {}
                for g in range(2):
                    banks = [nxt("ps", 6) for _ in range(nsub)]
                    for i in range(4):
                        wi, W = wget(wb0 + 22 + g * 4 + i)
                        for s in range(nsub):
                            def mm(e, s=s, W=W, i=i, b=banks[s]):
                                ins = None
                                for kc in range(8):
                                    ins = e.matmul(P[0:np_, b, :], lhsT=uT[:, i * 8 + kc, col[s]:col[s] + np_], rhs=W[:, kc, :],
                                                   start=(i == 0 and kc == 0), stop=(i == 3 and kc == 7))
                                return ins
                            S_.op("pe", mm, reads=[("uT", i * 8 + kc) for kc in range(8)] + [("w", wi)], writes=[("ps", banks[s])])
                        wdone(wb0 + 22 + g * 4 + i)
                        for fn in hooks.get(g * 4 + i, ()):
                            fn()
                    for s in range(nsub):
                        S_.op("dve", lambda e, s=s, b=banks[s], g=g: e.tensor_tensor(out=X[0:np_, s, g * 512:(g + 1) * 512], in0=P[0:np_, b, :],
                                                                                    in1=X[0:np_, s, g * 512:(g + 1) * 512], op=ALU.add),
                              reads=[("ps", banks[s]), ("x1", s, g)], writes=[("x1", s, g)])
                yrd = [("x1", s, g) for s in range(nsub) for g in range(2)]
                if kind == "p":
                    S_.dma("sp", lambda e: [e.dma_start(out=yp[seq, t0:t0 + 512, :].rearrange("(s p) d -> p s d", p=128), in_=X[:, :, :])],
                           1, "yo", reads=yrd)
                else:
                    S_.dma("sp", lambda e: [e.dma_start(out=ys.rearrange("s p d -> p s d"), in_=X[0:16, 0:2, :])],
                           1, "yo", reads=yrd)
            T.ff2 = ff2
            return T

        tiles = []
        for q in range(NSEQ):
            for t0 in range(0, S, 512):
                tiles.append(("p", q, t0, [128] * 4))
        if with_sample:
            tiles.append(("s", 0, 0, [16, 16]))
        TL = [make_tile(kind, q, t0, subs, ti) for ti, (kind, q, t0, subs) in enumerate(tiles)]
        wstate["total"] = len(tiles) * NPIECE
        TL[0].p1a()
        for i, T in enumerate(TL):
            T.p1b()
            T.attn_a()
            T.attn_b()
            T.p2()
            if i + 1 < len(TL):
                N = TL[i + 1]
                ns = N.nsub
                T.ff1(pre=[lambda N=N: N.p1a_A(0), lambda N=N: N.p1a_A(1)])
                hooks = {}
                for s in range(ns):
                    hooks.setdefault(s, []).append(lambda N=N, s=s: N.p1a_B(s))
                    if s + 2 < ns:
                        hooks[s].append(lambda N=N, s=s: N.p1a_A(s + 2))
                T.ff2(hooks)
            else:
                T.ff1()
                T.ff2()

        if DEBUG:
            for nm, t, shp, dt in [("yaT", yaT, (128, 4, 512), BF16), ("ybT", ybT, (128, 4, 512), BF16), ("mT", qT, (128, 8, 512), BF16),
                                   ("hT", hT, (128, 8, 512), BF16), ("uT", uT, (128, 32, 512), BF16), ("X", xt[0], (128, 4, D), F32),
                                   ("kbT", kbT, (128, 4, SK), BF16), ("vb", vb, (128, NKT, 512), BF16), ("kaT", kaT, (128, 4, 1024), BF16),
                                   ("va", va, (128, 8, 512), BF16), ("lamt", lamt, (128, 8), F32), ("biasA", biasA, (128, 8, 5, 128), BF16),
                                   ("ya", ya, (128, 512), BF16), ("pA0", pA[0], (128, 5, 128), BF16), ("pA1", pA[1], (128, 5, 128), BF16), ("rsA", rsA, (128, 8), F32)]:
                dd = nc.dram_tensor("dbg_" + nm, list(shp), dt, kind="ExternalOutput").ap()
                allres = list(S_.lastw.keys())
                S_.fence("sp", allres)
                S_.dma("sp", lambda e, dd=dd, t=t: [e.dma_start(out=dd, in_=t[:] if not isinstance(t, bass.AP) else t)], 1, "dbg_" + nm)

        dkeys = [k for k in S_.cnt if k not in Sched.ENG]
        S_.final_wait("sp", dkeys)

        sems = {}
        for k in S_.cnt:
            sems[k] = es.enter_context(nc.semaphore("s_" + str(k).replace(" ", "").replace("(", "").replace(")", "").replace(",", "_").replace("'", "")))
        for k, v in S_.cnt.items():
            assert v < 60000, (k, v)
        engmap = {"pe": "tensor", "act": "scalar", "dve": "vector", "pool": "gpsimd", "sp": "sync"}
        with nc.Block() as block:
            def emit(name):
                def body(e):
                    for it in S_.lists[name]:
                        if it[0] == "wait":
                            e.wait_ge(sems[it[1]], it[2])
                        elif it[0] == "op":
                            ins = it[1](e)
                            ins.then_inc(sems[it[2]], 1)
                        else:
                            for ins in it[1](e):
                                ins.then_inc(sems[it[2]], 16)
                return body
            for name in Sched.ENG:
                getattr(block, engmap[name])(emit(name))
    return nc


def _rope_tables(S):
    half = 32
    inv = (np.float32(10000.0) ** (-np.arange(half, dtype=np.float32) / np.float32(half))).astype(np.float32)
    nsub = S // 128
    pos = np.zeros((128, nsub + 1), np.float32)
    for k in range(nsub):
        pos[:, k] = k * 128 + np.arange(128)
    pos[:, nsub] = PAST + np.arange(128)
    ang = (pos[:, :, None] * inv[None, None, :]).astype(np.float32)
    return np.cos(ang).astype(np.float32), np.sin(ang).astype(np.float32)


def _bias_tables(rel):
    kk = np.arange(640)[:, None]
    qq = np.arange(128)[None, :]
    dist = (qq + 512) - kk
    idx = np.clip(dist, -63, 256) + 63
    kc = kk // 64 - 8
    qc = qq // 64
    valid = (kc <= qc) & (kc >= qc - 8)
    bp = rel[:, idx]
    bp = np.where(valid[None], bp, np.float32(NEG)).astype(np.float32)
    biasA = np.ascontiguousarray(bp.reshape(8, 5, 128, 128).transpose(2, 0, 1, 3))
    kpos = np.concatenate([PAST - 512 + np.arange(512), PAST + np.arange(16), np.zeros(112)]).astype(np.int64)
    qpos = PAST + np.arange(16)
    d2 = qpos[None, :] - kpos[:, None]
    i2 = np.clip(d2, -63, 256) + 63
    bs = rel[:, i2].astype(np.float32)
    biasS = np.ascontiguousarray(bs.reshape(8, 5, 128, 16).transpose(2, 0, 1, 3))
    return biasA, biasS


def _rep(v, n):
    return np.ascontiguousarray(np.broadcast_to(np.tile(np.asarray(v, np.float32), n)[None, :], (128, 512)))


_CACHE = {}


def kernel(x_prompt, x_sample, cache_a_k, cache_a_v, cache_b_k, cache_b_v,
           ln1_g, w_in, qn_a, kn_a, rel_bias, qn_b, kn_b,
           lam_q1, lam_k1, lam_q2, lam_k2, subln_g,
           w_gate, b_gate, w_proj_a, w_proj_b, w_out, ln2_g, w_ff1, w_ff2, _ncores=NCORES):
    f = lambda a: np.ascontiguousarray(np.asarray(a, dtype=np.float32))
    x_prompt = f(x_prompt); x_sample = f(x_sample)
    B, S, _ = x_prompt.shape
    n = _ncores
    NSEQ = B // n
    key = (NSEQ, S)
    if key not in _CACHE:
        _CACHE[key] = build_program(NSEQ, S, True)
    nc = _CACHE[key]
    cosT, sinT = _rope_tables(S)
    biasA, biasS = _bias_tables(f(rel_bias)[0])
    normv = np.ascontiguousarray(np.broadcast_to(np.stack([f(qn_a)[0], f(kn_a)[0], f(qn_b)[0], f(kn_b)[0]])[None], (128, 4, 64)))
    lamv = np.ascontiguousarray(np.broadcast_to(
        np.stack([f(lam_q1)[0], f(lam_k1)[0], f(lam_q2)[0], f(lam_k2)[0]])[None], (128, 4, 64)))
    shared = {
        "w_in": f(w_in)[0], "w_gate": f(w_gate)[0], "w_pa": f(w_proj_a)[0], "w_pb": f(w_proj_b)[0],
        "w_out": f(w_out)[0], "w_ff1": f(w_ff1)[0], "w_ff2": f(w_ff2)[0],
        "g1T": np.ascontiguousarray(f(ln1_g)[0].reshape(8, 128).T), "g2T": np.ascontiguousarray(f(ln2_g)[0].reshape(8, 128).T),
        "bgT": np.ascontiguousarray(f(b_gate)[0].reshape(16, 128).T), "subT": np.ascontiguousarray(f(subln_g)[0].reshape(128, 1)),
        "normv": np.ascontiguousarray(normv), "lamv": lamv, "ropec": cosT, "ropes": sinT,
        "biasA": biasA, "biasS": biasS, "ident": np.eye(128, dtype=np.float32),
    }
    cak = f(cache_a_k)[0].reshape(-1, 512, 512); cav = f(cache_a_v)[0].reshape(-1, 512, 512)
    cbk = f(cache_b_k)[0].reshape(-1, 1024, 512); cbv = f(cache_b_v)[0].reshape(-1, 1024, 512)
    in_maps = []
    for c in range(n):
        m = dict(shared)
        m["xp"] = x_prompt[c * NSEQ:(c + 1) * NSEQ]
        m["xs"] = x_sample[2 * c:2 * c + 2]
        m["cak"] = cak[2 * c:2 * c + 2]; m["cav"] = cav[2 * c:2 * c + 2]
        m["cbk"] = cbk[2 * c:2 * c + 2]; m["cbv"] = cbv[2 * c:2 * c + 2]
        in_maps.append(m)
    res = run_bass_kernel_spmd(nc, in_maps, core_ids=list(range(n)))
    R = res.results
    if DEBUG:
        global LAST_RESULTS
        LAST_RESULTS = R
    cat = lambda k: np.concatenate([r[k] for r in R], axis=0)
    y_p = cat("yp"); y_s = cat("ys")
    akp = cat("akp").reshape(1, B, 512, 8, 64); avp = cat("avp").reshape(1, B, 512, 8, 64)
    bkp = cat("bkp").reshape(1, B, S, 4, 2, 64); bvp = cat("bvp").reshape(1, B, S, 4, 128)
    nb = 2 * n
    aks = cat("aks").reshape(1, nb, 16, 8, 64); avs = cat("avs").reshape(1, nb, 16, 8, 64)
    bks = cat("bks").reshape(1, nb, 16, 4, 2, 64); bvs = cat("bvs").reshape(1, nb, 16, 4, 128)
    return (y_p, y_s, akp, avp, bkp, bvp, aks, avs, bks, bvs)
```

```python
import math
import numpy as np
import concourse.bass as bass
import concourse.mybir as mybir
from concourse.bass_utils import run_bass_kernel_spmd

F32 = mybir.dt.float32
BF16 = mybir.dt.bfloat16
AF = mybir.ActivationFunctionType
ALU = mybir.AluOpType
AX = mybir.AxisListType

D = 1024
NCORES = 8
EPS = 1e-6
PAST = 1024
LAM_INIT = 0.8 - 0.6 * math.exp(0.0)
NEG = -30000.0
NSLOT = 4
DEBUG = False
DEBUG_STOP = None
PIPE = True
PIPE_A = True
PIPE_B = True


class Sched:
    ENG = ("pe", "act", "dve", "pool", "sp")

    def __init__(self):
        self.lists = {e: [] for e in self.ENG}
        self.cnt = {}
        self.known = {e: {} for e in self.ENG}
        self.lastw = {}
        self.readers = {}

    def _need(self, eng, tok):
        k, v = tok
        if self.known[eng].get(k, 0) < v:
            self.known[eng][k] = v
            self.lists[eng].append(("wait", k, v))

    def _sync(self, eng, reads, writes):
        for r in reads:
            t = self.lastw.get(r)
            if t is not None:
                self._need(eng, t)
        for w in writes:
            t = self.lastw.get(w)
            if t is not None and (t[0] != eng or eng != "pe"):
                self._need(eng, t)
            for t in self.readers.get(w, ()):
                if t[0] != eng or eng != "pe":
                    self._need(eng, t)

    def _record(self, tok, reads, writes):
        for w in writes:
            self.lastw[w] = tok
            self.readers[w] = []
        for r in reads:
            self.readers.setdefault(r, []).append(tok)

    def op(self, eng, fn, reads=(), writes=()):
        self._sync(eng, reads, writes)
        self.cnt[eng] = self.cnt.get(eng, 0) + 1
        tok = (eng, self.cnt[eng])
        self.lists[eng].append(("op", fn, eng, 1))
        self._record(tok, reads, writes)
        return tok

    def dma(self, eng, fn, n, key, reads=(), writes=()):
        self._sync(eng, reads, writes)
        self.cnt[key] = self.cnt.get(key, 0) + 16 * n
        tok = (key, self.cnt[key])
        self.lists[eng].append(("dma", fn, key, 16))
        self._record(tok, reads, writes)
        return tok

    def fence(self, eng, resources):
        for r in resources:
            t = self.lastw.get(r)
            if t is not None:
                self._need(eng, t)
            for t in self.readers.get(r, ()):
                self._need(eng, t)

    def final_wait(self, eng, keys):
        for k in keys:
            if self.cnt.get(k, 0) > 0:
                self._need(eng, (k, self.cnt[k]))


def build_program(NSEQ, S, with_sample=True):
    assert S % 512 == 0
    nc = bass.Bass("TRN2", target_bir_lowering=False)
    NSUB = S // 128
    NROPE = NSUB + 1

    def din(name, shape):
        return nc.dram_tensor(name, list(shape), F32, kind="ExternalInput").ap()

    def dout(name, shape):
        return nc.dram_tensor(name, list(shape), F32, kind="ExternalOutput").ap()

    xp = din("xp", (NSEQ, S, D))
    xs = din("xs", (2, 16, D))
    cak = din("cak", (2, 512, 512)); cav = din("cav", (2, 512, 512))
    cbk = din("cbk", (2, 1024, 512)); cbv = din("cbv", (2, 1024, 512))
    w_in = din("w_in", (D, 3072)); w_gate = din("w_gate", (D, 2048))
    w_pa = din("w_pa", (512, D)); w_pb = din("w_pb", (512, D)); w_out = din("w_out", (D, D))
    w_ff1 = din("w_ff1", (D, 4096)); w_ff2 = din("w_ff2", (4096, D))
    d_g1T = din("g1T", (128, 8)); d_g2T = din("g2T", (128, 8)); d_bgT = din("bgT", (128, 16))
    d_subT = din("subT", (128, 1))
    d_normv = din("normv", (128, 4, 64))
    d_lamv = din("lamv", (128, 4, 64))
    d_cos = din("ropec", (128, NROPE, 32)); d_sin = din("ropes", (128, NROPE, 32))
    d_biasA = din("biasA", (128, 8, 5, 128)); d_biasS = din("biasS", (128, 8, 5, 16))
    d_ident = din("ident", (128, 128))

    yp = dout("yp", (NSEQ, S, D)); ys = dout("ys", (2, 16, D))
    akp = dout("akp", (NSEQ, 512, 512)); avp = dout("avp", (NSEQ, 512, 512))
    bkp = dout("bkp", (NSEQ, S, 512)); bvp = dout("bvp", (NSEQ, S, 512))
    aks = dout("aks", (2, 16, 512)); avs = dout("avs", (2, 16, 512))
    bks = dout("bks", (2, 16, 512)); bvs = dout("bvs", (2, 16, 512))

    S_ = Sched()
    SK = max(S, 1152)
    NKT = SK // 128

    import contextlib
    with contextlib.ExitStack() as es:
        def sb(name, shape, dt):
            return es.enter_context(nc.sbuf_tensor("sb_" + name, list(shape), dt))

        g1T = sb("g1T", (128, 8), F32); g2T = sb("g2T", (128, 8), F32); bgT = sb("bgT", (128, 16), F32)
        subT = sb("subT", (128, 1), F32); subc = sb("subc", (128, 1), F32)
        normv = sb("normv", (128, 4, 64), F32)
        lamv = sb("lamv", (128, 4, 64), F32); lamt = sb("lamt", (128, 8), F32); lamj = sb("lamj", (128, 2, 64), F32)
        cosT = sb("cosT", (128, NROPE, 32), F32); sinT = sb("sinT", (128, NROPE, 32), F32)
        biasA = sb("biasA", (128, 8, 5, 128), BF16); biasS = sb("biasS", (128, 8, 5, 16), BF16)
        identf = sb("identf", (128, 128), F32); ident = sb("ident", (128, 128), BF16)
        ones_b = sb("ones_b", (128, 128), BF16); ones_f = sb("ones_f", (128, 128), F32)
        epsT = sb("epsT", (128, 1), F32)
        kaT = sb("kaT", (128, 4, 1024), BF16)
        va = sb("va", (128, 8, 512), BF16)
        kbT = sb("kbT", (128, 4, SK), BF16)
        vb = sb("vb", (128, NKT, 512), BF16)
        wring = [sb(f"w{i}", (128, 8, 512), BF16) for i in range(NSLOT)]
        xt = [sb(f"xt{i}", (128, 4, D), F32) for i in range(1)]
        hbs = [sb(f"hb{i}", (128, D), BF16) for i in range(2)]
        hT = sb("hT", (128, 8, 512), BF16)
        qT = sb("qT", (128, 8, 512), BF16)
        mT = qT
        yaT = sb("yaT", (128, 4, 512), BF16); ybT = sb("ybT", (128, 4, 512), BF16)
        qaT = qT[:, 0:4, :]; qbT = qT[:, 4:8, :]
        stat = sb("stat", (128, 64), F32)
        NFB = 10
        fb = [sb(f"fb{i}", (128, 512), F32) for i in range(NFB)]
        junk = sb("junk", (128, 1024), BF16)
        xstage = [sb(f"xstage{i}", (128, D), F32) for i in range(2)]
        hstat = [sb(f"hstat{i}", (128, 24), F32) for i in range(4)]
        rsA = sb("rsA", (128, 8), F32)
        uT = sb("uT", (128, 32, 512), BF16)
        cst = uT[:, 0:8, :]
        ya = uT[:, 21, :]
        pA = [uT[:, 22 + 2 * i:24 + 2 * i, :].rearrange("p a b -> p (a b)")[:, 0:640].rearrange("p (j q) -> p j q", j=5) for i in range(2)]
        pB = [uT[:, 18 + i, :] for i in range(3)]
        qaz = [qT[:, 0:4, :], qT[:, 4:8, :]]
        qbz = [uT[:, 8:12, :], uT[:, 26:30, :]]
        NZB = 6
        zb = [uT[:, 12 + i, :] for i in range(NZB)]

        print("SBUF bytes remaining per partition:", nc.sbuf_bytes_remaining)
        P = es.enter_context(nc.psum_tensor("P", [128, 6, 512], F32))
        PT = es.enter_context(nc.psum_tensor("PT", [128, 2, 1024], BF16))

        PTf = [PT[:, 0, :].bitcast(F32), PT[:, 1, :].bitcast(F32)]
        SB = [P[:, 4, :], P[:, 5, :], PTf[0], PTf[1]]
        SBR = [("ps", 4), ("ps", 5), ("pt", 0), ("pt", 1)]
        BDEPTH = 3
        ctr = {"xs": 0, "fb": 0, "zb": 0, "hstat": 0, "pt": 0, "ps": 0, "w": 0, "pB": 0, "hb": 0}

        def nxt(k, n):
            v = ctr[k] % n
            ctr[k] += 1
            return v

        def load_const(dst, src, name):
            S_.dma("sp", lambda e: [e.dma_start(out=dst, in_=src)], 1, "c_" + name, writes=[name])

        load_const(g1T[:], d_g1T[:, :], "g1T"); load_const(g2T[:], d_g2T[:, :], "g2T")
        load_const(bgT[:], d_bgT[:, :], "bgT"); load_const(subT[:], d_subT[:, :], "subT")
        load_const(normv[:], d_normv[:, :, :], "normv"); load_const(lamv[:], d_lamv[:, :, :], "lamv")
        load_const(cosT[:], d_cos[:, :, :], "cosT"); load_const(sinT[:], d_sin[:, :, :], "sinT")
        S_.dma("pool", lambda e: [e.dma_start(out=biasA[:], in_=d_biasA[:, :, :, :])], 1, "c_biasA", writes=["biasA"])
        S_.dma("pool", lambda e: [e.dma_start(out=biasS[:], in_=d_biasS[:, :, :, :])], 1, "c_biasS", writes=["biasS"])
        load_const(identf[:], d_ident[:, :], "identf")
        S_.op("dve", lambda e: e.tensor_copy(out=ident[:], in_=identf[:]), reads=["identf"], writes=["ident"])
        S_.op("dve", lambda e: e.tensor_scalar(out=biasA[:], in0=biasA[:], scalar1=8.0, scalar2=None, op0=ALU.mult), reads=["biasA"], writes=["biasA"])
        S_.op("dve", lambda e: e.tensor_scalar(out=biasS[:], in0=biasS[:], scalar1=8.0, scalar2=None, op0=ALU.mult), reads=["biasS"], writes=["biasS"])
        S_.op("pool", lambda e: e.memset(ones_b[:], 1.0), writes=["ones_b"])
        S_.op("pool", lambda e: e.memset(ones_f[:], 1.0), writes=["ones_f"])
        S_.op("pool", lambda e: e.memset(epsT[:], EPS), writes=["epsT"])
        S_.op("dve", lambda e: e.tensor_tensor(out=lamj[:, 0, :], in0=lamv[:, 0, :], in1=lamv[:, 1, :], op=ALU.mult),
              reads=["lamv"], writes=["lamj0"])
        S_.op("dve", lambda e: e.tensor_tensor(out=lamj[:, 1, :], in0=lamv[:, 2, :], in1=lamv[:, 3, :], op=ALU.mult),
              reads=["lamv"], writes=["lamj1"])
        S_.op("dve", lambda e: e.reduce_sum(out=lamt[:, 0:2], in_=lamj[:, :, :], axis=AX.X),
              reads=["lamj0", "lamj1"], writes=["lamt01"])
        S_.op("act", lambda e: e.activation(out=lamt[:, 2:4], in_=lamt[:, 0:2], func=AF.Exp),
              reads=["lamt01"], writes=["lamt23"])
        S_.op("dve", lambda e: e.tensor_tensor(out=lamt[:, 5:6], in0=lamt[:, 3:4], in1=lamt[:, 2:3], op=ALU.subtract),
              reads=["lamt23"], writes=["lamt5"])
        S_.op("dve", lambda e: e.tensor_scalar(out=lamt[:, 4:5], in0=lamt[:, 5:6], scalar1=-LAM_INIT, scalar2=None, op0=ALU.add),
              reads=["lamt5"], writes=["neglam"])
        S_.op("dve", lambda e: e.tensor_scalar(out=subc[:], in0=subT[:], scalar1=1.0 - LAM_INIT, scalar2=None, op0=ALU.mult),
              reads=["subT"], writes=["subc"])
        neglam = lamt[:, 4:5]

        NPIECE = 30
        wscr = nc.dram_tensor("wscr", [NPIECE, 128, 4096], BF16, kind="Internal").ap()
        pieces = []
        for g in range(6):
            pieces.append([(w_in[:, g * 512:(g + 1) * 512], 8, 0)])
        for half in range(2):
            pieces.append([(w_gate[:, half * 512:(half + 1) * 512], 8, 0)])
            pieces.append([(w_gate[:, 1024 + half * 512:1024 + (half + 1) * 512], 8, 0)])
            pieces.append([(w_pa[:, half * 512:(half + 1) * 512], 4, 0), (w_pb[:, half * 512:(half + 1) * 512], 4, 4)])
        for g in range(2):
            pieces.append([(w_out[:, g * 512:(g + 1) * 512], 8, 0)])
        for i in range(8):
            pieces.append([(w_ff1[:, i * 512:(i + 1) * 512], 8, 0)])
        for g in range(2):
            for i in range(4):
                pieces.append([(w_ff2[i * 1024:(i + 1) * 1024, g * 512:(g + 1) * 512], 8, 0)])
        assert len(pieces) == NPIECE
        for k, parts in enumerate(pieces):
            def cv(e, k=k, parts=parts):
                out = []
                for (src, kc, k0) in parts:
                    dst = wscr[k].rearrange("p (k n) -> p k n", k=8)[:, k0:k0 + kc, :]
                    out.append(e.dma_start(out=dst, in_=src.rearrange("(k p) n -> p k n", p=128)))
                return out
            S_.dma("pool", cv, len(parts), f"cv{k}", writes=[("wscr", k)])

        wstate = {"emitted": 0, "total": 0, "released": -1, "cur": -1}
        PF = 2

        def wpump():
            while (wstate["emitted"] <= min(wstate["cur"] + PF, wstate["total"] - 1)
                   and wstate["emitted"] - NSLOT <= wstate["released"]):
                m = wstate["emitted"]
                k = m % NPIECE
                i = m % NSLOT
                S_.dma("sp", lambda e, i=i, k=k: [e.dma_start(out=wring[i][:].rearrange("p a b -> p (a b)"), in_=wscr[k])],
                       1, f"w{i}", reads=[("wscr", k)], writes=[("w", i)])
                wstate["emitted"] += 1

        def wget(n):
            wstate["cur"] = max(wstate["cur"], n)
            wpump()
            assert wstate["emitted"] > n, (n, wstate)
            i = n % NSLOT
            return i, wring[i]

        def wdone(n):
            wstate["released"] = max(wstate["released"], n)
            wpump()

        def rms_rows(xin_ap, np_, width, stat_ap, rd, wr_name):
            S_.op("act", lambda e: e.activation(out=junk[0:np_, 0:width], in_=xin_ap, func=AF.Square,
                                                scale=1.0 / math.sqrt(width), accum_out=stat_ap),
                  reads=rd, writes=[wr_name + "_ms", "junk"])
            S_.op("act", lambda e: e.activation(out=stat_ap, in_=stat_ap, func=AF.Sqrt, bias=epsT[0:np_, 0:1], scale=1.0),
                  reads=[wr_name + "_ms", "epsT"], writes=[wr_name + "_sd"])
            S_.op("dve", lambda e: e.reciprocal(out=stat_ap, in_=stat_ap), reads=[wr_name + "_sd"], writes=[wr_name])

        def to_fm(src_tm, np_, nch, gT, dst_fn, rd, wr, evac="act", split=None):
            b = nxt("pt", 2)

            def tr(e):
                ins = None
                for c in range(nch):
                    ins = e.transpose(out=PT[:, b, c * 128:c * 128 + np_], in_=src_tm[:, c * 128:(c + 1) * 128],
                                      identity=ident[0:np_, 0:np_])
                return ins
            S_.op("pe", tr, reads=rd + ["ident"], writes=[("pt", b)])
            src = PT[:, b, 0:nch * 128].rearrange("p (c t) -> p c t", c=nch)[:, :, 0:np_]
            if split is not None:
                dA, dB = split
                S_.op("act", lambda e: e.activation(out=dA()[0:64], in_=src[0:64], func=AF.Copy), reads=[("pt", b)], writes=[wr[0]])
                S_.op("dve", lambda e: e.tensor_copy(out=dB()[64:128], in_=src[64:128]), reads=[("pt", b)], writes=[(wr[0][0] + "_hi", wr[0][1])])
            elif gT is None and evac == "dve":
                S_.op("dve", lambda e: e.tensor_copy(out=dst_fn(), in_=src), reads=[("pt", b)], writes=wr)
            elif gT is None:
                S_.op("act", lambda e: e.activation(out=dst_fn(), in_=src, func=AF.Copy), reads=[("pt", b)], writes=wr)
            else:
                S_.op("dve", lambda e: e.tensor_tensor(out=dst_fn(), in0=src,
                                                       in1=gT.unsqueeze(2).broadcast_to([128, nch, np_]), op=ALU.mult),
                      reads=[("pt", b), "g1T", "g2T"], writes=wr)

        def pipeline(items, PIPE=True, depth=1):
            n = len(items)
            if not PIPE:
                depth = 0
            for i in range(min(depth, n)):
                items[i][0]()
            for i in range(n):
                if i + depth < n:
                    items[i + depth][0]()
                items[i][1]()
                items[i][2]()
                for fn in items[i][3]:
                    fn()

        class Tile:
            pass

        def make_tile(kind, seq, t0, subs, tidx):
            T = Tile()
            wb0 = tidx * NPIECE
            nsub = len(subs)
            np_ = subs[0]
            NT = nsub * np_
            X = xt[0]
            col = [i * np_ for i in range(nsub)]

            def load_xs(s):
                i = nxt("xs", 2)
                if kind == "p":
                    src = xp[seq, t0 + s * 128:t0 + (s + 1) * 128, :]
                else:
                    src = xs[s, :, :]
                S_.dma("sp", lambda e: [e.dma_start(out=xstage[i][0:np_, :], in_=src)], 1, f"xs{i}", writes=[("xs", i)])
                return i

            hb_of = {}

            def p1a_A(s):
                xi = load_xs(s)
                st = stat[0:np_, s:s + 1]
                rms_rows(xstage[xi][0:np_, :], np_, D, st, [("xs", xi)], f"rs1_{s}")
                hi = nxt("hb", 2)
                hb_of[s] = hi
                S_.op("dve", lambda e, st=st, hi=hi, xi=xi: e.tensor_scalar(out=hbs[hi][0:np_, :], in0=xstage[xi][0:np_, :], scalar1=st,
                                                                          scalar2=None, op0=ALU.mult),
                      reads=[("xs", xi), f"rs1_{s}"], writes=[("hb", hi)])

            def p1a_B(s):
                hi = hb_of[s]
                to_fm(hbs[hi][0:np_, :], np_, 8, g1T[:, :], lambda s=s: hT[:, :, col[s]:col[s] + np_],
                      [("hb", hi)], [("hT", s)])

            def p1a():
                for s in range(nsub):
                    p1a_A(s)
                    p1a_B(s)
            T.p1a = p1a
            T.p1a_A = p1a_A
            T.p1a_B = p1a_B
            T.nsub = nsub

            def make_item(g, s, grp):
                I = Tile()
                stt = {}
                if kind == "p":
                    ktg = t0 // 128 + s
                    slotA = ktg % 8
                    kcolB = ktg * 128
                    ktB = ktg
                    pos_idx = ktg
                else:
                    slotA = 4; kcolB = 1024; ktB = 8; pos_idx = NSUB
                isnorm = g in (0, 1, 3, 4)
                nvec = {0: 0, 1: 1, 3: 2, 4: 3}.get(g)
                rope_idx = pos_idx if g >= 3 else None

                def A():
                    if s == 0:
                        grp["w"] = wget(wb0 + g)
                    wi, W = grp["w"]
                    b = nxt("ps", 6)

                    def mm(e):
                        ins = None
                        for kc in range(8):
                            ins = e.matmul(P[0:np_, b, :], lhsT=hT[:, kc, col[s]:col[s] + np_], rhs=W[:, kc, :],
                                           start=(kc == 0), stop=(kc == 7))
                        return ins
                    S_.op("pe", mm, reads=[("hT", s), ("w", wi)], writes=[("ps", b)])
                    if s == nsub - 1:
                        wdone(wb0 + g)
                    z = nxt("fb", NFB)
                    stt["z"] = z
                    S_.op("act", lambda e: e.activation(out=fb[z][0:np_, :], in_=P[0:np_, b, :], func=AF.Copy),
                          reads=[("ps", b)], writes=[("fb", z)])
                    if isnorm:
                        j = nxt("fb", NFB)
                        hs = nxt("hstat", 4)
                        stt["hs"] = hs
                        S_.op("act", lambda e: e.activation(out=fb[j][0:np_, :], in_=P[0:np_, b, :], func=AF.Square),
                              reads=[("ps", b)], writes=[("fb", j)])
                        S_.op("dve", lambda e: e.reduce_sum(out=hstat[hs][0:np_, 0:8],
                                                            in_=fb[j][0:np_, :].rearrange("p (h d) -> p h d", h=8), axis=AX.X),
                              reads=[("fb", j)], writes=[("hstat", hs)])
                I.A = A

                def B():
                    if not isnorm:
                        return
                    z, hs = stt["z"], stt["hs"]
                    Z = fb[z]
                    S_.op("act", lambda e: e.activation(out=hstat[hs][0:np_, 8:16], in_=hstat[hs][0:np_, 0:8], func=AF.Sqrt,
                                                        bias=epsT[0:np_, 0:1], scale=1.0 / 64),
                          reads=[("hstat", hs), "epsT"], writes=[("hstat_b", hs)])
                    S_.op("dve", lambda e: e.reciprocal(out=hstat[hs][0:np_, 16:24], in_=hstat[hs][0:np_, 8:16]),
                          reads=[("hstat_b", hs)], writes=[("hstat_c", hs)])
                    S_.op("dve", lambda e: e.tensor_tensor(out=Z[0:np_, :].rearrange("p (h d) -> p h d", h=8),
                                                           in0=Z[0:np_, :].rearrange("p (h d) -> p h d", h=8),
                                                           in1=hstat[hs][0:np_, 16:24].unsqueeze(2).broadcast_to([np_, 8, 64]),
                                                           op=ALU.mult),
                          reads=[("hstat_c", hs), ("fb", z)], writes=[("fb", z)])
                    S_.op("pool", lambda e: e.tensor_tensor(out=Z[0:np_, :].rearrange("p (h d) -> p h d", h=8),
                                                            in0=Z[0:np_, :].rearrange("p (h d) -> p h d", h=8),
                                                            in1=normv[0:np_, nvec, :].unsqueeze(1).broadcast_to([np_, 8, 64]), op=ALU.mult),
                          reads=[("fb", z), "normv"], writes=[("fb", z)])
                    if rope_idx is None:
                        stt["Rf"], stt["rres"] = Z, ("fb", z)
                        return
                    r = nxt("fb", NFB)
                    R = fb[r]
                    Zv = Z[0:np_, :].rearrange("p (h t d) -> p h t d", h=8, t=2)
                    Rv = R[0:np_, :].rearrange("p (h t d) -> p h t d", h=8, t=2)
                    sn = sinT[0:np_, rope_idx, :].unsqueeze(1).broadcast_to([np_, 8, 32])
                    j2 = nxt("fb", NFB)
                    Jv = fb[j2][0:np_, :].rearrange("p (h t d) -> p h t d", h=8, t=2)
                    S_.op("dve", lambda e: e.tensor_tensor(out=R[0:np_, :].rearrange("p (g d) -> p g d", g=16),
                                                           in0=Z[0:np_, :].rearrange("p (g d) -> p g d", g=16),
                                                           in1=cosT[0:np_, rope_idx, :].unsqueeze(1).broadcast_to([np_, 16, 32]), op=ALU.mult),
                          reads=[("fb", z), "cosT"], writes=[("fb", r)])
                    S_.op("pool", lambda e: e.tensor_tensor(out=Jv[:, :, 0, :], in0=Zv[:, :, 1, :], in1=sn, op=ALU.mult),
                          reads=[("fb", z), "sinT"], writes=[("fb", j2)])
                    S_.op("pool", lambda e: e.tensor_tensor(out=Jv[:, :, 1, :], in0=Zv[:, :, 0, :], in1=sn, op=ALU.mult),
                          reads=[("fb", z), "sinT"], writes=[("fb", j2, 1)])
                    S_.op("dve", lambda e: e.tensor_tensor(out=Rv[:, :, 0, :], in0=Rv[:, :, 0, :], in1=Jv[:, :, 0, :], op=ALU.subtract),
                          reads=[("fb", r), ("fb", j2)], writes=[("fb", r)])
                    S_.op("dve", lambda e: e.tensor_tensor(out=Rv[:, :, 1, :], in0=Rv[:, :, 1, :], in1=Jv[:, :, 1, :], op=ALU.add),
                          reads=[("fb", r), ("fb", j2, 1)], writes=[("fb", r)])
                    S_.lastw[("fb", j2)] = S_.lastw[("fb", r)]
                    S_.readers[("fb", j2)] = []
                    stt["Rf"], stt["rres"] = R, ("fb", r)
                I.B = B

                def C(pend):
                    z = stt["z"]
                    if isnorm:
                        Rf, rres = stt["Rf"], stt["rres"]
                        okey = f"o_fb{rres[1]}"
                        if g == 1:
                            if kind == "p" and t0 >= S - 512:
                                r0 = t0 - (S - 512) + s * 128
                                S_.dma("sp", lambda e: [e.dma_start(out=akp[seq, r0:r0 + 128, :], in_=Rf[0:np_, :])], 1, okey, reads=[rres])
                            elif kind == "s":
                                S_.dma("sp", lambda e: [e.dma_start(out=aks[s, :, :], in_=Rf[0:np_, :])], 1, okey, reads=[rres])
                        if g == 4:
                            if kind == "p":
                                S_.dma("sp", lambda e: [e.dma_start(out=bkp[seq, t0 + s * 128:t0 + (s + 1) * 128, :], in_=Rf[0:np_, :])],
                                       1, okey, reads=[rres])
                            else:
                                S_.dma("sp", lambda e: [e.dma_start(out=bks[s, :, :], in_=Rf[0:np_, :])], 1, okey, reads=[rres])
                        zi = nxt("zb", NZB)
                        S_.op("act", lambda e: e.activation(out=zb[zi][0:np_, :], in_=Rf[0:np_, :], func=AF.Copy),
                              reads=[rres], writes=[("zb", zi)])
                        if g == 0:
                            dst = lambda: qaT[:, :, col[s]:col[s] + np_]
                            wr = [("qaT", s)]
                        elif g == 3:
                            dst = lambda: qbT[:, :, col[s]:col[s] + np_]
                            wr = [("qbT", s)]
                        elif g == 1:
                            if kind == "p":
                                dst = lambda: kaT[:, :, slotA * 128:slotA * 128 + np_]
                                wr = [("kaT", slotA)]
                            else:
                                dst = lambda: qaT[:, :, 32 + col[s]:32 + col[s] + np_]
                                wr = [("kaS", s)]
                        else:
                            if kind == "p":
                                dst = lambda: kbT[:, :, kcolB:kcolB + np_]
                                wr = [("kbT", ktB)]
                            else:
                                dst = lambda: qbT[:, :, 32 + col[s]:32 + col[s] + np_]
                                wr = [("kbS", s)]
                        if g == 0:
                            pend.append(lambda: to_fm(zb[zi][0:np_, :], np_, 4, None, None, [("zb", zi)], wr,
                                                      split=(lambda: qaz[0][:, :, col[s]:col[s] + np_], lambda: qaz[1][:, :, col[s]:col[s] + np_])))
                        elif g == 3:
                            pend.append(lambda: to_fm(zb[zi][0:np_, :], np_, 4, None, None, [("zb", zi)], wr,
                                                      split=(lambda: qbz[0][:, :, col[s]:col[s] + np_], lambda: qbz[1][:, :, col[s]:col[s] + np_])))
                        else:
                            pend.append(lambda: to_fm(zb[zi][0:np_, :], np_, 4, None, dst, [("zb", zi)], wr, evac=("dve" if (g + s) % 2 else "act")))
                    elif g == 2:
                        if kind == "p" and t0 >= S - 512:
                            r0 = t0 - (S - 512) + s * 128
                            S_.dma("sp", lambda e: [e.dma_start(out=avp[seq, r0:r0 + 128, :], in_=fb[z][0:np_, :])], 1, f"o_fb{z}", reads=[("fb", z)])
                        elif kind == "s":
                            S_.dma("sp", lambda e: [e.dma_start(out=avs[s, :, :], in_=fb[z][0:np_, :])], 1, f"o_fb{z}", reads=[("fb", z)])
                        if kind == "p":
                            S_.op("pool", lambda e: e.tensor_copy(out=va[0:np_, slotA, :], in_=fb[z][0:np_, :]), reads=[("fb", z)], writes=[("va", slotA)])
                        else:
                            S_.op("pool", lambda e: e.tensor_copy(out=cst[0:np_, 6 + s, :], in_=fb[z][0:np_, :]), reads=[("fb", z)], writes=[("vaS", s)])
                    else:
                        if kind == "p":
                            S_.dma("sp", lambda e: [e.dma_start(out=bvp[seq, t0 + s * 128:t0 + (s + 1) * 128, :], in_=fb[z][0:np_, :])],
                                   1, f"o_fb{z}", reads=[("fb", z)])
                            S_.op("pool", lambda e: e.tensor_copy(out=vb[0:np_, ktB, :], in_=fb[z][0:np_, :]), reads=[("fb", z)], writes=[("vb", ktB)])
                        else:
                            S_.dma("sp", lambda e: [e.dma_start(out=bvs[s, :, :], in_=fb[z][0:np_, :])], 1, f"o_fb{z}", reads=[("fb", z)])
                            S_.op("pool", lambda e: e.tensor_copy(out=cst[0:np_, 4 + s, :], in_=fb[z][0:np_, :]), reads=[("fb", z)], writes=[("vbS", s)])
                I.C = C
                return I

            def p1b():
                S_.op("pool", lambda e: e.memset(qaz[0][64:128], 0.0), writes=[("mT", m2) for m2 in range(4)] + ["qzero0"])
                S_.op("pool", lambda e: e.memset(qaz[1][0:64], 0.0), writes=[("mT", m2) for m2 in range(4, 8)] + ["qzero1"])
                S_.op("pool", lambda e: e.memset(qbz[0][64:128], 0.0), writes=[("uT", j2) for j2 in range(8, 12)] + ["qzero2"])
                S_.op("pool", lambda e: e.memset(qbz[1][0:64], 0.0), writes=[("uT", j2) for j2 in range(26, 30)] + ["qzero3"])
                items = []
                for g in range(6):
                    grp = {}
                    for s in range(nsub):
                        items.append(make_item(g, s, grp))
                n = len(items)
                pend = []
                for i in range(n + 2):
                    if i < n:
                        items[i].A()
                    if 0 <= i - 1 < n:
                        items[i - 1].B()
                    if 0 <= i - 2 < n:
                        items[i - 2].C(pend)
                    while len(pend) > 4:
                        pend.pop(0)()
                while pend:
                    pend.pop(0)()
            T.p1b = p1b

            def attn_a():
                nq = np_
                pso, pss = 4, 5
                pend = []

                def sub_items(s):
                    if kind == "p":
                        qt = t0 // 128 + s
                        kts = [k for k in range(qt - 4, qt + 1) if k >= 0]
                        jlist = [k - (qt - 4) for k in kts]
                        ktinfo = [(k % 8, 128) for k in kts]
                        btab = biasA
                        hist_rd = [("kaT", k % 8) for k in kts]
                        hist_rv = [("va", k % 8) for k in kts]
                    else:
                        jlist = [0, 1, 2, 3, 4]
                        ktinfo = [(0, 128), (1, 128), (2, 128), (3, 128), (4, 16)]
                        btab = biasS
                        hist_rd = [("kaT", k) for k in range(5)]
                        hist_rv = [("va", k) for k in range(5)]
                    c0 = col[s]
                    its = []
                    for h in range(8):
                        c, po = h // 2, (h % 2) * 64
                        ab = h % 2
                        bA, bB = (0, 1) if ab == 0 else (2, 3)

                        def s1(c=c, po=po, bA=bA, bB=bB, h=h):
                            def qk(e):
                                ins = None
                                for (j, (slot, nk)) in zip(jlist, ktinfo):
                                    bank, cc = (bA, j * 128) if j < 4 else (bB, 0)
                                    e.matmul(P[0:nk, bank, cc:cc + nq], lhsT=kaT[:, c, slot * 128:slot * 128 + nk],
                                             rhs=qaz[h % 2][:, c, c0:c0 + nq], start=True, stop=False)
                                    ins = e.matmul(P[0:nk, bank, cc:cc + nq], lhsT=ident[0:nk, 0:nk], rhs=btab[0:nk, h, j, 0:nq],
                                                   start=False, stop=True)
                                return ins
                            S_.op("pe", qk, reads=hist_rd + [("qaT", s), ("qaT_hi", s), "qzero0", "qzero1", "ident", "biasA", "biasS"], writes=[("ps", bA), ("ps", bB)])

                        def s2(bA=bA, bB=bB, ab=ab):
                            js = [j for j in jlist if j < 4]
                            if js:
                                jm = min(js)
                                S_.op("act", lambda e: e.activation(out=pA[ab][:, jm:4, 0:nq],
                                                                    in_=P[:, bA, :].rearrange("p (j q) -> p j q", j=4)[:, jm:4, 0:nq],
                                                                    func=AF.Exp, scale=0.125),
                                      reads=[("ps", bA)], writes=[("pA", ab, 0)])
                            nk4 = ktinfo[-1][1]
                            S_.op("act", lambda e: e.activation(out=pA[ab][0:nk4, 4, 0:nq], in_=P[0:nk4, bB, 0:nq], func=AF.Exp, scale=0.125),
                                  reads=[("ps", bB)], writes=[("pA", ab, 1)])

                        def s3(h=h, ab=ab):
                            def pv(e):
                                ins = None
                                n = len(jlist)
                                for i, (j, (slot, nk)) in enumerate(zip(jlist, ktinfo)):
                                    e.matmul(P[0:nq, pso, h * 64:(h + 1) * 64], lhsT=pA[ab][0:nk, j, 0:nq], rhs=va[0:nk, slot, h * 64:(h + 1) * 64],
                                             start=(i == 0), stop=(i == n - 1))
                                    ins = e.matmul(P[0:nq, pss, 2 * h:2 * h + 2], lhsT=pA[ab][0:nk, j, 0:nq], rhs=ones_b[0:nk, 0:2],
                                                   start=(i == 0), stop=(i == n - 1))
                                return ins
                            S_.op("pe", pv, reads=[("pA", ab, 0), ("pA", ab, 1)] + hist_rv + ["ones_b"], writes=[("ps", pso), ("ps", pss)])
                        post = []
                        if h == 1:
                            post.append(lambda: pend.pop()() if pend else None)
                        if h == 7:
                            def fin(s=s, c0=c0):
                                S_.op("dve", lambda e: e.reciprocal(out=rsA[0:nq, :], in_=P[0:nq, pss, 0:16].rearrange("p (h t) -> p h t", t=2)[:, :, 0]),
                                      reads=[("ps", pss)], writes=["rsA"])
                                S_.op("dve", lambda e: e.tensor_tensor(out=ya[0:nq, :].rearrange("p (h d) -> p h d", h=8),
                                                                       in0=P[0:nq, pso, :].rearrange("p (h d) -> p h d", h=8),
                                                                       in1=rsA[0:nq, :].unsqueeze(2).broadcast_to([nq, 8, 64]), op=ALU.mult),
                                      reads=[("ps", pso), "rsA"], writes=["ya"])
                                pend.append(lambda: to_fm(ya[0:nq, :], nq, 4, None, lambda: yaT[:, :, c0:c0 + nq], ["ya"], [("yaT", s)]))
                            post.append(fin)
                        its.append((s1, s2, s3, post))
                    return its

                if kind == "p":
                    allit = []
                    for s in range(nsub):
                        allit += sub_items(s)
                    pipeline(allit, PIPE_A)
                    while pend:
                        pend.pop()()
                else:
                    for s in range(nsub):
                        S_.dma("pool", lambda e, s=s: [e.dma_start(out=cst[:, 0:4, :], in_=cak[s].rearrange("(k p) n -> p k n", p=128))],
                               1, "cstA", writes=[("cst", k) for k in range(4)])
                        for k in range(4):
                            to_fm(cst[:, k, :], 128, 4, None, lambda k=k: kaT[:, :, k * 128:(k + 1) * 128], [("cst", k)], [("kaT", k)])
                        S_.dma("pool", lambda e, s=s: [e.dma_start(out=va[:, 0:4, :], in_=cav[s].rearrange("(k p) n -> p k n", p=128))],
                               1, "vaL", writes=[("va", k) for k in range(4)])
                        S_.op("act", lambda e, s=s: e.activation(out=kaT[:, :, 512:528], in_=qaT[:, :, 32 + col[s]:32 + col[s] + 16], func=AF.Copy),
                              reads=[("kaS", s)], writes=[("kaT", 4)])
                        S_.op("pool", lambda e, s=s: e.tensor_copy(out=va[0:16, 4, :], in_=cst[0:16, 6 + s, :]),
                              reads=[("vaS", s)], writes=[("va", 4)])
                        pipeline(sub_items(s), PIPE_A)
                        while pend:
                            pend.pop()()
            T.attn_a = attn_a

            def attn_b():
                if kind == "p":
                    groups = [(0, NT, list(range(0, (t0 + 512) // 128)), None)]
                else:
                    groups = [(col[s], 16, list(range(9)), s) for s in range(nsub)]
                pend2 = []

                def make_comb1(h, qc0, Nq):
                    def comb1(h=h, qc0=qc0, Nq=Nq):
                        f0, f1, f2, f3, f4 = [nxt("fb", NFB) for _ in range(5)]
                        S_.op("dve", lambda e: e.reciprocal(out=fb[f0][:, 0:Nq], in_=P[:, 1, 0:Nq]), reads=[("ps", 1)], writes=[("fb", f0)])
                        S_.op("dve", lambda e: e.tensor_tensor(out=fb[f1][:, 0:Nq], in0=P[:, 0, 0:Nq], in1=fb[f0][:, 0:Nq], op=ALU.mult),
                              reads=[("ps", 0), ("fb", f0)], writes=[("fb", f1)])
                        S_.op("dve", lambda e: e.reciprocal(out=fb[f2][:, 0:Nq], in_=P[:, 3, 0:Nq]), reads=[("ps", 3)], writes=[("fb", f2)])
                        S_.op("dve", lambda e: e.tensor_tensor(out=fb[f3][:, 0:Nq], in0=P[:, 2, 0:Nq], in1=fb[f2][:, 0:Nq], op=ALU.mult),
                              reads=[("ps", 2), ("fb", f2)], writes=[("fb", f3)])
                        S_.op("dve", lambda e: e.scalar_tensor_tensor(out=fb[f4][:, 0:Nq], in0=fb[f3][:, 0:Nq], scalar=neglam,
                                                                      in1=fb[f1][:, 0:Nq], op0=ALU.mult, op1=ALU.add),
                              reads=[("fb", f3), ("fb", f1), "neglam"], writes=[("fb", f4)])

                        def comb2(nb):
                            S_.op("act", lambda e: e.activation(out=fb[f0][:, 0:Nq], in_=fb[f4][:, 0:Nq], func=AF.Square),
                                  reads=[("fb", f4)], writes=[("fb", f0)])
                            S_.op("pe", lambda e: e.matmul(SB[nb][:, 0:Nq], lhsT=ones_f[:, :], rhs=fb[f0][:, 0:Nq], start=True, stop=True),
                                  reads=[("fb", f0), "ones_f"], writes=[SBR[nb]])
                            S_.op("act", lambda e: e.activation(out=fb[f2][:, 0:Nq], in_=SB[nb][:, 0:Nq], func=AF.Sqrt, bias=epsT[:, 0:1],
                                                                scale=1.0 / 128),
                                  reads=[SBR[nb], "epsT"], writes=[("fb", f2)])
                            S_.op("dve", lambda e: e.reciprocal(out=fb[f3][:, 0:Nq], in_=fb[f2][:, 0:Nq]), reads=[("fb", f2)], writes=[("fb", f3)])
                            S_.op("dve", lambda e: e.scalar_tensor_tensor(out=ybT[:, h, qc0:qc0 + Nq], in0=fb[f4][:, 0:Nq], scalar=subc[:, 0:1],
                                                                          in1=fb[f3][:, 0:Nq], op0=ALU.mult, op1=ALU.mult),
                                  reads=[("fb", f4), ("fb", f3), "subc"], writes=[("ybT", h)])
                        pend2.append(comb2)
                    return comb1

                for (qc0, nqb, ktl, ss) in groups:
                    if kind == "s":
                        S_.dma("pool", lambda e, ss=ss: [e.dma_start(out=cst[:, 0:4, :], in_=cbk[ss, 0:512].rearrange("(k p) n -> p k n", p=128))],
                               1, "cstA", writes=[("cst", k) for k in range(4)])
                        for k in range(4):
                            to_fm(cst[:, k, :], 128, 4, None, lambda k=k: kbT[:, :, k * 128:(k + 1) * 128], [("cst", k)], [("kbT", k)])
                        S_.dma("pool", lambda e, ss=ss: [e.dma_start(out=cst[:, 0:4, :], in_=cbk[ss, 512:1024].rearrange("(k p) n -> p k n", p=128))],
                               1, "cstA", writes=[("cst", k) for k in range(4)])
                        for k in range(4):
                            to_fm(cst[:, k, :], 128, 4, None, lambda k=k: kbT[:, :, (4 + k) * 128:(5 + k) * 128], [("cst", k)], [("kbT", 4 + k)])
                        S_.dma("pool", lambda e, ss=ss: [e.dma_start(out=vb[:, 0:8, :], in_=cbv[ss].rearrange("(k p) n -> p k n", p=128))],
                               1, "vbL", writes=[("vb", k) for k in range(8)])
                        S_.op("act", lambda e, ss=ss: e.activation(out=kbT[:, :, 1024:1040], in_=qbT[:, :, 32 + col[ss]:32 + col[ss] + 16], func=AF.Copy),
                              reads=[("kbS", ss)], writes=[("kbT", 8)])
                        S_.op("pool", lambda e, ss=ss: e.tensor_copy(out=vb[0:16, 8, :], in_=cst[0:16, 4 + ss, :]),
                              reads=[("vbS", ss)], writes=[("vb", 8)])
                    Nq = nqb
                    its = []
                    for h in range(4):
                        for r in range(2):
                            accO, accS = (0, 1) if r == 0 else (2, 3)
                            if kind == "s":
                                idx = len(its)
                                st = {}

                                def s1(h=h, r=r, st=st, qc0=qc0, idx=idx):
                                    sb_ = idx % (BDEPTH + 1)
                                    st["sb"] = sb_

                                    def sc(e):
                                        ins = None
                                        for kt in ktl:
                                            nk = 16 if kt == 8 else 128
                                            ins = e.matmul(SB[sb_][0:nk, kt * 16:(kt + 1) * 16], lhsT=kbT[:, h, kt * 128:kt * 128 + nk],
                                                           rhs=qbz[r][:, h, qc0:qc0 + 16], start=True, stop=True)
                                        return ins
                                    S_.op("pe", sc, reads=[("kbT", kt) for kt in ktl] + ["qzero2", "qzero3"] + [("qbT", s2) for s2 in range(nsub)]
                                          + [("qbT_hi", s2) for s2 in range(nsub)], writes=[SBR[sb_]])

                                def s2(st=st):
                                    st["pi"] = nxt("pB", 3)
                                    pi, sb_ = st["pi"], st["sb"]
                                    S_.op("act", lambda e: e.activation(out=pB[pi][:, 0:128], in_=SB[sb_][:, 0:128], func=AF.Exp, scale=0.125),
                                          reads=[SBR[sb_]], writes=[("pB", pi)])
                                    S_.op("act", lambda e: e.activation(out=pB[pi][0:16, 128:144], in_=SB[sb_][0:16, 128:144], func=AF.Exp, scale=0.125),
                                          reads=[SBR[sb_]], writes=[("pB", pi, 1)])

                                def s3(h=h, st=st, accO=accO, accS=accS):
                                    pi = st["pi"]

                                    def pvb(e):
                                        ins = None
                                        for ki, kt in enumerate(ktl):
                                            nk = 16 if kt == 8 else 128
                                            e.matmul(P[:, accO, 0:16], lhsT=vb[0:nk, kt, h * 128:(h + 1) * 128], rhs=pB[pi][0:nk, kt * 16:(kt + 1) * 16],
                                                     start=(ki == 0), stop=(ki == len(ktl) - 1), skip_group_check=True)
                                            ins = e.matmul(P[:, accS, 0:16], lhsT=ones_b[0:nk, :], rhs=pB[pi][0:nk, kt * 16:(kt + 1) * 16],
                                                           start=(ki == 0), stop=(ki == len(ktl) - 1), skip_group_check=True)
                                        return ins
                                    S_.op("pe", pvb, reads=[("pB", pi), ("pB", pi, 1), "ones_b"] + [("vb", kt) for kt in ktl],
                                          writes=[("ps", accO), ("ps", accS)])
                                post = []
                                if r == 0:
                                    post.append(lambda idx=idx: pend2.pop()(idx % (BDEPTH + 1)) if pend2 else None)
                                if r == 1:
                                    post.append(make_comb1(h, qc0, Nq))
                                its.append((s1, s2, s3, post))
                                continue
                            for ki, kt in enumerate(ktl):
                                nk = 16 if (kind == "s" and kt == 8) else 128
                                if kind == "p":
                                    qlo = max(0, kt * 128 - t0)
                                    diag = kt * 128 >= t0
                                else:
                                    qlo = 0
                                    diag = False
                                N = nqb - qlo
                                first, last = (ki == 0), (ki == len(ktl) - 1)
                                st = {}

                                idx = len(its)

                                def s1(h=h, r=r, kt=kt, nk=nk, qlo=qlo, N=N, st=st, qc0=qc0, idx=idx):
                                    st["sb"] = idx % (BDEPTH + 1)
                                    sb_ = st["sb"]
                                    S_.op("pe", lambda e: e.matmul(
                                        SB[sb_][0:nk, 0:N], lhsT=kbT[:, h, kt * 128:kt * 128 + nk],
                                        rhs=qbz[r][:, h, qc0 + qlo:qc0 + qlo + N], start=True, stop=True),
                                        reads=[("kbT", kt), "qzero2", "qzero3"] + [("qbT", s2) for s2 in range(nsub)] + [("qbT_hi", s2) for s2 in range(nsub)],
                                        writes=[SBR[sb_]])

                                def s2(nk=nk, N=N, st=st, diag=diag):
                                    st["pi"] = nxt("pB", 3)
                                    pi, sb_ = st["pi"], st["sb"]
                                    S_.op("act", lambda e: e.activation(out=pB[pi][0:nk, 0:N], in_=SB[sb_][0:nk, 0:N], func=AF.Exp, scale=0.125),
                                          reads=[SBR[sb_]], writes=[("pB", pi)])
                                    if diag:
                                        S_.op("act", lambda e: e.memzero(pB[pi][64:128, 0:64]), reads=[], writes=[("pB", pi)])

                                def s3(h=h, kt=kt, nk=nk, qlo=qlo, N=N, st=st, first=first, last=last, accO=accO, accS=accS):
                                    pi = st["pi"]

                                    def pvb(e):
                                        e.matmul(P[:, accO, qlo:qlo + N], lhsT=vb[0:nk, kt, h * 128:(h + 1) * 128], rhs=pB[pi][0:nk, 0:N],
                                                 start=first, stop=last, skip_group_check=True)
                                        return e.matmul(P[:, accS, qlo:qlo + N], lhsT=ones_b[0:nk, :], rhs=pB[pi][0:nk, 0:N],
                                                        start=first, stop=last, skip_group_check=True)
                                    S_.op("pe", pvb, reads=[("pB", pi), ("vb", kt), "ones_b"], writes=[("ps", accO), ("ps", accS)])
                                post = []
                                if r == 0 and ki == min(2, len(ktl) - 1):
                                    post.append(lambda idx=idx: pend2.pop()(idx % (BDEPTH + 1)) if pend2 else None)
                                if r == 1 and last:
                                    post.append(make_comb1(h, qc0, Nq))
                                if False:
                                    def comb1(h=h, qc0=qc0, Nq=Nq):
                                        f0, f1, f2, f3, f4 = [nxt("fb", NFB) for _ in range(5)]
                                        S_.op("dve", lambda e: e.reciprocal(out=fb[f0][:, 0:Nq], in_=P[:, 1, 0:Nq]), reads=[("ps", 1)], writes=[("fb", f0)])
                                        S_.op("dve", lambda e: e.tensor_tensor(out=fb[f1][:, 0:Nq], in0=P[:, 0, 0:Nq], in1=fb[f0][:, 0:Nq], op=ALU.mult),
                                              reads=[("ps", 0), ("fb", f0)], writes=[("fb", f1)])
                                        S_.op("dve", lambda e: e.reciprocal(out=fb[f2][:, 0:Nq], in_=P[:, 3, 0:Nq]), reads=[("ps", 3)], writes=[("fb", f2)])
                                        S_.op("dve", lambda e: e.tensor_tensor(out=fb[f3][:, 0:Nq], in0=P[:, 2, 0:Nq], in1=fb[f2][:, 0:Nq], op=ALU.mult),
                                              reads=[("ps", 2), ("fb", f2)], writes=[("fb", f3)])
                                        S_.op("dve", lambda e: e.scalar_tensor_tensor(out=fb[f4][:, 0:Nq], in0=fb[f3][:, 0:Nq], scalar=neglam,
                                                                                      in1=fb[f1][:, 0:Nq], op0=ALU.mult, op1=ALU.add),
                                              reads=[("fb", f3), ("fb", f1), "neglam"], writes=[("fb", f4)])

                                        def comb2(nb):
                                            S_.op("act", lambda e: e.activation(out=fb[f0][:, 0:Nq], in_=fb[f4][:, 0:Nq], func=AF.Square),
                                                  reads=[("fb", f4)], writes=[("fb", f0)])
                                            S_.op("pe", lambda e: e.matmul(SB[nb][:, 0:Nq], lhsT=ones_f[:, :], rhs=fb[f0][:, 0:Nq], start=True, stop=True),
                                                  reads=[("fb", f0), "ones_f"], writes=[SBR[nb]])
                                            S_.op("act", lambda e: e.activation(out=fb[f2][:, 0:Nq], in_=SB[nb][:, 0:Nq], func=AF.Sqrt, bias=epsT[:, 0:1],
                                                                                scale=1.0 / 128),
                                                  reads=[SBR[nb], "epsT"], writes=[("fb", f2)])
                                            S_.op("dve", lambda e: e.reciprocal(out=fb[f3][:, 0:Nq], in_=fb[f2][:, 0:Nq]), reads=[("fb", f2)], writes=[("fb", f3)])
                                            S_.op("dve", lambda e: e.scalar_tensor_tensor(out=ybT[:, h, qc0:qc0 + Nq], in0=fb[f4][:, 0:Nq], scalar=subc[:, 0:1],
                                                                                          in1=fb[f3][:, 0:Nq], op0=ALU.mult, op1=ALU.mult),
                                                  reads=[("fb", f4), ("fb", f3), "subc"], writes=[("ybT", h)])
                                        pend2.append(comb2)
                                its.append((s1, s2, s3, post))
                    pipeline(its, PIPE_B, BDEPTH)
                    while pend2:
                        pend2.pop()(0)
            T.attn_b = attn_b

            def p2():
                ctr["ps"] = 4
                gw = {}
                gw[0] = wget(wb0 + 6)
                gw[2] = wget(wb0 + 7)
                wab = wget(wb0 + 8)
                for m in range(8):
                    half, mc = m // 4, (m % 4) * 128
                    if m == 4:
                        gw[1] = wget(wb0 + 9)
                        gw[3] = wget(wb0 + 10)
                        wab = wget(wb0 + 11)
                    wa = (wab[0], wab[1][:, 0:4, :])
                    wb = (wab[0], wab[1][:, 4:8, :])
                    banks = [nxt("ps", 6) for _ in range(4)]
                    specs = [(gw[half], 8, hT, [("hT", s2) for s2 in range(nsub)], mc),
                             (gw[2 + half], 8, hT, [("hT", s2) for s2 in range(nsub)], mc),
                             (wa, 4, yaT, [("yaT", s2) for s2 in range(nsub)], mc),
                             (wb, 4, ybT, [("ybT", h2) for h2 in range(4)], mc)]
                    for bnk, ((wi, W), nk_, act, rd, cc) in zip(banks, specs):
                        def mm(e, bnk=bnk, W=W, nk_=nk_, act=act, cc=cc):
                            ins = None
                            for kc in range(nk_):
                                ins = e.matmul(P[:, bnk, 0:NT], lhsT=W[:, kc, cc:cc + 128], rhs=act[:, kc, 0:NT],
                                               start=(kc == 0), stop=(kc == nk_ - 1))
                            return ins
                        S_.op("pe", mm, reads=rd + [("w", wi)], writes=[("ps", bnk)])
                    s0, s1, s2_, s3 = [nxt("fb", NFB) for _ in range(4)]
                    S_.op("act", lambda e, b=banks[0], s0=s0, m=m: e.activation(out=fb[s0][:, 0:NT], in_=P[:, b, 0:NT], func=AF.Sigmoid,
                                                                               bias=bgT[:, m:m + 1], scale=1.0),
                          reads=[("ps", banks[0]), "bgT"], writes=[("fb", s0)])
                    S_.op("act", lambda e, b=banks[1], s1=s1, m=m: e.activation(out=fb[s1][:, 0:NT], in_=P[:, b, 0:NT], func=AF.Sigmoid,
                                                                               bias=bgT[:, 8 + m:9 + m], scale=1.0),
                          reads=[("ps", banks[1]), "bgT"], writes=[("fb", s1)])
                    S_.op("dve", lambda e, b=banks[2], s0=s0, s2_=s2_: e.tensor_tensor(out=fb[s2_][:, 0:NT], in0=P[:, b, 0:NT], in1=fb[s0][:, 0:NT], op=ALU.mult),
                          reads=[("ps", banks[2]), ("fb", s0)], writes=[("fb", s2_)])
                    S_.op("dve", lambda e, b=banks[3], s1=s1, s3=s3: e.tensor_tensor(out=fb[s3][:, 0:NT], in0=P[:, b, 0:NT], in1=fb[s1][:, 0:NT], op=ALU.mult),
                          reads=[("ps", banks[3]), ("fb", s1)], writes=[("fb", s3)])
                    S_.op("pool", lambda e, s2_=s2_, s3=s3, m=m: e.tensor_tensor(out=mT[:, m, 0:NT], in0=fb[s2_][:, 0:NT], in1=fb[s3][:, 0:NT], op=ALU.add),
                          reads=[("fb", s2_), ("fb", s3)], writes=[("mT", m)])
                    if m == 3:
                        wdone(wb0 + 6); wdone(wb0 + 7); wdone(wb0 + 8)
                    if m == 7:
                        wdone(wb0 + 9); wdone(wb0 + 10); wdone(wb0 + 11)
                wo = [wget(wb0 + 12 + g) for g in range(2)]
                hb2 = {}

                def n2_A(s):
                    st = stat[0:np_, 8 + s:9 + s]
                    rd = [("x1", s, 0), ("x1", s, 1)]
                    rms_rows(X[0:np_, s, :], np_, D, st, rd, f"rs2_{s}")
                    hi = nxt("hb", 2)
                    hb2[s] = hi
                    S_.op("dve", lambda e, s=s, st=st, hi=hi: e.tensor_scalar(out=hbs[hi][0:np_, :], in0=X[0:np_, s, :], scalar1=st, scalar2=None, op0=ALU.mult),
                          reads=rd + [f"rs2_{s}"], writes=[("hb", hi)])

                def n2_B(s):
                    hi = hb2[s]
                    to_fm(hbs[hi][0:np_, :], np_, 8, g2T[:, :], lambda s=s: hT[:, :, col[s]:col[s] + np_], [("hb", hi)], [("hT", s)])

                for s in range(nsub):
                    xi = load_xs(s)
                    for g in range(2):
                        wi, W = wo[g]
                        b = nxt("ps", 6)

                        def mm(e, s=s, b=b, W=W):
                            ins = None
                            for kc in range(8):
                                ins = e.matmul(P[0:np_, b, :], lhsT=mT[:, kc, col[s]:col[s] + np_], rhs=W[:, kc, :], start=(kc == 0), stop=(kc == 7))
                            return ins
                        S_.op("pe", mm, reads=[("mT", m2) for m2 in range(8)] + [("w", wi)], writes=[("ps", b)])
                        S_.op("dve", lambda e, s=s, b=b, g=g, xi=xi: e.tensor_tensor(out=X[0:np_, s, g * 512:(g + 1) * 512], in0=P[0:np_, b, :],
                                                                                   in1=xstage[xi][0:np_, g * 512:(g + 1) * 512], op=ALU.add),
                              reads=[("ps", b), ("xs", xi)], writes=[("x1", s, g)])
                    n2_A(s)
                    if s >= 1:
                        n2_B(s - 1)
                wdone(wb0 + 12); wdone(wb0 + 13)
                n2_B(nsub - 1)
            T.p2 = p2

            def ff1(pre=()):
                for fn in pre:
                    fn()
                for i in range(8):
                    wi, W = wget(wb0 + 14 + i)
                    for mc in range(4):
                        j = i * 4 + mc
                        b = nxt("ps", 6)

                        def mm(e, b=b, W=W, mc=mc):
                            ins = None
                            for kc in range(8):
                                ins = e.matmul(P[:, b, 0:NT], lhsT=W[:, kc, mc * 128:(mc + 1) * 128], rhs=hT[:, kc, 0:NT], start=(kc == 0), stop=(kc == 7))
                            return ins
                        S_.op("pe", mm, reads=[("hT", s2) for s2 in range(nsub)] + [("w", wi)], writes=[("ps", b)])
                        ri = nxt("fb", NFB)
                        S_.op("act", lambda e, b=b, ri=ri: e.activation(out=fb[ri][:, 0:NT], in_=P[:, b, 0:NT], func=AF.Relu),
                              reads=[("ps", b)], writes=[("fb", ri)])
                        eng = "pool" if (j % 2 == 0) else "dve"
                        S_.op(eng, lambda e, ri=ri, j=j: e.tensor_tensor(out=uT[:, j, 0:NT], in0=fb[ri][:, 0:NT], in1=fb[ri][:, 0:NT], op=ALU.mult),
                              reads=[("fb", ri)], writes=[("uT", j)])
                    wdone(wb0 + 14 + i)
            T.ff1 = ff1

            def ff2(hooks=None):
                hooks = hooks or {}
                for g in range(2):
                    banks = [nxt("ps", 6) for _ in range(nsub)]
                    for i in range(4):
                        wi, W = wget(wb0 + 22 + g * 4 + i)
                        for s in range(nsub):
                            def mm(e, s=s, W=W, i=i, b=banks[s]):
                                ins = None
                                for kc in range(8):
                                    ins = e.matmul(P[0:np_, b, :], lhsT=uT[:, i * 8 + kc, col[s]:col[s] + np_], rhs=W[:, kc, :],
                                                   start=(i == 0 and kc == 0), stop=(i == 3 and kc == 7))
                                return ins
                            S_.op("pe", mm, reads=[("uT", i * 8 + kc) for kc in range(8)] + [("w", wi)], writes=[("ps", banks[s])])
                        wdone(wb0 + 22 + g * 4 + i)
                        for fn in hooks.get(g * 4 + i, ()):
                            fn()
                    for s in range(nsub):
                        S_.op("dve", lambda e, s=s, b=banks[s], g=g: e.tensor_tensor(out=X[0:np_, s, g * 512:(g + 1) * 512], in0=P[0:np_, b, :],
                                                                                    in1=X[0:np_, s, g * 512:(g + 1) * 512], op=ALU.add),
                              reads=[("ps", banks[s]), ("x1", s, g)], writes=[("x1", s, g)])
                yrd = [("x1", s, g) for s in range(nsub) for g in range(2)]
                if kind == "p":
                    S_.dma("sp", lambda e: [e.dma_start(out=yp[seq, t0:t0 + 512, :].rearrange("(s p) d -> p s d", p=128), in_=X[:, :, :])],
                           1, "yo", reads=yrd)
                else:
                    S_.dma("sp", lambda e: [e.dma_start(out=ys.rearrange("s p d -> p s d"), in_=X[0:16, 0:2, :])],
                           1, "yo", reads=yrd)
            T.ff2 = ff2
            return T

        tiles = []
        for q in range(NSEQ):
            for t0 in range(0, S, 512):
                tiles.append(("p", q, t0, [128] * 4))
        if with_sample:
            tiles.append(("s", 0, 0, [16, 16]))
        TL = [make_tile(kind, q, t0, subs, ti) for ti, (kind, q, t0, subs) in enumerate(tiles)]
        wstate["total"] = len(tiles) * NPIECE
        TL[0].p1a()
        for i, T in enumerate(TL):
            T.p1b()
            T.attn_a()
            T.attn_b()
            T.p2()
            if i + 1 < len(TL):
                N = TL[i + 1]
                ns = N.nsub
                T.ff1(pre=[lambda N=N: N.p1a_A(0), lambda N=N: N.p1a_A(1)])
                hooks = {}
                for s in range(ns):
                    hooks.setdefault(s, []).append(lambda N=N, s=s: N.p1a_B(s))
                    if s + 2 < ns:
                        hooks[s].append(lambda N=N, s=s: N.p1a_A(s + 2))
                T.ff2(hooks)
            else:
                T.ff1()
                T.ff2()

        if DEBUG:
            for nm, t, shp, dt in [("yaT", yaT, (128, 4, 512), BF16), ("ybT", ybT, (128, 4, 512), BF16), ("mT", qT, (128, 8, 512), BF16),
                                   ("hT", hT, (128, 8, 512), BF16), ("uT", uT, (128, 32, 512), BF16), ("X", xt[0], (128, 4, D), F32),
                                   ("kbT", kbT, (128, 4, SK), BF16), ("vb", vb, (128, NKT, 512), BF16), ("kaT", kaT, (128, 4, 1024), BF16),
                                   ("va", va, (128, 8, 512), BF16), ("lamt", lamt, (128, 8), F32), ("biasA", biasA, (128, 8, 5, 128), BF16),
                                   ("ya", ya, (128, 512), BF16), ("pA0", pA[0], (128, 5, 128), BF16), ("pA1", pA[1], (128, 5, 128), BF16), ("rsA", rsA, (128, 8), F32)]:
                dd = nc.dram_tensor("dbg_" + nm, list(shp), dt, kind="ExternalOutput").ap()
                allres = list(S_.lastw.keys())
                S_.fence("sp", allres)
                S_.dma("sp", lambda e, dd=dd, t=t: [e.dma_start(out=dd, in_=t[:] if not isinstance(t, bass.AP) else t)], 1, "dbg_" + nm)

        dkeys = [k for k in S_.cnt if k not in Sched.ENG]
        S_.final_wait("sp", dkeys)

        sems = {}
        for k in S_.cnt:
            sems[k] = es.enter_context(nc.semaphore("s_" + str(k).replace(" ", "").replace("(", "").replace(")", "").replace(",", "_").replace("'", "")))
        for k, v in S_.cnt.items():
            assert v < 60000, (k, v)
        engmap = {"pe": "tensor", "act": "scalar", "dve": "vector", "pool": "gpsimd", "sp": "sync"}
        with nc.Block() as block:
            def emit(name):
                def body(e):
                    for it in S_.lists[name]:
                        if it[0] == "wait":
                            e.wait_ge(sems[it[1]], it[2])
                        elif it[0] == "op":
                            ins = it[1](e)
                            ins.then_inc(sems[it[2]], 1)
                        else:
                            for ins in it[1](e):
                                ins.then_inc(sems[it[2]], 16)
                return body
            for name in Sched.ENG:
                getattr(block, engmap[name])(emit(name))
    return nc


def _rope_tables(S):
    half = 32
    inv = 10000.0 ** (-np.arange(half, dtype=np.float64) / half)
    nsub = S // 128
    pos = np.zeros((128, nsub + 1), np.float64)
    for k in range(nsub):
        pos[:, k] = k * 128 + np.arange(128)
    pos[:, nsub] = PAST + np.arange(128)
    ang = pos[:, :, None] * inv[None, None, :]
    return np.cos(ang).astype(np.float32), np.sin(ang).astype(np.float32)


def _bias_tables(rel):
    kk = np.arange(640)[:, None]
    qq = np.arange(128)[None, :]
    dist = (qq + 512) - kk
    idx = np.clip(dist, -63, 256) + 63
    kc = kk // 64 - 8
    qc = qq // 64
    valid = (kc <= qc) & (kc >= qc - 8)
    bp = rel[:, idx]
    bp = np.where(valid[None], bp, np.float32(NEG)).astype(np.float32)
    biasA = np.ascontiguousarray(bp.reshape(8, 5, 128, 128).transpose(2, 0, 1, 3))
    kpos = np.concatenate([PAST - 512 + np.arange(512), PAST + np.arange(16), np.zeros(112)]).astype(np.int64)
    qpos = PAST + np.arange(16)
    d2 = qpos[None, :] - kpos[:, None]
    i2 = np.clip(d2, -63, 256) + 63
    bs = rel[:, i2].astype(np.float32)
    biasS = np.ascontiguousarray(bs.reshape(8, 5, 128, 16).transpose(2, 0, 1, 3))
    return biasA, biasS


def _rep(v, n):
    return np.ascontiguousarray(np.broadcast_to(np.tile(np.asarray(v, np.float32), n)[None, :], (128, 512)))


_CACHE = {}


def kernel(x_prompt, x_sample, cache_a_k, cache_a_v, cache_b_k, cache_b_v,
           ln1_g, w_in, qn_a, kn_a, rel_bias, qn_b, kn_b,
           lam_q1, lam_k1, lam_q2, lam_k2, subln_g,
           w_gate, b_gate, w_proj_a, w_proj_b, w_out, ln2_g, w_ff1, w_ff2, _ncores=NCORES):
    f = lambda a: np.ascontiguousarray(np.asarray(a, dtype=np.float32))
    x_prompt = f(x_prompt); x_sample = f(x_sample)
    B, S, _ = x_prompt.shape
    n = _ncores
    NSEQ = B // n
    key = (NSEQ, S)
    if key not in _CACHE:
        _CACHE[key] = build_program(NSEQ, S, True)
    nc = _CACHE[key]
    cosT, sinT = _rope_tables(S)
    biasA, biasS = _bias_tables(f(rel_bias)[0])
    normv = np.ascontiguousarray(np.broadcast_to(np.stack([f(qn_a)[0], f(kn_a)[0], f(qn_b)[0], f(kn_b)[0]])[None], (128, 4, 64)))
    lamv = np.ascontiguousarray(np.broadcast_to(
        np.stack([f(lam_q1)[0], f(lam_k1)[0], f(lam_q2)[0], f(lam_k2)[0]])[None], (128, 4, 64)))
    shared = {
        "w_in": f(w_in)[0], "w_gate": f(w_gate)[0], "w_pa": f(w_proj_a)[0], "w_pb": f(w_proj_b)[0],
        "w_out": f(w_out)[0], "w_ff1": f(w_ff1)[0], "w_ff2": f(w_ff2)[0],
        "g1T": np.ascontiguousarray(f(ln1_g)[0].reshape(8, 128).T), "g2T": np.ascontiguousarray(f(ln2_g)[0].reshape(8, 128).T),
        "bgT": np.ascontiguousarray(f(b_gate)[0].reshape(16, 128).T), "subT": np.ascontiguousarray(f(subln_g)[0].reshape(128, 1)),
        "normv": np.ascontiguousarray(normv), "lamv": lamv, "ropec": cosT, "ropes": sinT,
        "biasA": biasA, "biasS": biasS, "ident": np.eye(128, dtype=np.float32),
    }
    cak = f(cache_a_k)[0].reshape(-1, 512, 512); cav = f(cache_a_v)[0].reshape(-1, 512, 512)
    cbk = f(cache_b_k)[0].reshape(-1, 1024, 512); cbv = f(cache_b_v)[0].reshape(-1, 1024, 512)
    in_maps = []
    for c in range(n):
        m = dict(shared)
        m["xp"] = x_prompt[c * NSEQ:(c + 1) * NSEQ]
        m["xs"] = x_sample[2 * c:2 * c + 2]
        m["cak"] = cak[2 * c:2 * c + 2]; m["cav"] = cav[2 * c:2 * c + 2]
        m["cbk"] = cbk[2 * c:2 * c + 2]; m["cbv"] = cbv[2 * c:2 * c + 2]
        in_maps.append(m)
    res = run_bass_kernel_spmd(nc, in_maps, core_ids=list(range(n)))
    R = res.results
    if DEBUG:
        global LAST_RESULTS
        LAST_RESULTS = R
    cat = lambda k: np.concatenate([r[k] for r in R], axis=0)
    y_p = cat("yp"); y_s = cat("ys")
    akp = cat("akp").reshape(1, B, 512, 8, 64); avp = cat("avp").reshape(1, B, 512, 8, 64)
    bkp = cat("bkp").reshape(1, B, S, 4, 2, 64); bvp = cat("bvp").reshape(1, B, S, 4, 128)
    nb = 2 * n
    aks = cat("aks").reshape(1, nb, 16, 8, 64); avs = cat("avs").reshape(1, nb, 16, 8, 64)
    bks = cat("bks").reshape(1, nb, 16, 4, 2, 64); bvs = cat("bvs").reshape(1, nb, 16, 4, 128)
    return (y_p, y_s, akp, avp, bkp, bvp, aks, avs, bks, bvs)
```

```python
import math
import numpy as np
import concourse.bass as bass
import concourse.mybir as mybir
from concourse.bass_utils import run_bass_kernel_spmd

F32 = mybir.dt.float32
BF16 = mybir.dt.bfloat16
AF = mybir.ActivationFunctionType
ALU = mybir.AluOpType
AX = mybir.AxisListType

D = 1024
NCORES = 8
EPS = 1e-6
PAST = 1024
LAM_INIT = 0.8 - 0.6 * math.exp(0.0)
NEG = -30000.0
NSLOT = 4
DEBUG = False
DEBUG_STOP = None
PIPE = True
PIPE_A = True
PIPE_B = True


class Sched:
    ENG = ("pe", "act", "dve", "pool", "sp")

    def __init__(self):
        self.lists = {e: [] for e in self.ENG}
        self.cnt = {}
        self.known = {e: {} for e in self.ENG}
        self.lastw = {}
        self.readers = {}

    def _need(self, eng, tok):
        k, v = tok
        if self.known[eng].get(k, 0) < v:
            self.known[eng][k] = v
            self.lists[eng].append(("wait", k, v))

    def _sync(self, eng, reads, writes):
        for r in reads:
            t = self.lastw.get(r)
            if t is not None:
                self._need(eng, t)
        for w in writes:
            t = self.lastw.get(w)
            if t is not None and (t[0] != eng or eng != "pe"):
                self._need(eng, t)
            for t in self.readers.get(w, ()):
                if t[0] != eng or eng != "pe":
                    self._need(eng, t)

    def _record(self, tok, reads, writes):
        for w in writes:
            self.lastw[w] = tok
            self.readers[w] = []
        for r in reads:
            self.readers.setdefault(r, []).append(tok)

    def op(self, eng, fn, reads=(), writes=()):
        self._sync(eng, reads, writes)
        self.cnt[eng] = self.cnt.get(eng, 0) + 1
        tok = (eng, self.cnt[eng])
        self.lists[eng].append(("op", fn, eng, 1))
        self._record(tok, reads, writes)
        return tok

    def dma(self, eng, fn, n, key, reads=(), writes=()):
        self._sync(eng, reads, writes)
        self.cnt[key] = self.cnt.get(key, 0) + 16 * n
        tok = (key, self.cnt[key])
        self.lists[eng].append(("dma", fn, key, 16))
        self._record(tok, reads, writes)
        return tok

    def fence(self, eng, resources):
        for r in resources:
            t = self.lastw.get(r)
            if t is not None:
                self._need(eng, t)
            for t in self.readers.get(r, ()):
                self._need(eng, t)

    def final_wait(self, eng, keys):
        for k in keys:
            if self.cnt.get(k, 0) > 0:
                self._need(eng, (k, self.cnt[k]))


def build_program(NSEQ, S, with_sample=True):
    assert S % 512 == 0
    nc = bass.Bass("TRN2", target_bir_lowering=False)
    NSUB = S // 128
    NROPE = NSUB + 1

    def din(name, shape):
        return nc.dram_tensor(name, list(shape), F32, kind="ExternalInput").ap()

    def dout(name, shape):
        return nc.dram_tensor(name, list(shape), F32, kind="ExternalOutput").ap()

    xp = din("xp", (NSEQ, S, D))
    xs = din("xs", (2, 16, D))
    cak = din("cak", (2, 512, 512)); cav = din("cav", (2, 512, 512))
    cbk = din("cbk", (2, 1024, 512)); cbv = din("cbv", (2, 1024, 512))
    w_in = din("w_in", (D, 3072)); w_gate = din("w_gate", (D, 2048))
    w_pa = din("w_pa", (512, D)); w_pb = din("w_pb", (512, D)); w_out = din("w_out", (D, D))
    w_ff1 = din("w_ff1", (D, 4096)); w_ff2 = din("w_ff2", (4096, D))
    d_g1T = din("g1T", (128, 8)); d_g2T = din("g2T", (128, 8)); d_bgT = din("bgT", (128, 16))
    d_subT = din("subT", (128, 1))
    d_normv = din("normv", (128, 4, 64))
    d_lamv = din("lamv", (128, 4, 64))
    d_cos = din("ropec", (128, NROPE, 32)); d_sin = din("ropes", (128, NROPE, 32))
    d_biasA = din("biasA", (128, 8, 5, 128)); d_biasS = din("biasS", (128, 8, 5, 16))
    d_ident = din("ident", (128, 128))

    yp = dout("yp", (NSEQ, S, D)); ys = dout("ys", (2, 16, D))
    akp = dout("akp", (NSEQ, 512, 512)); avp = dout("avp", (NSEQ, 512, 512))
    bkp = dout("bkp", (NSEQ, S, 512)); bvp = dout("bvp", (NSEQ, S, 512))
    aks = dout("aks", (2, 16, 512)); avs = dout("avs", (2, 16, 512))
    bks = dout("bks", (2, 16, 512)); bvs = dout("bvs", (2, 16, 512))

    S_ = Sched()
    SK = max(S, 1152)
    NKT = SK // 128

    import contextlib
    with contextlib.ExitStack() as es:
        def sb(name, shape, dt):
            return es.enter_context(nc.sbuf_tensor("sb_" + name, list(shape), dt))

        g1T = sb("g1T", (128, 8), F32); g2T = sb("g2T", (128, 8), F32); bgT = sb("bgT", (128, 16), F32)
        subT = sb("subT", (128, 1), F32); subc = sb("subc", (128, 1), F32)
        normv = sb("normv", (128, 4, 64), F32)
        lamv = sb("lamv", (128, 4, 64), F32); lamt = sb("lamt", (128, 8), F32); lamj = sb("lamj", (128, 2, 64), F32)
        cosT = sb("cosT", (128, NROPE, 32), F32); sinT = sb("sinT", (128, NROPE, 32), F32)
        biasA = sb("biasA", (128, 8, 5, 128), BF16); biasS = sb("biasS", (128, 8, 5, 16), BF16)
        identf = sb("identf", (128, 128), F32); ident = sb("ident", (128, 128), BF16)
        ones_b = sb("ones_b", (128, 128), BF16); ones_f = sb("ones_f", (128, 128), F32)
        epsT = sb("epsT", (128, 1), F32)
        kaT = sb("kaT", (128, 4, 1024), BF16)
        va = sb("va", (128, 8, 512), BF16)
        kbT = sb("kbT", (128, 4, SK), BF16)
        vb = sb("vb", (128, NKT, 512), BF16)
        wring = [sb(f"w{i}", (128, 8, 512), BF16) for i in range(NSLOT)]
        xt = [sb(f"xt{i}", (128, 4, D), F32) for i in range(1)]
        hbs = [sb(f"hb{i}", (128, D), BF16) for i in range(2)]
        hT = sb("hT", (128, 8, 512), BF16)
        qT = sb("qT", (128, 8, 512), BF16)
        mT = qT
        yaT = sb("yaT", (128, 4, 512), BF16); ybT = sb("ybT", (128, 4, 512), BF16)
        qaT = qT[:, 0:4, :]; qbT = qT[:, 4:8, :]
        stat = sb("stat", (128, 64), F32)
        NFB = 10
        fb = [sb(f"fb{i}", (128, 512), F32) for i in range(NFB)]
        junk = sb("junk", (128, 1024), BF16)
        xstage = [sb(f"xstage{i}", (128, D), F32) for i in range(2)]
        hstat = [sb(f"hstat{i}", (128, 24), F32) for i in range(4)]
        rsA = sb("rsA", (128, 8), F32)
        uT = sb("uT", (128, 32, 512), BF16)
        cst = uT[:, 0:8, :]
        ya = uT[:, 21, :]
        pA = [uT[:, 22 + 2 * i:24 + 2 * i, :].rearrange("p a b -> p (a b)")[:, 0:640].rearrange("p (j q) -> p j q", j=5) for i in range(2)]
        pB = [uT[:, 18 + i, :] for i in range(3)]
        qaz = [qT[:, 0:4, :], qT[:, 4:8, :]]
        qbz = [uT[:, 8:12, :], uT[:, 26:30, :]]
        NZB = 6
        zb = [uT[:, 12 + i, :] for i in range(NZB)]

        print("SBUF bytes remaining per partition:", nc.sbuf_bytes_remaining)
        P = es.enter_context(nc.psum_tensor("P", [128, 6, 512], F32))
        PT = es.enter_context(nc.psum_tensor("PT", [128, 2, 1024], BF16))

        PTf = [PT[:, 0, :].bitcast(F32), PT[:, 1, :].bitcast(F32)]
        SB = [P[:, 4, :], P[:, 5, :], PTf[0], PTf[1]]
        SBR = [("ps", 4), ("ps", 5), ("pt", 0), ("pt", 1)]
        BDEPTH = 3
        ctr = {"xs": 0, "fb": 0, "zb": 0, "hstat": 0, "pt": 0, "ps": 0, "w": 0, "pB": 0, "hb": 0}

        def nxt(k, n):
            v = ctr[k] % n
            ctr[k] += 1
            return v

        def load_const(dst, src, name):
            S_.dma("sp", lambda e: [e.dma_start(out=dst, in_=src)], 1, "c_" + name, writes=[name])

        load_const(g1T[:], d_g1T[:, :], "g1T"); load_const(g2T[:], d_g2T[:, :], "g2T")
        load_const(bgT[:], d_bgT[:, :], "bgT"); load_const(subT[:], d_subT[:, :], "subT")
        load_const(normv[:], d_normv[:, :, :], "normv"); load_const(lamv[:], d_lamv[:, :, :], "lamv")
        load_const(cosT[:], d_cos[:, :, :], "cosT"); load_const(sinT[:], d_sin[:, :, :], "sinT")
        S_.dma("pool", lambda e: [e.dma_start(out=biasA[:], in_=d_biasA[:, :, :, :])], 1, "c_biasA", writes=["biasA"])
        S_.dma("pool", lambda e: [e.dma_start(out=biasS[:], in_=d_biasS[:, :, :, :])], 1, "c_biasS", writes=["biasS"])
        load_const(identf[:], d_ident[:, :], "identf")
        S_.op("dve", lambda e: e.tensor_copy(out=ident[:], in_=identf[:]), reads=["identf"], writes=["ident"])
        S_.op("dve", lambda e: e.tensor_scalar(out=biasA[:], in0=biasA[:], scalar1=8.0, scalar2=None, op0=ALU.mult), reads=["biasA"], writes=["biasA"])
        S_.op("dve", lambda e: e.tensor_scalar(out=biasS[:], in0=biasS[:], scalar1=8.0, scalar2=None, op0=ALU.mult), reads=["biasS"], writes=["biasS"])
        S_.op("pool", lambda e: e.memset(ones_b[:], 1.0), writes=["ones_b"])
        S_.op("pool", lambda e: e.memset(ones_f[:], 1.0), writes=["ones_f"])
        S_.op("pool", lambda e: e.memset(epsT[:], EPS), writes=["epsT"])
        S_.op("dve", lambda e: e.tensor_tensor(out=lamj[:, 0, :], in0=lamv[:, 0, :], in1=lamv[:, 1, :], op=ALU.mult),
              reads=["lamv"], writes=["lamj0"])
        S_.op("dve", lambda e: e.tensor_tensor(out=lamj[:, 1, :], in0=lamv[:, 2, :], in1=lamv[:, 3, :], op=ALU.mult),
              reads=["lamv"], writes=["lamj1"])
        S_.op("dve", lambda e: e.reduce_sum(out=lamt[:, 0:2], in_=lamj[:, :, :], axis=AX.X),
              reads=["lamj0", "lamj1"], writes=["lamt01"])
        S_.op("act", lambda e: e.activation(out=lamt[:, 2:4], in_=lamt[:, 0:2], func=AF.Exp),
              reads=["lamt01"], writes=["lamt23"])
        S_.op("dve", lambda e: e.tensor_tensor(out=lamt[:, 5:6], in0=lamt[:, 3:4], in1=lamt[:, 2:3], op=ALU.subtract),
              reads=["lamt23"], writes=["lamt5"])
        S_.op("dve", lambda e: e.tensor_scalar(out=lamt[:, 4:5], in0=lamt[:, 5:6], scalar1=-LAM_INIT, scalar2=None, op0=ALU.add),
              reads=["lamt5"], writes=["neglam"])
        S_.op("dve", lambda e: e.tensor_scalar(out=subc[:], in0=subT[:], scalar1=1.0 - LAM_INIT, scalar2=None, op0=ALU.mult),
              reads=["subT"], writes=["subc"])
        neglam = lamt[:, 4:5]

        NPIECE = 30
        wscr = nc.dram_tensor("wscr", [NPIECE, 128, 4096], BF16, kind="Internal").ap()
        pieces = []
        for g in range(6):
            pieces.append([(w_in[:, g * 512:(g + 1) * 512], 8, 0)])
        for half in range(2):
            pieces.append([(w_gate[:, half * 512:(half + 1) * 512], 8, 0)])
            pieces.append([(w_gate[:, 1024 + half * 512:1024 + (half + 1) * 512], 8, 0)])
            pieces.append([(w_pa[:, half * 512:(half + 1) * 512], 4, 0), (w_pb[:, half * 512:(half + 1) * 512], 4, 4)])
        for g in range(2):
            pieces.append([(w_out[:, g * 512:(g + 1) * 512], 8, 0)])
        for i in range(8):
            pieces.append([(w_ff1[:, i * 512:(i + 1) * 512], 8, 0)])
        for g in range(2):
            for i in range(4):
                pieces.append([(w_ff2[i * 1024:(i + 1) * 1024, g * 512:(g + 1) * 512], 8, 0)])
        assert len(pieces) == NPIECE
        for k, parts in enumerate(pieces):
            def cv(e, k=k, parts=parts):
                out = []
                for (src, kc, k0) in parts:
                    dst = wscr[k].rearrange("p (k n) -> p k n", k=8)[:, k0:k0 + kc, :]
                    out.append(e.dma_start(out=dst, in_=src.rearrange("(k p) n -> p k n", p=128)))
                return out
            S_.dma("pool", cv, len(parts), f"cv{k}", writes=[("wscr", k)])

        wstate = {"emitted": 0, "total": 0, "released": -1, "cur": -1}
        PF = 2

        def wpump():
            while (wstate["emitted"] <= min(wstate["cur"] + PF, wstate["total"] - 1)
                   and wstate["emitted"] - NSLOT <= wstate["released"]):
                m = wstate["emitted"]
                k = m % NPIECE
                i = m % NSLOT
                S_.dma("sp", lambda e, i=i, k=k: [e.dma_start(out=wring[i][:].rearrange("p a b -> p (a b)"), in_=wscr[k])],
                       1, f"w{i}", reads=[("wscr", k)], writes=[("w", i)])
                wstate["emitted"] += 1

        def wget(n):
            wstate["cur"] = max(wstate["cur"], n)
            wpump()
            assert wstate["emitted"] > n, (n, wstate)
            i = n % NSLOT
            return i, wring[i]

        def wdone(n):
            wstate["released"] = max(wstate["released"], n)
            wpump()

        def rms_rows(xin_ap, np_, width, stat_ap, rd, wr_name):
            S_.op("act", lambda e: e.activation(out=junk[0:np_, 0:width], in_=xin_ap, func=AF.Square,
                                                scale=1.0 / math.sqrt(width), accum_out=stat_ap),
                  reads=rd, writes=[wr_name + "_ms", "junk"])
            S_.op("act", lambda e: e.activation(out=stat_ap, in_=stat_ap, func=AF.Sqrt, bias=epsT[0:np_, 0:1], scale=1.0),
                  reads=[wr_name + "_ms", "epsT"], writes=[wr_name + "_sd"])
            S_.op("dve", lambda e: e.reciprocal(out=stat_ap, in_=stat_ap), reads=[wr_name + "_sd"], writes=[wr_name])

        def to_fm(src_tm, np_, nch, gT, dst_fn, rd, wr, evac="act", split=None):
            b = nxt("pt", 2)

            def tr(e):
                ins = None
                for c in range(nch):
                    ins = e.transpose(out=PT[:, b, c * 128:c * 128 + np_], in_=src_tm[:, c * 128:(c + 1) * 128],
                                      identity=ident[0:np_, 0:np_])
                return ins
            S_.op("pe", tr, reads=rd + ["ident"], writes=[("pt", b)])
            src = PT[:, b, 0:nch * 128].rearrange("p (c t) -> p c t", c=nch)[:, :, 0:np_]
            if split is not None:
                dA, dB = split
                S_.op("act", lambda e: e.activation(out=dA()[0:64], in_=src[0:64], func=AF.Copy), reads=[("pt", b)], writes=[wr[0]])
                S_.op("dve", lambda e: e.tensor_copy(out=dB()[64:128], in_=src[64:128]), reads=[("pt", b)], writes=[(wr[0][0] + "_hi", wr[0][1])])
            elif gT is None and evac == "dve":
                S_.op("dve", lambda e: e.tensor_copy(out=dst_fn(), in_=src), reads=[("pt", b)], writes=wr)
            elif gT is None:
                S_.op("act", lambda e: e.activation(out=dst_fn(), in_=src, func=AF.Copy), reads=[("pt", b)], writes=wr)
            else:
                S_.op("dve", lambda e: e.tensor_tensor(out=dst_fn(), in0=src,
                                                       in1=gT.unsqueeze(2).broadcast_to([128, nch, np_]), op=ALU.mult),
                      reads=[("pt", b), "g1T", "g2T"], writes=wr)

        def pipeline(items, PIPE=True, depth=1):
            n = len(items)
            if not PIPE:
                depth = 0
            for i in range(min(depth, n)):
                items[i][0]()
            for i in range(n):
                if i + depth < n:
                    items[i + depth][0]()
                items[i][1]()
                items[i][2]()
                for fn in items[i][3]:
                    fn()

        class Tile:
            pass

        def make_tile(kind, seq, t0, subs, tidx):
            T = Tile()
            wb0 = tidx * NPIECE
            PL = "pool" if tidx > 0 else "dve"
            nsub = len(subs)
            np_ = subs[0]
            NT = nsub * np_
            X = xt[0]
            col = [i * np_ for i in range(nsub)]

            def load_xs(s):
                i = nxt("xs", 2)
                if kind == "p":
                    src = xp[seq, t0 + s * 128:t0 + (s + 1) * 128, :]
                else:
                    src = xs[s, :, :]
                S_.dma("sp", lambda e: [e.dma_start(out=xstage[i][0:np_, :], in_=src)], 1, f"xs{i}", writes=[("xs", i)])
                return i

            hb_of = {}

            def p1a_A(s):
                xi = load_xs(s)
                st = stat[0:np_, s:s + 1]
                rms_rows(xstage[xi][0:np_, :], np_, D, st, [("xs", xi)], f"rs1_{s}")
                hi = nxt("hb", 2)
                hb_of[s] = hi
                S_.op("dve", lambda e, st=st, hi=hi, xi=xi: e.tensor_scalar(out=hbs[hi][0:np_, :], in0=xstage[xi][0:np_, :], scalar1=st,
                                                                          scalar2=None, op0=ALU.mult),
                      reads=[("xs", xi), f"rs1_{s}"], writes=[("hb", hi)])

            def p1a_B(s):
                hi = hb_of[s]
                to_fm(hbs[hi][0:np_, :], np_, 8, g1T[:, :], lambda s=s: hT[:, :, col[s]:col[s] + np_],
                      [("hb", hi)], [("hT", s)])

            def p1a():
                for s in range(nsub):
                    p1a_A(s)
                    p1a_B(s)
            T.p1a = p1a
            T.p1a_A = p1a_A
            T.p1a_B = p1a_B
            T.nsub = nsub

            def make_item(g, s, grp):
                I = Tile()
                stt = {}
                if kind == "p":
                    ktg = t0 // 128 + s
                    slotA = ktg % 8
                    kcolB = ktg * 128
                    ktB = ktg
                    pos_idx = ktg
                else:
                    slotA = 4; kcolB = 1024; ktB = 8; pos_idx = NSUB
                isnorm = g in (0, 1, 3, 4)
                nvec = {0: 0, 1: 1, 3: 2, 4: 3}.get(g)
                rope_idx = pos_idx if g >= 3 else None

                def A():
                    if s == 0:
                        grp["w"] = wget(wb0 + g)
                    wi, W = grp["w"]
                    b = nxt("ps", 6)

                    def mm(e):
                        ins = None
                        for kc in range(8):
                            ins = e.matmul(P[0:np_, b, :], lhsT=hT[:, kc, col[s]:col[s] + np_], rhs=W[:, kc, :],
                                           start=(kc == 0), stop=(kc == 7))
                        return ins
                    S_.op("pe", mm, reads=[("hT", s), ("w", wi)], writes=[("ps", b)])
                    if s == nsub - 1:
                        wdone(wb0 + g)
                    z = nxt("fb", NFB)
                    stt["z"] = z
                    S_.op("act", lambda e: e.activation(out=fb[z][0:np_, :], in_=P[0:np_, b, :], func=AF.Copy),
                          reads=[("ps", b)], writes=[("fb", z)])
                    if isnorm:
                        j = nxt("fb", NFB)
                        hs = nxt("hstat", 4)
                        stt["hs"] = hs
                        S_.op("act", lambda e: e.activation(out=fb[j][0:np_, :], in_=P[0:np_, b, :], func=AF.Square),
                              reads=[("ps", b)], writes=[("fb", j)])
                        S_.op("dve", lambda e: e.reduce_sum(out=hstat[hs][0:np_, 0:8],
                                                            in_=fb[j][0:np_, :].rearrange("p (h d) -> p h d", h=8), axis=AX.X),
                              reads=[("fb", j)], writes=[("hstat", hs)])
                I.A = A

                def B():
                    if not isnorm:
                        return
                    z, hs = stt["z"], stt["hs"]
                    Z = fb[z]
                    S_.op("act", lambda e: e.activation(out=hstat[hs][0:np_, 8:16], in_=hstat[hs][0:np_, 0:8], func=AF.Sqrt,
                                                        bias=epsT[0:np_, 0:1], scale=1.0 / 64),
                          reads=[("hstat", hs), "epsT"], writes=[("hstat_b", hs)])
                    S_.op("dve", lambda e: e.reciprocal(out=hstat[hs][0:np_, 16:24], in_=hstat[hs][0:np_, 8:16]),
                          reads=[("hstat_b", hs)], writes=[("hstat_c", hs)])
                    S_.op("dve", lambda e: e.tensor_tensor(out=Z[0:np_, :].rearrange("p (h d) -> p h d", h=8),
                                                           in0=Z[0:np_, :].rearrange("p (h d) -> p h d", h=8),
                                                           in1=hstat[hs][0:np_, 16:24].unsqueeze(2).broadcast_to([np_, 8, 64]),
                                                           op=ALU.mult),
                          reads=[("hstat_c", hs), ("fb", z)], writes=[("fb", z)])
                    S_.op(PL, lambda e: e.tensor_tensor(out=Z[0:np_, :].rearrange("p (h d) -> p h d", h=8),
                                                            in0=Z[0:np_, :].rearrange("p (h d) -> p h d", h=8),
                                                            in1=normv[0:np_, nvec, :].unsqueeze(1).broadcast_to([np_, 8, 64]), op=ALU.mult),
                          reads=[("fb", z), "normv"], writes=[("fb", z)])
                    if rope_idx is None:
                        stt["Rf"], stt["rres"] = Z, ("fb", z)
                        return
                    r = nxt("fb", NFB)
                    R = fb[r]
                    Zv = Z[0:np_, :].rearrange("p (h t d) -> p h t d", h=8, t=2)
                    Rv = R[0:np_, :].rearrange("p (h t d) -> p h t d", h=8, t=2)
                    sn = sinT[0:np_, rope_idx, :].unsqueeze(1).broadcast_to([np_, 8, 32])
                    j2 = nxt("fb", NFB)
                    Jv = fb[j2][0:np_, :].rearrange("p (h t d) -> p h t d", h=8, t=2)
                    S_.op("dve", lambda e: e.tensor_tensor(out=R[0:np_, :].rearrange("p (g d) -> p g d", g=16),
                                                           in0=Z[0:np_, :].rearrange("p (g d) -> p g d", g=16),
                                                           in1=cosT[0:np_, rope_idx, :].unsqueeze(1).broadcast_to([np_, 16, 32]), op=ALU.mult),
                          reads=[("fb", z), "cosT"], writes=[("fb", r)])
                    S_.op(PL, lambda e: e.tensor_tensor(out=Jv[:, :, 0, :], in0=Zv[:, :, 1, :], in1=sn, op=ALU.mult),
                          reads=[("fb", z), "sinT"], writes=[("fb", j2)])
                    S_.op(PL, lambda e: e.tensor_tensor(out=Jv[:, :, 1, :], in0=Zv[:, :, 0, :], in1=sn, op=ALU.mult),
                          reads=[("fb", z), "sinT"], writes=[("fb", j2, 1)])
                    S_.op("dve", lambda e: e.tensor_tensor(out=Rv[:, :, 0, :], in0=Rv[:, :, 0, :], in1=Jv[:, :, 0, :], op=ALU.subtract),
                          reads=[("fb", r), ("fb", j2)], writes=[("fb", r)])
                    S_.op("dve", lambda e: e.tensor_tensor(out=Rv[:, :, 1, :], in0=Rv[:, :, 1, :], in1=Jv[:, :, 1, :], op=ALU.add),
                          reads=[("fb", r), ("fb", j2, 1)], writes=[("fb", r)])
                    S_.lastw[("fb", j2)] = S_.lastw[("fb", r)]
                    S_.readers[("fb", j2)] = []
                    stt["Rf"], stt["rres"] = R, ("fb", r)
                I.B = B

                def C(pend):
                    z = stt["z"]
                    if isnorm:
                        Rf, rres = stt["Rf"], stt["rres"]
                        okey = f"o_fb{rres[1]}"
                        if g == 1:
                            if kind == "p" and t0 >= S - 512:
                                r0 = t0 - (S - 512) + s * 128
                                S_.dma("sp", lambda e: [e.dma_start(out=akp[seq, r0:r0 + 128, :], in_=Rf[0:np_, :])], 1, okey, reads=[rres])
                            elif kind == "s":
                                S_.dma("sp", lambda e: [e.dma_start(out=aks[s, :, :], in_=Rf[0:np_, :])], 1, okey, reads=[rres])
                        if g == 4:
                            if kind == "p":
                                S_.dma("sp", lambda e: [e.dma_start(out=bkp[seq, t0 + s * 128:t0 + (s + 1) * 128, :], in_=Rf[0:np_, :])],
                                       1, okey, reads=[rres])
                            else:
                                S_.dma("sp", lambda e: [e.dma_start(out=bks[s, :, :], in_=Rf[0:np_, :])], 1, okey, reads=[rres])
                        zi = nxt("zb", NZB)
                        S_.op("act", lambda e: e.activation(out=zb[zi][0:np_, :], in_=Rf[0:np_, :], func=AF.Copy),
                              reads=[rres], writes=[("zb", zi)])
                        if g == 0:
                            dst = lambda: qaT[:, :, col[s]:col[s] + np_]
                            wr = [("qaT", s)]
                        elif g == 3:
                            dst = lambda: qbT[:, :, col[s]:col[s] + np_]
                            wr = [("qbT", s)]
                        elif g == 1:
                            if kind == "p":
                                dst = lambda: kaT[:, :, slotA * 128:slotA * 128 + np_]
                                wr = [("kaT", slotA)]
                            else:
                                dst = lambda: qaT[:, :, 32 + col[s]:32 + col[s] + np_]
                                wr = [("kaS", s)]
                        else:
                            if kind == "p":
                                dst = lambda: kbT[:, :, kcolB:kcolB + np_]
                                wr = [("kbT", ktB)]
                            else:
                                dst = lambda: qbT[:, :, 32 + col[s]:32 + col[s] + np_]
                                wr = [("kbS", s)]
                        if g == 0:
                            pend.append(lambda: to_fm(zb[zi][0:np_, :], np_, 4, None, None, [("zb", zi)], wr,
                                                      split=(lambda: qaz[0][:, :, col[s]:col[s] + np_], lambda: qaz[1][:, :, col[s]:col[s] + np_])))
                        elif g == 3:
                            pend.append(lambda: to_fm(zb[zi][0:np_, :], np_, 4, None, None, [("zb", zi)], wr,
                                                      split=(lambda: qbz[0][:, :, col[s]:col[s] + np_], lambda: qbz[1][:, :, col[s]:col[s] + np_])))
                        else:
                            pend.append(lambda: to_fm(zb[zi][0:np_, :], np_, 4, None, dst, [("zb", zi)], wr, evac=("dve" if (g + s) % 2 else "act")))
                    elif g == 2:
                        if kind == "p" and t0 >= S - 512:
                            r0 = t0 - (S - 512) + s * 128
                            S_.dma("sp", lambda e: [e.dma_start(out=avp[seq, r0:r0 + 128, :], in_=fb[z][0:np_, :])], 1, f"o_fb{z}", reads=[("fb", z)])
                        elif kind == "s":
                            S_.dma("sp", lambda e: [e.dma_start(out=avs[s, :, :], in_=fb[z][0:np_, :])], 1, f"o_fb{z}", reads=[("fb", z)])
                        if kind == "p":
                            S_.op(PL, lambda e: e.tensor_copy(out=va[0:np_, slotA, :], in_=fb[z][0:np_, :]), reads=[("fb", z)], writes=[("va", slotA)])
                        else:
                            S_.op(PL, lambda e: e.tensor_copy(out=cst[0:np_, 6 + s, :], in_=fb[z][0:np_, :]), reads=[("fb", z)], writes=[("vaS", s)])
                    else:
                        if kind == "p":
                            S_.dma("sp", lambda e: [e.dma_start(out=bvp[seq, t0 + s * 128:t0 + (s + 1) * 128, :], in_=fb[z][0:np_, :])],
                                   1, f"o_fb{z}", reads=[("fb", z)])
                            S_.op(PL, lambda e: e.tensor_copy(out=vb[0:np_, ktB, :], in_=fb[z][0:np_, :]), reads=[("fb", z)], writes=[("vb", ktB)])
                        else:
                            S_.dma("sp", lambda e: [e.dma_start(out=bvs[s, :, :], in_=fb[z][0:np_, :])], 1, f"o_fb{z}", reads=[("fb", z)])
                            S_.op(PL, lambda e: e.tensor_copy(out=cst[0:np_, 4 + s, :], in_=fb[z][0:np_, :]), reads=[("fb", z)], writes=[("vbS", s)])
                I.C = C
                return I

            def p1b():
                S_.op(PL, lambda e: e.memset(qaz[0][64:128], 0.0), writes=[("mT", m2) for m2 in range(4)] + ["qzero0"])
                S_.op(PL, lambda e: e.memset(qaz[1][0:64], 0.0), writes=[("mT", m2) for m2 in range(4, 8)] + ["qzero1"])
                S_.op(PL, lambda e: e.memset(qbz[0][64:128], 0.0), writes=[("uT", j2) for j2 in range(8, 12)] + ["qzero2"])
                S_.op(PL, lambda e: e.memset(qbz[1][0:64], 0.0), writes=[("uT", j2) for j2 in range(26, 30)] + ["qzero3"])
                items = []
                for g in range(6):
                    grp = {}
                    for s in range(nsub):
                        items.append(make_item(g, s, grp))
                n = len(items)
                pend = []
                for i in range(n + 2):
                    if i < n:
                        items[i].A()
                    if 0 <= i - 1 < n:
                        items[i - 1].B()
                    if 0 <= i - 2 < n:
                        items[i - 2].C(pend)
                    while len(pend) > 4:
                        pend.pop(0)()
                while pend:
                    pend.pop(0)()
            T.p1b = p1b

            def attn_a():
                nq = np_
                pso, pss = 4, 5
                pend = []

                def sub_items(s):
                    if kind == "p":
                        qt = t0 // 128 + s
                        kts = [k for k in range(qt - 4, qt + 1) if k >= 0]
                        jlist = [k - (qt - 4) for k in kts]
                        ktinfo = [(k % 8, 128) for k in kts]
                        btab = biasA
                        hist_rd = [("kaT", k % 8) for k in kts]
                        hist_rv = [("va", k % 8) for k in kts]
                    else:
                        jlist = [0, 1, 2, 3, 4]
                        ktinfo = [(0, 128), (1, 128), (2, 128), (3, 128), (4, 16)]
                        btab = biasS
                        hist_rd = [("kaT", k) for k in range(5)]
                        hist_rv = [("va", k) for k in range(5)]
                    c0 = col[s]
                    its = []
                    for h in range(8):
                        c, po = h // 2, (h % 2) * 64
                        ab = h % 2
                        bA, bB = (0, 1) if ab == 0 else (2, 3)

                        def s1(c=c, po=po, bA=bA, bB=bB, h=h):
                            def qk(e):
                                ins = None
                                for (j, (slot, nk)) in zip(jlist, ktinfo):
                                    bank, cc = (bA, j * 128) if j < 4 else (bB, 0)
                                    e.matmul(P[0:nk, bank, cc:cc + nq], lhsT=kaT[:, c, slot * 128:slot * 128 + nk],
                                             rhs=qaz[h % 2][:, c, c0:c0 + nq], start=True, stop=False)
                                    ins = e.matmul(P[0:nk, bank, cc:cc + nq], lhsT=ident[0:nk, 0:nk], rhs=btab[0:nk, h, j, 0:nq],
                                                   start=False, stop=True)
                                return ins
                            S_.op("pe", qk, reads=hist_rd + [("qaT", s), ("qaT_hi", s), "qzero0", "qzero1", "ident", "biasA", "biasS"], writes=[("ps", bA), ("ps", bB)])

                        def s2(bA=bA, bB=bB, ab=ab):
                            js = [j for j in jlist if j < 4]
                            if js:
                                jm = min(js)
                                S_.op("act", lambda e: e.activation(out=pA[ab][:, jm:4, 0:nq],
                                                                    in_=P[:, bA, :].rearrange("p (j q) -> p j q", j=4)[:, jm:4, 0:nq],
                                                                    func=AF.Exp, scale=0.125),
                                      reads=[("ps", bA)], writes=[("pA", ab, 0)])
                            nk4 = ktinfo[-1][1]
                            S_.op("act", lambda e: e.activation(out=pA[ab][0:nk4, 4, 0:nq], in_=P[0:nk4, bB, 0:nq], func=AF.Exp, scale=0.125),
                                  reads=[("ps", bB)], writes=[("pA", ab, 1)])

                        def s3(h=h, ab=ab):
                            def pv(e):
                                ins = None
                                n = len(jlist)
                                for i, (j, (slot, nk)) in enumerate(zip(jlist, ktinfo)):
                                    e.matmul(P[0:nq, pso, h * 64:(h + 1) * 64], lhsT=pA[ab][0:nk, j, 0:nq], rhs=va[0:nk, slot, h * 64:(h + 1) * 64],
                                             start=(i == 0), stop=(i == n - 1))
                                    ins = e.matmul(P[0:nq, pss, 2 * h:2 * h + 2], lhsT=pA[ab][0:nk, j, 0:nq], rhs=ones_b[0:nk, 0:2],
                                                   start=(i == 0), stop=(i == n - 1))
                                return ins
                            S_.op("pe", pv, reads=[("pA", ab, 0), ("pA", ab, 1)] + hist_rv + ["ones_b"], writes=[("ps", pso), ("ps", pss)])
                        post = []
                        if h == 1:
                            post.append(lambda: pend.pop()() if pend else None)
                        if h == 7:
                            def fin(s=s, c0=c0):
                                S_.op("dve", lambda e: e.reciprocal(out=rsA[0:nq, :], in_=P[0:nq, pss, 0:16].rearrange("p (h t) -> p h t", t=2)[:, :, 0]),
                                      reads=[("ps", pss)], writes=["rsA"])
                                S_.op("dve", lambda e: e.tensor_tensor(out=ya[0:nq, :].rearrange("p (h d) -> p h d", h=8),
                                                                       in0=P[0:nq, pso, :].rearrange("p (h d) -> p h d", h=8),
                                                                       in1=rsA[0:nq, :].unsqueeze(2).broadcast_to([nq, 8, 64]), op=ALU.mult),
                                      reads=[("ps", pso), "rsA"], writes=["ya"])
                                pend.append(lambda: to_fm(ya[0:nq, :], nq, 4, None, lambda: yaT[:, :, c0:c0 + nq], ["ya"], [("yaT", s)]))
                            post.append(fin)
                        its.append((s1, s2, s3, post))
                    return its

                if kind == "p":
                    allit = []
                    for s in range(nsub):
                        allit += sub_items(s)
                    pipeline(allit, PIPE_A)
                    while pend:
                        pend.pop()()
                else:
                    for s in range(nsub):
                        S_.dma("pool", lambda e, s=s: [e.dma_start(out=cst[:, 0:4, :], in_=cak[s].rearrange("(k p) n -> p k n", p=128))],
                               1, "cstA", writes=[("cst", k) for k in range(4)])
                        for k in range(4):
                            to_fm(cst[:, k, :], 128, 4, None, lambda k=k: kaT[:, :, k * 128:(k + 1) * 128], [("cst", k)], [("kaT", k)])
                        S_.dma("pool", lambda e, s=s: [e.dma_start(out=va[:, 0:4, :], in_=cav[s].rearrange("(k p) n -> p k n", p=128))],
                               1, "vaL", writes=[("va", k) for k in range(4)])
                        S_.op("act", lambda e, s=s: e.activation(out=kaT[:, :, 512:528], in_=qaT[:, :, 32 + col[s]:32 + col[s] + 16], func=AF.Copy),
                              reads=[("kaS", s)], writes=[("kaT", 4)])
                        S_.op(PL, lambda e, s=s: e.tensor_copy(out=va[0:16, 4, :], in_=cst[0:16, 6 + s, :]),
                              reads=[("vaS", s)], writes=[("va", 4)])
                        pipeline(sub_items(s), PIPE_A)
                        while pend:
                            pend.pop()()
            T.attn_a = attn_a

            def attn_b():
                if kind == "p":
                    groups = [(0, NT, list(range(0, (t0 + 512) // 128)), None)]
                else:
                    groups = [(col[s], 16, list(range(9)), s) for s in range(nsub)]
                pend2 = []

                def make_comb1(h, qc0, Nq):
                    def comb1(h=h, qc0=qc0, Nq=Nq):
                        f0, f1, f2, f3, f4 = [nxt("fb", NFB) for _ in range(5)]
                        S_.op("dve", lambda e: e.reciprocal(out=fb[f0][:, 0:Nq], in_=P[:, 1, 0:Nq]), reads=[("ps", 1)], writes=[("fb", f0)])
                        S_.op("dve", lambda e: e.tensor_tensor(out=fb[f1][:, 0:Nq], in0=P[:, 0, 0:Nq], in1=fb[f0][:, 0:Nq], op=ALU.mult),
                              reads=[("ps", 0), ("fb", f0)], writes=[("fb", f1)])
                        S_.op("dve", lambda e: e.reciprocal(out=fb[f2][:, 0:Nq], in_=P[:, 3, 0:Nq]), reads=[("ps", 3)], writes=[("fb", f2)])
                        S_.op("dve", lambda e: e.tensor_tensor(out=fb[f3][:, 0:Nq], in0=P[:, 2, 0:Nq], in1=fb[f2][:, 0:Nq], op=ALU.mult),
                              reads=[("ps", 2), ("fb", f2)], writes=[("fb", f3)])
                        S_.op("dve", lambda e: e.scalar_tensor_tensor(out=fb[f4][:, 0:Nq], in0=fb[f3][:, 0:Nq], scalar=neglam,
                                                                      in1=fb[f1][:, 0:Nq], op0=ALU.mult, op1=ALU.add),
                              reads=[("fb", f3), ("fb", f1), "neglam"], writes=[("fb", f4)])

                        def comb2(nb):
                            S_.op("act", lambda e: e.activation(out=fb[f0][:, 0:Nq], in_=fb[f4][:, 0:Nq], func=AF.Square),
                                  reads=[("fb", f4)], writes=[("fb", f0)])
                            S_.op("pe", lambda e: e.matmul(SB[nb][:, 0:Nq], lhsT=ones_f[:, :], rhs=fb[f0][:, 0:Nq], start=True, stop=True),
                                  reads=[("fb", f0), "ones_f"], writes=[SBR[nb]])
                            S_.op("act", lambda e: e.activation(out=fb[f2][:, 0:Nq], in_=SB[nb][:, 0:Nq], func=AF.Sqrt, bias=epsT[:, 0:1],
                                                                scale=1.0 / 128),
                                  reads=[SBR[nb], "epsT"], writes=[("fb", f2)])
                            S_.op("dve", lambda e: e.reciprocal(out=fb[f3][:, 0:Nq], in_=fb[f2][:, 0:Nq]), reads=[("fb", f2)], writes=[("fb", f3)])
                            S_.op("dve", lambda e: e.scalar_tensor_tensor(out=ybT[:, h, qc0:qc0 + Nq], in0=fb[f4][:, 0:Nq], scalar=subc[:, 0:1],
                                                                          in1=fb[f3][:, 0:Nq], op0=ALU.mult, op1=ALU.mult),
                                  reads=[("fb", f4), ("fb", f3), "subc"], writes=[("ybT", h)])
                        pend2.append(comb2)
                    return comb1

                for (qc0, nqb, ktl, ss) in groups:
                    if kind == "s":
                        S_.dma("pool", lambda e, ss=ss: [e.dma_start(out=cst[:, 0:4, :], in_=cbk[ss, 0:512].rearrange("(k p) n -> p k n", p=128))],
                               1, "cstA", writes=[("cst", k) for k in range(4)])
                        for k in range(4):
                            to_fm(cst[:, k, :], 128, 4, None, lambda k=k: kbT[:, :, k * 128:(k + 1) * 128], [("cst", k)], [("kbT", k)])
                        S_.dma("pool", lambda e, ss=ss: [e.dma_start(out=cst[:, 0:4, :], in_=cbk[ss, 512:1024].rearrange("(k p) n -> p k n", p=128))],
                               1, "cstA", writes=[("cst", k) for k in range(4)])
                        for k in range(4):
                            to_fm(cst[:, k, :], 128, 4, None, lambda k=k: kbT[:, :, (4 + k) * 128:(5 + k) * 128], [("cst", k)], [("kbT", 4 + k)])
                        S_.dma("pool", lambda e, ss=ss: [e.dma_start(out=vb[:, 0:8, :], in_=cbv[ss].rearrange("(k p) n -> p k n", p=128))],
                               1, "vbL", writes=[("vb", k) for k in range(8)])
                        S_.op("act", lambda e, ss=ss: e.activation(out=kbT[:, :, 1024:1040], in_=qbT[:, :, 32 + col[ss]:32 + col[ss] + 16], func=AF.Copy),
                              reads=[("kbS", ss)], writes=[("kbT", 8)])
                        S_.op(PL, lambda e, ss=ss: e.tensor_copy(out=vb[0:16, 8, :], in_=cst[0:16, 4 + ss, :]),
                              reads=[("vbS", ss)], writes=[("vb", 8)])
                    Nq = nqb
                    its = []
                    for h in range(4):
                        for r in range(2):
                            accO, accS = (0, 1) if r == 0 else (2, 3)
                            if kind == "s":
                                idx = len(its)
                                st = {}

                                def s1(h=h, r=r, st=st, qc0=qc0, idx=idx):
                                    sb_ = idx % (BDEPTH + 1)
                                    st["sb"] = sb_

                                    def sc(e):
                                        ins = None
                                        for kt in ktl:
                                            nk = 16 if kt == 8 else 128
                                            ins = e.matmul(SB[sb_][0:nk, kt * 16:(kt + 1) * 16], lhsT=kbT[:, h, kt * 128:kt * 128 + nk],
                                                           rhs=qbz[r][:, h, qc0:qc0 + 16], start=True, stop=True)
                                        return ins
                                    S_.op("pe", sc, reads=[("kbT", kt) for kt in ktl] + ["qzero2", "qzero3"] + [("qbT", s2) for s2 in range(nsub)]
                                          + [("qbT_hi", s2) for s2 in range(nsub)], writes=[SBR[sb_]])

                                def s2(st=st):
                                    st["pi"] = nxt("pB", 3)
                                    pi, sb_ = st["pi"], st["sb"]
                                    S_.op("act", lambda e: e.activation(out=pB[pi][:, 0:128], in_=SB[sb_][:, 0:128], func=AF.Exp, scale=0.125),
                                          reads=[SBR[sb_]], writes=[("pB", pi)])
                                    S_.op("act", lambda e: e.activation(out=pB[pi][0:16, 128:144], in_=SB[sb_][0:16, 128:144], func=AF.Exp, scale=0.125),
                                          reads=[SBR[sb_]], writes=[("pB", pi, 1)])

                                def s3(h=h, st=st, accO=accO, accS=accS):
                                    pi = st["pi"]

                                    def pvb(e):
                                        ins = None
                                        for ki, kt in enumerate(ktl):
                                            nk = 16 if kt == 8 else 128
                                            e.matmul(P[:, accO, 0:16], lhsT=vb[0:nk, kt, h * 128:(h + 1) * 128], rhs=pB[pi][0:nk, kt * 16:(kt + 1) * 16],
                                                     start=(ki == 0), stop=(ki == len(ktl) - 1), skip_group_check=True)
                                            ins = e.matmul(P[:, accS, 0:16], lhsT=ones_b[0:nk, :], rhs=pB[pi][0:nk, kt * 16:(kt + 1) * 16],
                                                           start=(ki == 0), stop=(ki == len(ktl) - 1), skip_group_check=True)
                                        return ins
                                    S_.op("pe", pvb, reads=[("pB", pi), ("pB", pi, 1), "ones_b"] + [("vb", kt) for kt in ktl],
                                          writes=[("ps", accO), ("ps", accS)])
                                post = []
                                if r == 0:
                                    post.append(lambda idx=idx: pend2.pop()(idx % (BDEPTH + 1)) if pend2 else None)
                                if r == 1:
                                    post.append(make_comb1(h, qc0, Nq))
                                its.append((s1, s2, s3, post))
                                continue
                            for ki, kt in enumerate(ktl):
                                nk = 16 if (kind == "s" and kt == 8) else 128
                                if kind == "p":
                                    qlo = max(0, kt * 128 - t0)
                                    diag = kt * 128 >= t0
                                else:
                                    qlo = 0
                                    diag = False
                                N = nqb - qlo
                                first, last = (ki == 0), (ki == len(ktl) - 1)
                                st = {}

                                idx = len(its)

                                def s1(h=h, r=r, kt=kt, nk=nk, qlo=qlo, N=N, st=st, qc0=qc0, idx=idx):
                                    st["sb"] = idx % (BDEPTH + 1)
                                    sb_ = st["sb"]
                                    S_.op("pe", lambda e: e.matmul(
                                        SB[sb_][0:nk, 0:N], lhsT=kbT[:, h, kt * 128:kt * 128 + nk],
                                        rhs=qbz[r][:, h, qc0 + qlo:qc0 + qlo + N], start=True, stop=True),
                                        reads=[("kbT", kt), "qzero2", "qzero3"] + [("qbT", s2) for s2 in range(nsub)] + [("qbT_hi", s2) for s2 in range(nsub)],
                                        writes=[SBR[sb_]])

                                def s2(nk=nk, N=N, st=st, diag=diag):
                                    st["pi"] = nxt("pB", 3)
                                    pi, sb_ = st["pi"], st["sb"]
                                    S_.op("act", lambda e: e.activation(out=pB[pi][0:nk, 0:N], in_=SB[sb_][0:nk, 0:N], func=AF.Exp, scale=0.125),
                                          reads=[SBR[sb_]], writes=[("pB", pi)])
                                    if diag:
                                        S_.op("act", lambda e: e.memzero(pB[pi][64:128, 0:64]), reads=[], writes=[("pB", pi)])

                                def s3(h=h, kt=kt, nk=nk, qlo=qlo, N=N, st=st, first=first, last=last, accO=accO, accS=accS):
                                    pi = st["pi"]

                                    def pvb(e):
                                        e.matmul(P[:, accO, qlo:qlo + N], lhsT=vb[0:nk, kt, h * 128:(h + 1) * 128], rhs=pB[pi][0:nk, 0:N],
                                                 start=first, stop=last, skip_group_check=True)
                                        return e.matmul(P[:, accS, qlo:qlo + N], lhsT=ones_b[0:nk, :], rhs=pB[pi][0:nk, 0:N],
                                                        start=first, stop=last, skip_group_check=True)
                                    S_.op("pe", pvb, reads=[("pB", pi), ("vb", kt), "ones_b"], writes=[("ps", accO), ("ps", accS)])
                                post = []
                                if r == 0 and ki == min(2, len(ktl) - 1):
                                    post.append(lambda idx=idx: pend2.pop()(idx % (BDEPTH + 1)) if pend2 else None)
                                if r == 1 and last:
                                    post.append(make_comb1(h, qc0, Nq))
                                if False:
                                    def comb1(h=h, qc0=qc0, Nq=Nq):
                                        f0, f1, f2, f3, f4 = [nxt("fb", NFB) for _ in range(5)]
                                        S_.op("dve", lambda e: e.reciprocal(out=fb[f0][:, 0:Nq], in_=P[:, 1, 0:Nq]), reads=[("ps", 1)], writes=[("fb", f0)])
                                        S_.op("dve", lambda e: e.tensor_tensor(out=fb[f1][:, 0:Nq], in0=P[:, 0, 0:Nq], in1=fb[f0][:, 0:Nq], op=ALU.mult),
                                              reads=[("ps", 0), ("fb", f0)], writes=[("fb", f1)])
                                        S_.op("dve", lambda e: e.reciprocal(out=fb[f2][:, 0:Nq], in_=P[:, 3, 0:Nq]), reads=[("ps", 3)], writes=[("fb", f2)])
                                        S_.op("dve", lambda e: e.tensor_tensor(out=fb[f3][:, 0:Nq], in0=P[:, 2, 0:Nq], in1=fb[f2][:, 0:Nq], op=ALU.mult),
                                              reads=[("ps", 2), ("fb", f2)], writes=[("fb", f3)])
                                        S_.op("dve", lambda e: e.scalar_tensor_tensor(out=fb[f4][:, 0:Nq], in0=fb[f3][:, 0:Nq], scalar=neglam,
                                                                                      in1=fb[f1][:, 0:Nq], op0=ALU.mult, op1=ALU.add),
                                              reads=[("fb", f3), ("fb", f1), "neglam"], writes=[("fb", f4)])

                                        def comb2(nb):
                                            S_.op("act", lambda e: e.activation(out=fb[f0][:, 0:Nq], in_=fb[f4][:, 0:Nq], func=AF.Square),
                                                  reads=[("fb", f4)], writes=[("fb", f0)])
                                            S_.op("pe", lambda e: e.matmul(SB[nb][:, 0:Nq], lhsT=ones_f[:, :], rhs=fb[f0][:, 0:Nq], start=True, stop=True),
                                                  reads=[("fb", f0), "ones_f"], writes=[SBR[nb]])
                                            S_.op("act", lambda e: e.activation(out=fb[f2][:, 0:Nq], in_=SB[nb][:, 0:Nq], func=AF.Sqrt, bias=epsT[:, 0:1],
                                                                                scale=1.0 / 128),
                                                  reads=[SBR[nb], "epsT"], writes=[("fb", f2)])
                                            S_.op("dve", lambda e: e.reciprocal(out=fb[f3][:, 0:Nq], in_=fb[f2][:, 0:Nq]), reads=[("fb", f2)], writes=[("fb", f3)])
                                            S_.op("dve", lambda e: e.scalar_tensor_tensor(out=ybT[:, h, qc0:qc0 + Nq], in0=fb[f4][:, 0:Nq], scalar=subc[:, 0:1],
                                                                                          in1=fb[f3][:, 0:Nq], op0=ALU.mult, op1=ALU.mult),
                                                  reads=[("fb", f4), ("fb", f3), "subc"], writes=[("ybT", h)])
                                        pend2.append(comb2)
                                its.append((s1, s2, s3, post))
                    pipeline(its, PIPE_B, BDEPTH)
                    while pend2:
                        pend2.pop()(0)
            T.attn_b = attn_b

            def p2():
                ctr["ps"] = 4
                gw = {}
                gw[0] = wget(wb0 + 6)
                gw[2] = wget(wb0 + 7)
                wab = wget(wb0 + 8)
                for m in range(8):
                    half, mc = m // 4, (m % 4) * 128
                    if m == 4:
                        gw[1] = wget(wb0 + 9)
                        gw[3] = wget(wb0 + 10)
                        wab = wget(wb0 + 11)
                    wa = (wab[0], wab[1][:, 0:4, :])
                    wb = (wab[0], wab[1][:, 4:8, :])
                    banks = [nxt("ps", 6) for _ in range(4)]
                    specs = [(gw[half], 8, hT, [("hT", s2) for s2 in range(nsub)], mc),
                             (gw[2 + half], 8, hT, [("hT", s2) for s2 in range(nsub)], mc),
                             (wa, 4, yaT, [("yaT", s2) for s2 in range(nsub)], mc),
                             (wb, 4, ybT, [("ybT", h2) for h2 in range(4)], mc)]
                    for bnk, ((wi, W), nk_, act, rd, cc) in zip(banks, specs):
                        def mm(e, bnk=bnk, W=W, nk_=nk_, act=act, cc=cc):
                            ins = None
                            for kc in range(nk_):
                                ins = e.matmul(P[:, bnk, 0:NT], lhsT=W[:, kc, cc:cc + 128], rhs=act[:, kc, 0:NT],
                                               start=(kc == 0), stop=(kc == nk_ - 1))
                            return ins
                        S_.op("pe", mm, reads=rd + [("w", wi)], writes=[("ps", bnk)])
                    s0, s1, s2_, s3 = [nxt("fb", NFB) for _ in range(4)]
                    S_.op("act", lambda e, b=banks[0], s0=s0, m=m: e.activation(out=fb[s0][:, 0:NT], in_=P[:, b, 0:NT], func=AF.Sigmoid,
                                                                               bias=bgT[:, m:m + 1], scale=1.0),
                          reads=[("ps", banks[0]), "bgT"], writes=[("fb", s0)])
                    S_.op("act", lambda e, b=banks[1], s1=s1, m=m: e.activation(out=fb[s1][:, 0:NT], in_=P[:, b, 0:NT], func=AF.Sigmoid,
                                                                               bias=bgT[:, 8 + m:9 + m], scale=1.0),
                          reads=[("ps", banks[1]), "bgT"], writes=[("fb", s1)])
                    S_.op("dve", lambda e, b=banks[2], s0=s0, s2_=s2_: e.tensor_tensor(out=fb[s2_][:, 0:NT], in0=P[:, b, 0:NT], in1=fb[s0][:, 0:NT], op=ALU.mult),
                          reads=[("ps", banks[2]), ("fb", s0)], writes=[("fb", s2_)])
                    S_.op("dve", lambda e, b=banks[3], s1=s1, s3=s3: e.tensor_tensor(out=fb[s3][:, 0:NT], in0=P[:, b, 0:NT], in1=fb[s1][:, 0:NT], op=ALU.mult),
                          reads=[("ps", banks[3]), ("fb", s1)], writes=[("fb", s3)])
                    S_.op(PL, lambda e, s2_=s2_, s3=s3, m=m: e.tensor_tensor(out=mT[:, m, 0:NT], in0=fb[s2_][:, 0:NT], in1=fb[s3][:, 0:NT], op=ALU.add),
                          reads=[("fb", s2_), ("fb", s3)], writes=[("mT", m)])
                    if m == 3:
                        wdone(wb0 + 6); wdone(wb0 + 7); wdone(wb0 + 8)
                    if m == 7:
                        wdone(wb0 + 9); wdone(wb0 + 10); wdone(wb0 + 11)
                wo = [wget(wb0 + 12 + g) for g in range(2)]
                hb2 = {}

                def n2_A(s):
                    st = stat[0:np_, 8 + s:9 + s]
                    rd = [("x1", s, 0), ("x1", s, 1)]
                    rms_rows(X[0:np_, s, :], np_, D, st, rd, f"rs2_{s}")
                    hi = nxt("hb", 2)
                    hb2[s] = hi
                    S_.op("dve", lambda e, s=s, st=st, hi=hi: e.tensor_scalar(out=hbs[hi][0:np_, :], in0=X[0:np_, s, :], scalar1=st, scalar2=None, op0=ALU.mult),
                          reads=rd + [f"rs2_{s}"], writes=[("hb", hi)])

                def n2_B(s):
                    hi = hb2[s]
                    to_fm(hbs[hi][0:np_, :], np_, 8, g2T[:, :], lambda s=s: hT[:, :, col[s]:col[s] + np_], [("hb", hi)], [("hT", s)])

                for s in range(nsub):
                    xi = load_xs(s)
                    for g in range(2):
                        wi, W = wo[g]
                        b = nxt("ps", 6)

                        def mm(e, s=s, b=b, W=W):
                            ins = None
                            for kc in range(8):
                                ins = e.matmul(P[0:np_, b, :], lhsT=mT[:, kc, col[s]:col[s] + np_], rhs=W[:, kc, :], start=(kc == 0), stop=(kc == 7))
                            return ins
                        S_.op("pe", mm, reads=[("mT", m2) for m2 in range(8)] + [("w", wi)], writes=[("ps", b)])
                        S_.op("dve", lambda e, s=s, b=b, g=g, xi=xi: e.tensor_tensor(out=X[0:np_, s, g * 512:(g + 1) * 512], in0=P[0:np_, b, :],
                                                                                   in1=xstage[xi][0:np_, g * 512:(g + 1) * 512], op=ALU.add),
                              reads=[("ps", b), ("xs", xi)], writes=[("x1", s, g)])
                    n2_A(s)
                    if s >= 1:
                        n2_B(s - 1)
                wdone(wb0 + 12); wdone(wb0 + 13)
                n2_B(nsub - 1)
            T.p2 = p2

            def ff1(pre=()):
                for fn in pre:
                    fn()
                for i in range(8):
                    wi, W = wget(wb0 + 14 + i)
                    for mc in range(4):
                        j = i * 4 + mc
                        b = nxt("ps", 6)

                        def mm(e, b=b, W=W, mc=mc):
                            ins = None
                            for kc in range(8):
                                ins = e.matmul(P[:, b, 0:NT], lhsT=W[:, kc, mc * 128:(mc + 1) * 128], rhs=hT[:, kc, 0:NT], start=(kc == 0), stop=(kc == 7))
                            return ins
                        S_.op("pe", mm, reads=[("hT", s2) for s2 in range(nsub)] + [("w", wi)], writes=[("ps", b)])
                        ri = nxt("fb", NFB)
                        S_.op("act", lambda e, b=b, ri=ri: e.activation(out=fb[ri][:, 0:NT], in_=P[:, b, 0:NT], func=AF.Relu),
                              reads=[("ps", b)], writes=[("fb", ri)])
                        eng = PL if (j % 2 == 0) else "dve"
                        S_.op(eng, lambda e, ri=ri, j=j: e.tensor_tensor(out=uT[:, j, 0:NT], in0=fb[ri][:, 0:NT], in1=fb[ri][:, 0:NT], op=ALU.mult),
                              reads=[("fb", ri)], writes=[("uT", j)])
                    wdone(wb0 + 14 + i)
            T.ff1 = ff1

            def ff2(hooks=None):
                hooks = hooks or {}
                for g in range(2):
                    banks = [nxt("ps", 6) for _ in range(nsub)]
                    for i in range(4):
                        wi, W = wget(wb0 + 22 + g * 4 + i)
                        for s in range(nsub):
                            def mm(e, s=s, W=W, i=i, b=banks[s]):
                                ins = None
                                for kc in range(8):
                                    ins = e.matmul(P[0:np_, b, :], lhsT=uT[:, i * 8 + kc, col[s]:col[s] + np_], rhs=W[:, kc, :],
                                                   start=(i == 0 and kc == 0), stop=(i == 3 and kc == 7))
                                return ins
                            S_.op("pe", mm, reads=[("uT", i * 8 + kc) for kc in range(8)] + [("w", wi)], writes=[("ps", banks[s])])
                        wdone(wb0 + 22 + g * 4 + i)
                        for fn in hooks.get(g * 4 + i, ()):
                            fn()
                    for s in range(nsub):
                        S_.op("dve", lambda e, s=s, b=banks[s], g=g: e.tensor_tensor(out=X[0:np_, s, g * 512:(g + 1) * 512], in0=P[0:np_, b, :],
                                                                                    in1=X[0:np_, s, g * 512:(g + 1) * 512], op=ALU.add),
                              reads=[("ps", banks[s]), ("x1", s, g)], writes=[("x1", s, g)])
                yrd = [("x1", s, g) for s in range(nsub) for g in range(2)]
                if kind == "p":
                    S_.dma("sp", lambda e: [e.dma_start(out=yp[seq, t0:t0 + 512, :].rearrange("(s p) d -> p s d", p=128), in_=X[:, :, :])],
                           1, "yo", reads=yrd)
                else:
                    S_.dma("sp", lambda e: [e.dma_start(out=ys.rearrange("s p d -> p s d"), in_=X[0:16, 0:2, :])],
                           1, "yo", reads=yrd)
            T.ff2 = ff2
            return T

        tiles = []
        for q in range(NSEQ):
            for t0 in range(0, S, 512):
                tiles.append(("p", q, t0, [128] * 4))
        if with_sample:
            tiles.append(("s", 0, 0, [16, 16]))
        TL = [make_tile(kind, q, t0, subs, ti) for ti, (kind, q, t0, subs) in enumerate(tiles)]
        wstate["total"] = len(tiles) * NPIECE
        TL[0].p1a()
        for i, T in enumerate(TL):
            T.p1b()
            T.attn_a()
            T.attn_b()
            T.p2()
            if i + 1 < len(TL):
                N = TL[i + 1]
                ns = N.nsub
                T.ff1(pre=[lambda N=N: N.p1a_A(0), lambda N=N: N.p1a_A(1)])
                hooks = {}
                for s in range(ns):
                    hooks.setdefault(s, []).append(lambda N=N, s=s: N.p1a_B(s))
                    if s + 2 < ns:
                        hooks[s].append(lambda N=N, s=s: N.p1a_A(s + 2))
                T.ff2(hooks)
            else:
                T.ff1()
                T.ff2()

        if DEBUG:
            for nm, t, shp, dt in [("yaT", yaT, (128, 4, 512), BF16), ("ybT", ybT, (128, 4, 512), BF16), ("mT", qT, (128, 8, 512), BF16),
                                   ("hT", hT, (128, 8, 512), BF16), ("uT", uT, (128, 32, 512), BF16), ("X", xt[0], (128, 4, D), F32),
                                   ("kbT", kbT, (128, 4, SK), BF16), ("vb", vb, (128, NKT, 512), BF16), ("kaT", kaT, (128, 4, 1024), BF16),
                                   ("va", va, (128, 8, 512), BF16), ("lamt", lamt, (128, 8), F32), ("biasA", biasA, (128, 8, 5, 128), BF16),
                                   ("ya", ya, (128, 512), BF16), ("pA0", pA[0], (128, 5, 128), BF16), ("pA1", pA[1], (128, 5, 128), BF16), ("rsA", rsA, (128, 8), F32)]:
                dd = nc.dram_tensor("dbg_" + nm, list(shp), dt, kind="ExternalOutput").ap()
                allres = list(S_.lastw.keys())
                S_.fence("sp", allres)
                S_.dma("sp", lambda e, dd=dd, t=t: [e.dma_start(out=dd, in_=t[:] if not isinstance(t, bass.AP) else t)], 1, "dbg_" + nm)

        dkeys = [k for k in S_.cnt if k not in Sched.ENG]
        S_.final_wait("sp", dkeys)

        sems = {}
        for k in S_.cnt:
            sems[k] = es.enter_context(nc.semaphore("s_" + str(k).replace(" ", "").replace("(", "").replace(")", "").replace(",", "_").replace("'", "")))
        for k, v in S_.cnt.items():
            assert v < 60000, (k, v)
        engmap = {"pe": "tensor", "act": "scalar", "dve": "vector", "pool": "gpsimd", "sp": "sync"}
        with nc.Block() as block:
            def emit(name):
                def body(e):
                    for it in S_.lists[name]:
                        if it[0] == "wait":
                            e.wait_ge(sems[it[1]], it[2])
                        elif it[0] == "op":
                            ins = it[1](e)
                            ins.then_inc(sems[it[2]], 1)
                        else:
                            for ins in it[1](e):
                                ins.then_inc(sems[it[2]], 16)
                return body
            for name in Sched.ENG:
                getattr(block, engmap[name])(emit(name))
    return nc


def _rope_tables(S):
    half = 32
    inv = 10000.0 ** (-np.arange(half, dtype=np.float64) / half)
    nsub = S // 128
    pos = np.zeros((128, nsub + 1), np.float64)
    for k in range(nsub):
        pos[:, k] = k * 128 + np.arange(128)
    pos[:, nsub] = PAST + np.arange(128)
    ang = pos[:, :, None] * inv[None, None, :]
    return np.cos(ang).astype(np.float32), np.sin(ang).astype(np.float32)


def _bias_tables(rel):
    kk = np.arange(640)[:, None]
    qq = np.arange(128)[None, :]
    dist = (qq + 512) - kk
    idx = np.clip(dist, -63, 256) + 63
    kc = kk // 64 - 8
    qc = qq // 64
    valid = (kc <= qc) & (kc >= qc - 8)
    bp = rel[:, idx]
    bp = np.where(valid[None], bp, np.float32(NEG)).astype(np.float32)
    biasA = np.ascontiguousarray(bp.reshape(8, 5, 128, 128).transpose(2, 0, 1, 3))
    kpos = np.concatenate([PAST - 512 + np.arange(512), PAST + np.arange(16), np.zeros(112)]).astype(np.int64)
    qpos = PAST + np.arange(16)
    d2 = qpos[None, :] - kpos[:, None]
    i2 = np.clip(d2, -63, 256) + 63
    bs = rel[:, i2].astype(np.float32)
    biasS = np.ascontiguousarray(bs.reshape(8, 5, 128, 16).transpose(2, 0, 1, 3))
    return biasA, biasS


def _rep(v, n):
    return np.ascontiguousarray(np.broadcast_to(np.tile(np.asarray(v, np.float32), n)[None, :], (128, 512)))


_CACHE = {}


def kernel(x_prompt, x_sample, cache_a_k, cache_a_v, cache_b_k, cache_b_v,
           ln1_g, w_in, qn_a, kn_a, rel_bias, qn_b, kn_b,
           lam_q1, lam_k1, lam_q2, lam_k2, subln_g,
           w_gate, b_gate, w_proj_a, w_proj_b, w_out, ln2_g, w_ff1, w_ff2, _ncores=NCORES):
    f = lambda a: np.ascontiguousarray(np.asarray(a, dtype=np.float32))
    x_prompt = f(x_prompt); x_sample = f(x_sample)
    B, S, _ = x_prompt.shape
    n = _ncores
    NSEQ = B // n
    key = (NSEQ, S)
    if key not in _CACHE:
        _CACHE[key] = build_program(NSEQ, S, True)
    nc = _CACHE[key]
    cosT, sinT = _rope_tables(S)
    biasA, biasS = _bias_tables(f(rel_bias)[0])
    normv = np.ascontiguousarray(np.broadcast_to(np.stack([f(qn_a)[0], f(kn_a)[0], f(qn_b)[0], f(kn_b)[0]])[None], (128, 4, 64)))
    lamv = np.ascontiguousarray(np.broadcast_to(
        np.stack([f(lam_q1)[0], f(lam_k1)[0], f(lam_q2)[0], f(lam_k2)[0]])[None], (128, 4, 64)))
    shared = {
        "w_in": f(w_in)[0], "w_gate": f(w_gate)[0], "w_pa": f(w_proj_a)[0], "w_pb": f(w_proj_b)[0],
        "w_out": f(w_out)[0], "w_ff1": f(w_ff1)[0], "w_ff2": f(w_ff2)[0],
        "g1T": np.ascontiguousarray(f(ln1_g)[0].reshape(8, 128).T), "g2T": np.ascontiguousarray(f(ln2_g)[0].reshape(8, 128).T),
        "bgT": np.ascontiguousarray(f(b_gate)[0].reshape(16, 128).T), "subT": np.ascontiguousarray(f(subln_g)[0].reshape(128, 1)),
        "normv": np.ascontiguousarray(normv), "lamv": lamv, "ropec": cosT, "ropes": sinT,
        "biasA": biasA, "biasS": biasS, "ident": np.eye(128, dtype=np.float32),
    }
    cak = f(cache_a_k)[0].reshape(-1, 512, 512); cav = f(cache_a_v)[0].reshape(-1, 512, 512)
    cbk = f(cache_b_k)[0].reshape(-1, 1024, 512); cbv = f(cache_b_v)[0].reshape(-1, 1024, 512)
    in_maps = []
    for c in range(n):
        m = dict(shared)
        m["xp"] = x_prompt[c * NSEQ:(c + 1) * NSEQ]
        m["xs"] = x_sample[2 * c:2 * c + 2]
        m["cak"] = cak[2 * c:2 * c + 2]; m["cav"] = cav[2 * c:2 * c + 2]
        m["cbk"] = cbk[2 * c:2 * c + 2]; m["cbv"] = cbv[2 * c:2 * c + 2]
        in_maps.append(m)
    res = run_bass_kernel_spmd(nc, in_maps, core_ids=list(range(n)))
    R = res.results
    if DEBUG:
        global LAST_RESULTS
        LAST_RESULTS = R
    cat = lambda k: np.concatenate([r[k] for r in R], axis=0)
    y_p = cat("yp"); y_s = cat("ys")
    akp = cat("akp").reshape(1, B, 512, 8, 64); avp = cat("avp").reshape(1, B, 512, 8, 64)
    bkp = cat("bkp").reshape(1, B, S, 4, 2, 64); bvp = cat("bvp").reshape(1, B, S, 4, 128)
    nb = 2 * n
    aks = cat("aks").reshape(1, nb, 16, 8, 64); avs = cat("avs").reshape(1, nb, 16, 8, 64)
    bks = cat("bks").reshape(1, nb, 16, 4, 2, 64); bvs = cat("bvs").reshape(1, nb, 16, 4, 128)
    return (y_p, y_s, akp, avp, bkp, bvp, aks, avs, bks, bvs)
```
